# Optimizing a Trainium2 kernel written in Bass

```python
import math
import jax, jax.numpy as jnp
from jax import lax
import numpy as np


D_MODEL = 1024
BATCH = 8
SEQ = 2048
DEPTH = 2

HEAD_DIM = 64
RWKV_DIM = D_MODEL // 2
RWKV_HEADS = RWKV_DIM // HEAD_DIM
ATT_Q_HEADS = (D_MODEL - RWKV_DIM) // HEAD_DIM
ATT_KV_HEADS = 2
ATT_GROUP = ATT_Q_HEADS // ATT_KV_HEADS
ATT_DIM = ATT_Q_HEADS * HEAD_DIM
KV_DIM = ATT_KV_HEADS * HEAD_DIM
LORA_W = 64
LORA_A = 64
LORA_G = 128
SHIFT_DIM = 3 * RWKV_DIM + LORA_W + LORA_A + LORA_G
IN_DIM = SHIFT_DIM + ATT_DIM + 2 * KV_DIM
WINDOW = 128
BLOCK = 128
CONV_WIDTH = 31
D_FF = 4 * D_MODEL
RMS_EPS = 1e-6
LN_EPS = 1e-5
GN_EPS = 64e-5
N_EVEN = (DEPTH + 1) // 2
N_ODD = DEPTH // 2

kernel_name = 'hybrid_rwkv7_swa_sink_conformer_conv'


def rmsnorm(x, g):
    x32 = x.astype(jnp.float32)
    y = x32 * lax.rsqrt(jnp.mean(x32 * x32, axis=-1, keepdims=True) + RMS_EPS)
    return (y * g.astype(jnp.float32)).astype(x.dtype)


def token_shift(p):
    return jnp.pad(p[:, :-1], ((0, 0), (1, 0), (0, 0)))


def rwkv7_scan(r, w, k, v, a, b):
    bsz, _, nh, n = r.shape

    def step(state, inp):
        r_t, w_t, k_t, v_t, a_t, b_t = inp
        sa = jnp.einsum('bhvk,bhk->bhv', state, a_t)
        state = (state * w_t[:, :, None, :]
                 + sa[..., None] * b_t[:, :, None, :]
                 + v_t[..., None] * k_t[:, :, None, :])
        y_t = jnp.einsum('bhvk,bhk->bhv', state, r_t)
        return state, y_t

    xs = tuple(jnp.moveaxis(t, 1, 0) for t in (r, w, k, v, a, b))
    s0 = jnp.zeros((bsz, nh, n, n), jnp.float32)
    _, y = lax.scan(step, s0, xs)
    return jnp.moveaxis(y, 0, 1)


def rwkv7_time_mix(p, mu, w0, w_up, a0, a_up, g_up, k_k, k_a, r_k, gn_g, gn_b):
    bsz, seq, _ = p.shape
    f32 = jnp.float32
    p = p + (token_shift(p) - p) * mu
    c1 = RWKV_DIM
    c2 = 2 * RWKV_DIM
    c3 = 3 * RWKV_DIM
    c4 = c3 + LORA_W
    c5 = c4 + LORA_A
    r, k, v, wl, al, gl = jnp.split(p, [c1, c2, c3, c4, c5], axis=-1)
    w = -jax.nn.softplus(-(w0 + jnp.tanh(wl) @ w_up)) - 0.5
    decay = jnp.exp(-jnp.exp(w.astype(f32)))
    a = jax.nn.sigmoid(a0 + al @ a_up)
    g = jax.nn.sigmoid(gl) @ g_up

    def heads(t):
        return t.astype(f32).reshape(bsz, seq, RWKV_HEADS, HEAD_DIM)

    r, k, v, a, decay = heads(r), heads(k), heads(v), heads(a), heads(decay)
    kk = k * k_k.astype(f32)
    kk = kk / jnp.maximum(jnp.sqrt(jnp.sum(kk * kk, axis=-1, keepdims=True)), 1e-12)
    k = k * (1.0 + (a - 1.0) * k_a.astype(f32))
    y = rwkv7_scan(r, decay, k, v, -kk, kk * a)
    mean = jnp.mean(y, axis=-1, keepdims=True)
    var = jnp.mean(jnp.square(y - mean), axis=-1, keepdims=True)
    y = (y - mean) * lax.rsqrt(var + GN_EPS) * gn_g.astype(f32) + gn_b.astype(f32)
    y = y + jnp.sum(r * k * r_k.astype(f32), axis=-1, keepdims=True) * v
    return (y.reshape(bsz, seq, RWKV_DIM) * g.astype(f32)).astype(p.dtype)


def sliding_window_sink_attention(q, k, v, sinks):
    bsz, seq, _, _ = q.shape
    nb = seq // BLOCK
    f32 = jnp.float32
    qb = q.reshape(bsz, nb, BLOCK, ATT_KV_HEADS, ATT_GROUP, HEAD_DIM)

    def with_prev(t):
        tb = t.reshape(bsz, nb, BLOCK, ATT_KV_HEADS, HEAD_DIM)
        prev = jnp.pad(tb[:, :-1], ((0, 0), (1, 0), (0, 0), (0, 0), (0, 0)))
        return jnp.concatenate([prev, tb], axis=2)

    kb, vb = with_prev(k), with_prev(v)
    scores = jnp.einsum('bnqhgd,bnkhd->bnhgqk', qb, kb).astype(f32) * (HEAD_DIM ** -0.5)
    qi = jnp.arange(BLOCK)[:, None]
    kj = jnp.arange(2 * BLOCK)[None, :]
    rel = qi + BLOCK - kj
    band = (rel >= 0) & (rel < WINDOW)
    key_pos = jnp.arange(nb)[:, None] * BLOCK + jnp.arange(2 * BLOCK)[None, :] - BLOCK
    mask = band[None] & (key_pos >= 0)[:, None, :]
    slopes = jnp.exp2(-8.0 * jnp.arange(1, ATT_Q_HEADS + 1, dtype=f32) / ATT_Q_HEADS)
    slopes = slopes.reshape(ATT_KV_HEADS, ATT_GROUP)
    scores = scores - slopes[:, :, None, None] * rel.astype(f32)
    scores = jnp.where(mask[None, :, None, None], scores, -jnp.inf)
    sink = sinks.astype(f32).reshape(ATT_KV_HEADS, ATT_GROUP)[None, None, :, :, None, None]
    m = jnp.maximum(jnp.max(scores, axis=-1, keepdims=True), sink)
    e = jnp.exp(scores - m)
    probs = e / (jnp.sum(e, axis=-1, keepdims=True) + jnp.exp(sink - m))
    out = jnp.einsum('bnhgqk,bnkhd->bnqhgd', probs.astype(v.dtype), vb)
    return out.reshape(bsz, seq, ATT_DIM)


def hybrid_mixer(h, w_in, mu, w0, w_up, a0, a_up, g_up, k_k, k_a, r_k, gn_g, gn_b, sinks, w_out):
    bsz, seq, _ = h.shape
    p = h @ w_in
    o1 = SHIFT_DIM
    o2 = o1 + ATT_DIM
    o3 = o2 + KV_DIM
    q = p[..., o1:o2].reshape(bsz, seq, ATT_Q_HEADS, HEAD_DIM)
    k = p[..., o2:o3].reshape(bsz, seq, ATT_KV_HEADS, HEAD_DIM)
    v = p[..., o3:].reshape(bsz, seq, ATT_KV_HEADS, HEAD_DIM)
    y_rwkv = rwkv7_time_mix(p[..., :o1], mu, w0, w_up, a0, a_up, g_up, k_k, k_a, r_k, gn_g, gn_b)
    y_att = sliding_window_sink_attention(q, k, v, sinks)
    return jnp.concatenate([y_rwkv, y_att.astype(y_rwkv.dtype)], axis=-1) @ w_out


def conformer_conv(h, pw1_w, pw1_b, dw_w, dw_b, ln_g, ln_b, pw2_w, pw2_b):
    u = h @ pw1_w + pw1_b
    u = u[..., :D_MODEL] * jax.nn.sigmoid(u[..., D_MODEL:])
    u = lax.conv_general_dilated(
        u, dw_w[:, None, :].astype(u.dtype), window_strides=(1,),
        padding=[(CONV_WIDTH - 1, 0)],
        dimension_numbers=('NWC', 'WIO', 'NWC'),
        feature_group_count=D_MODEL) + dw_b
    u32 = u.astype(jnp.float32)
    mean = jnp.mean(u32, axis=-1, keepdims=True)
    var = jnp.mean(jnp.square(u32 - mean), axis=-1, keepdims=True)
    u32 = (u32 - mean) * lax.rsqrt(var + LN_EPS) * ln_g.astype(jnp.float32) + ln_b.astype(jnp.float32)
    u = jax.nn.silu(u32).astype(h.dtype)
    return u @ pw2_w + pw2_b


def sqrelu_mlp(h, w1, w2):
    return jnp.square(jax.nn.relu(h @ w1)) @ w2


def setup_inputs(seed: int = 0) -> dict:
    key = jax.random.key(seed)
    ks = iter(jax.random.split(key, 40))

    def nrm(shape, scale):
        return scale * jax.random.normal(next(ks), shape, jnp.float32)

    def uni(shape, lo, hi):
        return jax.random.uniform(next(ks), shape, jnp.float32, lo, hi)

    hn = (N_EVEN, RWKV_HEADS, HEAD_DIM)
    return {
        'x': nrm((BATCH, SEQ, D_MODEL), 1.0),
        'norm_mix_g': 1.0 + nrm((DEPTH, D_MODEL), 0.02),
        'norm_ffn_g': 1.0 + nrm((DEPTH, D_MODEL), 0.02),
        'final_norm_g': 1.0 + nrm((D_MODEL,), 0.02),
        'hy_w_in': nrm((N_EVEN, D_MODEL, IN_DIM), D_MODEL ** -0.5),
        'hy_mu': uni((N_EVEN, SHIFT_DIM), 0.0, 1.0),
        'hy_w0': uni((N_EVEN, RWKV_DIM), -5.0, -0.5),
        'hy_w_up': nrm((N_EVEN, LORA_W, RWKV_DIM), 0.1 * LORA_W ** -0.5),
        'hy_a0': nrm((N_EVEN, RWKV_DIM), 0.1),
        'hy_a_up': nrm((N_EVEN, LORA_A, RWKV_DIM), 0.1 * LORA_A ** -0.5),
        'hy_g_up': nrm((N_EVEN, LORA_G, RWKV_DIM), LORA_G ** -0.5),
        'hy_k_k': 0.85 + nrm(hn, 0.02),
        'hy_k_a': 1.0 + nrm(hn, 0.02),
        'hy_r_k': nrm(hn, 0.1),
        'hy_gn_g': 1.0 + nrm(hn, 0.02),
        'hy_gn_b': nrm(hn, 0.02),
        'hy_sinks': nrm((N_EVEN, ATT_Q_HEADS), 1.0),
        'hy_w_out': nrm((N_EVEN, D_MODEL, D_MODEL), D_MODEL ** -0.5),
        'cv_pw1_w': nrm((N_ODD, D_MODEL, 2 * D_MODEL), D_MODEL ** -0.5),
        'cv_pw1_b': nrm((N_ODD, 2 * D_MODEL), 0.02),
        'cv_dw_w': nrm((N_ODD, CONV_WIDTH, D_MODEL), CONV_WIDTH ** -0.5),
        'cv_dw_b': nrm((N_ODD, D_MODEL), 0.02),
        'cv_ln_g': 1.0 + nrm((N_ODD, D_MODEL), 0.02),
        'cv_ln_b': nrm((N_ODD, D_MODEL), 0.02),
        'cv_pw2_w': nrm((N_ODD, D_MODEL, D_MODEL), D_MODEL ** -0.5),
        'cv_pw2_b': nrm((N_ODD, D_MODEL), 0.02),
        'mlp_w1': nrm((DEPTH, D_MODEL, D_FF), D_MODEL ** -0.5),
        'mlp_w2': nrm((DEPTH, D_FF, D_MODEL), D_FF ** -0.5),
    }


def reference(x, norm_mix_g, norm_ffn_g, final_norm_g,
              hy_w_in, hy_mu, hy_w0, hy_w_up, hy_a0, hy_a_up, hy_g_up,
              hy_k_k, hy_k_a, hy_r_k, hy_gn_g, hy_gn_b, hy_sinks, hy_w_out,
              cv_pw1_w, cv_pw1_b, cv_dw_w, cv_dw_b, cv_ln_g, cv_ln_b, cv_pw2_w, cv_pw2_b,
              mlp_w1, mlp_w2):
    for layer in range(DEPTH):
        i = layer // 2
        h = rmsnorm(x, norm_mix_g[layer])
        if layer % 2 == 0:
            mix = hybrid_mixer(h, hy_w_in[i], hy_mu[i], hy_w0[i], hy_w_up[i], hy_a0[i],
                               hy_a_up[i], hy_g_up[i], hy_k_k[i], hy_k_a[i], hy_r_k[i],
                               hy_gn_g[i], hy_gn_b[i], hy_sinks[i], hy_w_out[i])
        else:
            mix = conformer_conv(h, cv_pw1_w[i], cv_pw1_b[i], cv_dw_w[i], cv_dw_b[i],
                                 cv_ln_g[i], cv_ln_b[i], cv_pw2_w[i], cv_pw2_b[i])
        x = x + mix.astype(x.dtype)
        h = rmsnorm(x, norm_ffn_g[layer])
        x = x + sqrelu_mlp(h, mlp_w1[layer], mlp_w2[layer]).astype(x.dtype)
    return rmsnorm(x, final_norm_g)
```

```python
import numpy as np
from contextlib import ExitStack
import concourse.bass as bass
import concourse.mybir as mybir
from concourse.bass_utils import run_bass_kernel_spmd

F32 = mybir.dt.float32
BF16 = mybir.dt.bfloat16
ALU = mybir.AluOpType
AF = mybir.ActivationFunctionType
AX = mybir.AxisListType

NDMA = 12
BLK = 256
DTSZ = {mybir.dt.float32: 4, mybir.dt.bfloat16: 2, mybir.dt.float32r: 4}


class Prog:
    def __init__(self):
        self.nc = bass.Bass("TRN2", target_bir_lowering=False)
        self.es = ExitStack()
        self.ops = []
        self.tinfo = {}
        self.psum_names = set()
        self.ndma = 0

    def dram(self, name, shape, kind, dtype=F32):
        return self.nc.dram_tensor(name, list(shape), dtype, kind=kind).ap()

    def sb(self, name, shape, dtype=F32):
        t = self.es.enter_context(self.nc.sbuf_tensor(name, list(shape), dtype))
        ap = t[:]
        self.tinfo[ap.name] = int(np.prod(shape[1:])) * DTSZ[dtype]
        return ap

    def ps(self, name, shape, dtype=F32):
        t = self.es.enter_context(self.nc.psum_tensor(name, list(shape), dtype))
        ap = t[:]
        self.tinfo[ap.name] = int(np.prod(shape[1:]))
        self.psum_names.add(ap.name)
        return ap

    def blocks(self, ap):
        name = ap.name
        if name not in self.tinfo:
            return []
        if name in self.psum_names:
            return [(name, 0)]
        fs = self.tinfo[name]
        sz = DTSZ[ap.dtype]
        off = (int(ap.offset) * sz) % fs
        ext = 1
        for (st, cn) in ap.ap[1:]:
            ext += (cn - 1) * abs(st)
        ext *= sz
        b0 = off // BLK
        b1 = (off + ext - 1) // BLK
        return [(name, b) for b in range(b0, b1 + 1)]

    @staticmethod
    def _nfree(ap):
        n = 1
        for (st, cn) in ap.ap[1:]:
            n *= cn
        return n

    def _dur(self, eng, reads, writes, dma):
        n = self._nfree(writes[0]) if writes else 1
        if dma:
            return 0.1
        if eng == 'pe':
            return 0.06
        if eng == 'dve':
            return 0.08 + n / 960.0
        if eng == 'act':
            return 0.2 + n / 1200.0
        if eng == 'pool':
            return 0.3 + n / 480.0
        return 0.1

    def op(self, eng, fn, reads, writes, dma=False, rkeys=(), wkeys=()):
        rb = list(rkeys)
        for a in reads:
            if a is not None and not isinstance(a, (int, float)):
                rb += self.blocks(a)
        wb = list(wkeys)
        for a in writes:
            wb += self.blocks(a)
        wb += [b for b in rb if b[0] in self.psum_names]
        self.ops.append(dict(eng=eng, fn=fn, rb=rb, wb=wb, dma=dma, dur=self._dur(eng, reads, writes, dma)))

    @staticmethod
    def _cls(n):
        return 32 if n <= 32 else (64 if n <= 64 else 128)

    def _pemode(self, lhsT, tr):
        m = 1
        for (st, cn) in lhsT.ap[1:]:
            m *= cn
        return (int(lhsT.base_partition()), self._cls(int(lhsT.partition_size())), self._cls(m), tr)

    def mm(self, out, lhsT, rhs, start=True, stop=True):
        self.op('pe', lambda e: e.matmul(out, lhsT, rhs, start=start, stop=stop),
                [lhsT, rhs], [out])
        self.ops[-1]['pemode'] = (out.name, self._pemode(lhsT, False))
        n = self._nfree(rhs)
        base = max(0.055, n / 2200.0)
        dt_ = lhsT.dtype
        self.ops[-1]['dur'] = base * (4.0 if dt_ == mybir.dt.float32 else (2.0 if dt_ == mybir.dt.float32r else 1.0))

    def transpose(self, out, in_, ident):
        self.op('pe', lambda e: e.transpose(out, in_, ident), [in_, ident], [out])
        self.ops[-1]['pemode'] = (out.name, self._pemode(in_, True))
        self.ops[-1]['dur'] = 0.12

    def act(self, out, in_, func, bias=0.0, scale=1.0, eng='act'):
        self.op(eng, lambda e: e.activation(out, in_, func, bias=bias, scale=scale),
                [in_, bias, scale], [out])

    def tt(self, out, in0, in1, op, eng='dve'):
        self.op(eng, lambda e: e.tensor_tensor(out, in0, in1, op), [in0, in1], [out])

    def ts(self, out, in0, s1, op0, s2=None, op1=None, eng='dve'):
        if op1 is None:
            self.op(eng, lambda e: e.tensor_scalar(out, in0, s1, None, op0), [in0, s1], [out])
        else:
            self.op(eng, lambda e: e.tensor_scalar(out, in0, s1, s2, op0, op1), [in0, s1, s2], [out])

    def stt(self, out, in0, scalar, in1, op0, op1, eng='dve'):
        self.op(eng, lambda e: e.scalar_tensor_tensor(out, in0, scalar, in1, op0, op1),
                [in0, scalar, in1], [out])

    def copy(self, out, in_, eng='dve'):
        if eng == 'act':
            self.op(eng, lambda e: e.copy(out, in_), [in_], [out])
        else:
            self.op(eng, lambda e: e.tensor_copy(out, in_), [in_], [out])

    def memset(self, out, val, eng='dve'):
        self.op(eng, lambda e: e.memset(out, val), [], [out])

    def recip(self, out, in_):
        self.op('dve', lambda e: e.reciprocal(out, in_), [in_], [out])
        self.ops[-1]['dur'] = 0.1 + self._nfree(out) / 155.0

    def scan(self, out, d0, d1, init, op0, op1):
        self.op('dve', lambda e: e.tensor_tensor_scan(out, d0, d1, init, op0, op1), [d0, d1, init], [out])

    def dma(self, out, in_, q='sp', slow=False, rkeys=(), wkeys=()):
        if slow:
            self.op(q, lambda e: e.dma_start(out=out, in_=in_, allow_slow_non_contiguous=True), [in_], [out], dma=True,
                    rkeys=rkeys, wkeys=wkeys)
        else:
            self.op(q, lambda e: e.dma_start(out=out, in_=in_), [in_], [out], dma=True, rkeys=rkeys, wkeys=wkeys)

    def reschedule(self, window=96, lat=0.25, dma_lat=3.0):
        ops = self.ops
        n = len(ops)
        last_w, readers = {}, {}
        preds = [None] * n
        for i, o in enumerate(ops):
            d = set()
            for b in o['rb']:
                w = last_w.get(b)
                if w is not None:
                    d.add(w)
            for b in o['wb']:
                w = last_w.get(b)
                if w is not None:
                    d.add(w)
                d.update(readers.get(b, ()))
            d.discard(i)
            preds[i] = d
            for b in o['rb']:
                readers.setdefault(b, []).append(i)
            for b in o['wb']:
                last_w[b] = i
                readers[b] = []
        queues = {}
        for i, o in enumerate(ops):
            queues.setdefault(o['eng'], []).append(i)
        dmas = [i for i, o in enumerate(ops) if o['dma']]
        for k in range(NDMA, len(dmas)):
            preds[dmas[k]].add(dmas[k - NDMA])
        done = [None] * n
        start = [None] * n
        free = {e: 0.0 for e in queues}
        head = {e: 0 for e in queues}
        issued = [False] * n
        remaining = n
        INF = 1e30

        def ready_time(i):
            t = 0.0
            for p_ in preds[i]:
                if not issued[p_]:
                    return INF
                dp = done[p_] + (0.0 if ops[p_]['eng'] == ops[i]['eng'] and not ops[p_]['dma'] else lat)
                if dp > t:
                    t = dp
            return t
        while remaining:
            best = None
            for e, q in queues.items():
                h = head[e]
                while h < len(q) and issued[q[h]]:
                    h += 1
                head[e] = h
                if h >= len(q):
                    continue
                hi = q[h]
                hr = ready_time(hi)
                hstart = max(free[e], hr) if hr < INF else INF
                cand = (hstart, hi)
                if hstart > free[e] + 1e-9:
                    cnt = 0
                    j = h + 1
                    while j < len(q) and cnt < window:
                        oj = q[j]
                        j += 1
                        if issued[oj]:
                            continue
                        cnt += 1
                        if ops[oj]['dma']:
                            continue
                        r = ready_time(oj)
                        if r >= INF:
                            continue
                        s = max(free[e], r)
                        if s + ops[oj]['dur'] <= hstart + 1e-9 and s < cand[0]:
                            cand = (s, oj)
                if cand[0] < INF and (best is None or cand < best[0:2]):
                    best = (cand[0], cand[1], e)
            assert best is not None, "scheduler deadlock"
            s, i, e = best
            o = ops[i]
            start[i] = s
            issued[i] = True
            remaining -= 1
            free[e] = s + o['dur']
            done[i] = s + o['dur'] + (dma_lat if o['dma'] else 0.0)
        order = sorted(range(n), key=lambda i: (start[i], i))
        self.ops = [ops[i] for i in order]
        self.sim_time = max(done)

    def build(self, resched=True):
        nc = self.nc
        if resched:
            self.reschedule()
        ENG = ['pe', 'dve', 'act', 'pool', 'sp']
        engobj = {'pe': nc.tensor, 'dve': nc.vector, 'act': nc.scalar, 'pool': nc.gpsimd, 'sp': nc.sync}
        last_w = {}
        readers = {}
        bank_mode = {}
        known = {e: {} for e in ENG}
        eidx = {e: 0 for e in ENG}
        slot_last = [None] * NDMA
        slot_cnt = [0] * NDMA
        ndma = 0
        ops = self.ops
        for i, o in enumerate(ops):
            e = o['eng']
            deps = set()
            for b in o['rb']:
                w = last_w.get(b)
                if w is not None:
                    deps.add(w)
            for b in o['wb']:
                w = last_w.get(b)
                if w is not None:
                    deps.add(w)
                for r in readers.get(b, ()):
                    deps.add(r)
            forced = set()
            if 'pemode' in o:
                bank, mode = o['pemode']
                lm = bank_mode.get(bank)
                if lm is not None and lm[0] != mode:
                    deps.add(lm[1])
                    forced.add(lm[1])
                bank_mode[bank] = (mode, i)
            if o['dma']:
                k = ndma % NDMA
                ndma += 1
                o['slot'] = k
                if slot_last[k] is not None:
                    deps.add(slot_last[k])
                slot_cnt[k] += 1
                o['dval'] = 16 * slot_cnt[k]
                slot_last[k] = i
                o['key'] = ('dma', k)
                o['kidx'] = slot_cnt[k]
            else:
                eidx[e] += 1
                o['key'] = ('eng', e)
                o['kidx'] = eidx[e]
            waits = []
            vc = {}
            kn = known[e]
            for d in sorted(deps, reverse=True):
                if d == i:
                    continue
                p = ops[d]
                if p['key'] == ('eng', 'pe') and e == 'pe' and not o['dma'] and d not in forced:
                    continue
                if kn.get(p['key'], 0) >= p['kidx']:
                    continue
                waits.append(d)
                p['signal'] = True
                for k2, v2 in p['vc'].items():
                    if kn.get(k2, 0) < v2:
                        kn[k2] = v2
            o['waits'] = waits
            vc = dict(kn)
            vc[o['key']] = o['kidx']
            o['vc'] = vc
            o.setdefault('signal', False)
            if o['dma']:
                o['signal'] = True
            for b in o['rb']:
                readers.setdefault(b, []).append(i)
            for b in o['wb']:
                last_w[b] = i
                readers[b] = []
        self.final_dma = [(k, 16 * slot_cnt[k]) for k in range(NDMA) if slot_cnt[k]]
        sems = {e: self.es.enter_context(nc.semaphore("s_" + e)) for e in ENG}
        dsems = [self.es.enter_context(nc.semaphore("d_%d" % k)) for k in range(NDMA)]
        sig = {e: 0 for e in ENG}
        for o in ops:
            if o['dma']:
                o['sem'] = dsems[o['slot']]
                o['sval'] = o['dval']
            elif o['signal']:
                sig[o['eng']] += 1
                o['sem'] = sems[o['eng']]
                o['sval'] = sig[o['eng']]
        self.nwaits = sum(len(o['waits']) for o in ops)
        self.nsig = dict(sig)
        per = {e: [o for o in ops if o['eng'] == e] for e in ENG}
        block = self.es.enter_context(nc.Block())

        def emit(eng_name):
            def f(eo):
                for o in per[eng_name]:
                    for d in o['waits']:
                        p = ops[d]
                        eo.wait_ge(p['sem'], p['sval'])
                    ins = o['fn'](eo)
                    if o['dma']:
                        ins.then_inc(o['sem'], 16)
                    elif o['signal']:
                        ins.then_inc(o['sem'], 1)
                if eng_name == 'sp':
                    for k, v in self.final_dma:
                        eo.wait_ge(dsems[k], v)
            return f
        block.tensor(emit('pe'))
        block.vector(emit('dve'))
        block.scalar(emit('act'))
        block.gpsimd(emit('pool'))
        block.sync(emit('sp'))
        self.es.close()
        return nc


D = 1024
C0 = float(np.exp(-0.5))
NPC = 384
F32R = mybir.dt.float32r
RDT = BF16
O_ID, O_ONES2, O_MG, O_MNT, O_ID2, O_D0, O_BC, O_BP, O_SK, NCST = 0, 128, 256, 384, 512, 640, 1152, 2176, 3200, 3208


def host_consts():
    c = np.zeros((128, NCST), np.float32)
    c[:, O_ID:O_ID + 128] = np.eye(128)
    c[0:64, O_ONES2:O_ONES2 + 64] = 1.0
    c[64:128, O_ONES2 + 64:O_ONES2 + 128] = 1.0
    j = np.arange(64)[:, None]
    i = np.arange(64)[None, :]
    lt = (j < i).astype(np.float32)
    le = (j <= i).astype(np.float32)
    mg = np.zeros((128, 128), np.float32)
    mg[0:64, 0:64] = lt
    mg[0:64, 64:128] = le
    mg[64:128, 0:64] = lt
    mg[64:128, 64:128] = le
    c[:, O_MG:O_MG + 128] = mg
    c[0:64, O_MNT:O_MNT + 64] = (i.T > j.T).astype(np.float32).T * 0 + (np.arange(64)[:, None] > np.arange(64)[None, :])
    c[0:64, O_MNT + 64:O_MNT + 128] = c[0:64, O_MNT:O_MNT + 64]
    c[0:64, O_ID2:O_ID2 + 64] = np.eye(64)
    c[0:64, O_ID2 + 64:O_ID2 + 128] = np.eye(64)
    d0 = np.ones((512,), np.float32)
    d0[0::64] = 0.0
    c[:, O_D0:O_D0 + 512] = d0[None, :]
    key = np.arange(128)[:, None]
    q = np.arange(128)[None, :]
    for h in range(8):
        slope = 2.0 ** (-(h + 1))
        cur = np.where(q >= key, -slope * (q - key), -30000.0)
        prv = np.where(key > q, -slope * (q + 128 - key), -30000.0)
        c[:, O_BC + h * 128:O_BC + (h + 1) * 128] = cur
        c[:, O_BP + h * 128:O_BP + (h + 1) * 128] = prv
    return c


def host_pcol(inp):
    pc = np.zeros((128, NPC), np.float32)

    def put(col, vec):
        v = np.asarray(vec, np.float32).reshape(-1, 128)
        pc[:, col:col + v.shape[0]] = v.T
    put(0, inp['norm_mix_g'])
    put(16, inp['norm_ffn_g'])
    put(32, inp['final_norm_g'])
    put(40, inp['hy_mu'][0])
    put(54, inp['hy_w0'][0])
    put(58, inp['hy_a0'][0])
    put(62, inp['hy_k_k'][0])
    put(66, inp['hy_k_a'][0])
    put(70, inp['hy_r_k'][0])
    put(74, inp['hy_gn_g'][0])
    put(78, inp['hy_gn_b'][0])
    put(82, inp['cv_pw1_b'][0])
    put(98, inp['cv_dw_b'][0])
    put(106, inp['cv_ln_g'][0])
    put(114, inp['cv_ln_b'][0])
    put(122, inp['cv_pw2_b'][0])
    dw = np.asarray(inp['cv_dw_w'][0], np.float32)
    pc[:, 130:130 + 248] = dw.T.reshape(8, 128, 31).transpose(1, 0, 2).reshape(128, 248)
    return pc


def build(T=2048, dbg=(), stages=('att', 'rwkv', 'wout', 'mlp0', 'conv', 'mlp1'), nchunks=8, nhp=4, cut=99):
    p = Prog()
    NT = T // 512
    x = p.dram("x", [T, D], "ExternalInput")
    out = p.dram("out", [T, D], "ExternalOutput")
    w_in = p.dram("w_in", [D, 2560], "ExternalInput")
    w_out = p.dram("w_out", [D, D], "ExternalInput")
    pw1 = p.dram("pw1", [D, 2048], "ExternalInput")
    pw2 = p.dram("pw2", [D, D], "ExternalInput")
    w1 = p.dram("w1", [2, D, 4096], "ExternalInput")
    w2 = p.dram("w2", [2, 4096, D], "ExternalInput")
    lupd = p.dram("lup", [128, 512], "ExternalInput")
    gupd = p.dram("gup", [128, 512], "ExternalInput")
    cstd = p.dram("cst", [128, NCST], "ExternalInput")
    pcold = p.dram("pcol", [128, NPC], "ExternalInput")
    sinkd = p.dram("sinkb", [128, 8], "ExternalInput")
    dbg_outs = {}
    def wscr(name, K, M, lead=None):
        nt = (K // 512) * (M // 512)
        return p.dram(name, ([lead] if lead else []) + [nt, 128, 2048], "Internal", dtype=BF16)
    w_in_b = wscr("w_in_b", D, 2560)
    w_out_b = wscr("w_out_b", D, D)
    pw1_b = wscr("pw1_b", D, 2048)
    pw2_b = wscr("pw2_b", D, D)
    w1_b = wscr("w1_b", D, 4096, 2)
    w2_b = wscr("w2_b", 4096, D, 2)

    def dbg_out(name, ap):
        if name in dbg:
            shp = list(ap.shape)
            dt_ = p.dram("dbg_" + name, shp, "ExternalOutput", dtype=ap.dtype)
            p.dma(dt_, ap)

    cst = p.sb("cstsb", [128, NCST])
    pcol = p.sb("pcolsb", [128, NPC])
    ones = p.sb("ones", [128, 128])
    onesr = p.sb("onesr", [128, 128])
    sqr = [p.sb("sqr%d" % i, [128, 512]) for i in range(2)]
    esink = p.sb("esink", [128, 8])
    xT = p.sb("xT", [128, 8, T])
    wst = [p.sb("wst%d" % i, [128, 2, 512]) for i in range(2)]
    NWB = 2
    wbf = [p.sb("wbf%d" % i, [128, 4, 512], dtype=BF16) for i in range(NWB)]
    LUP = p.sb("LUP", [128, 512], dtype=BF16)
    GUP = p.sb("GUP", [128, 512], dtype=BF16)
    ones2b = p.sb("ones2b", [128, 128], dtype=BF16)
    Hbd = p.sb("Hbd", [128, 4, 128])
    UV = p.sb("UV", [128, 2, 128])
    ZV = p.sb("ZV", [128, 2, 64])
    Xn = [p.sb("Xn%d" % i, [64, 4, 128], dtype=RDT) for i in range(2)]
    Xtn = [p.sb("Xtn%d" % i, [64, 4, 128], dtype=RDT) for i in range(2)]
    Psh = [p.sb("Psh%d" % i, [64, 4, 128], dtype=RDT) for i in range(2)]
    Pa = p.sb("Pa", [64, 4, 128])
    Pb = [p.sb("Pb%d" % i, [64, 4, 128]) for i in range(2)]
    Zs = p.sb("Zs", [64, 128])
    BKhat = p.sb("BKhat", [128, 128])
    M2b = [p.sb("M2b%d" % i, [128, 4, 2, 128]) for i in range(2)]
    WC2 = [p.sb("WC2_%d" % i, [128, 4]) for i in range(2)]
    ST2 = [p.sb("ST2_%d" % i, [128, 4]) for i in range(2)]
    carry = p.sb("carry", [128, 16])
    kT = p.sb("kT", [128, 640], dtype=BF16)
    vTM = p.sb("vTM", [128, 5, 128], dtype=BF16)
    onesb = p.sb("onesb", [128, 128], dtype=BF16)
    st1 = p.sb("st1", [128, 512])
    NSCR = 19520
    scr = p.sb("scr", [128, NSCR])

    def bfv(a, b):
        return scr[:, a:b].bitcast(BF16)
    hT = bfv(17472, 17472 + 2048).rearrange("p (c t) -> p c t", t=512)
    hTf = scr[:, 8192:8192 + 4096].rearrange("p (c t) -> p c t", t=512)
    chalo = p.sb("chalo", [128, 8, 30], dtype=BF16)
    PS = [p.ps("ps%d" % i, [128, 512]) for i in range(8)]

    ident = cst[:, O_ID:O_ID + 128]
    ones2 = cst[:, O_ONES2:O_ONES2 + 128]
    maskG = cst[:, O_MG:O_MG + 128]
    maskNt2 = cst[0:64, O_MNT:O_MNT + 128]
    ident2 = cst[0:64, O_ID2:O_ID2 + 128]
    d0 = cst[:, O_D0:O_D0 + 512]
    Bcur = cst[:, O_BC:O_BC + 1024].rearrange("p (h q) -> p h q", q=128)
    Bprev = cst[:, O_BP:O_BP + 1024].rearrange("p (h q) -> p h q", q=128)

    def col(i):
        return pcol[:, i:i + 1]

    p.dma(cst, cstd)
    p.dma(pcol, pcold)
    p.dma(esink, sinkd)
    p.memset(ones, 1.0)
    p.copy(onesb, ones)
    p.copy(onesr.bitcast(F32R), ones)
    p.memset(Hbd, 0.0, eng='pool')
    p.memset(UV, 0.0, eng='pool')
    p.memset(ZV, 0.0, eng='pool')
    p.memset(carry, 0.0, eng='pool')
    p.act(esink, esink, AF.Exp)
    p.ts(pcol[:, 378:382], pcol[:, 66:70], -1.0, ALU.mult, 1.0, ALU.add)

    p.dma(scr[:, 8192:8704], lupd)
    p.dma(scr[:, 8704:9216], gupd)
    p.copy(LUP, scr[:, 8192:8704])
    p.copy(GUP, scr[:, 8704:9216], eng='act')
    p.copy(ones2b, ones2)
    xin_region = scr[:, 0:4 * D].rearrange("p (b d) -> p b d", d=D)
    for tt in range(NT):
        p.dma(xin_region, x[tt * 512:(tt + 1) * 512, :].rearrange("(b p) d -> p b d", p=128),
              q='sp')
        for c in range(8):
            ps_ = PS[4 + (c % 4)]
            for b in range(4):
                p.transpose(ps_[:, b * 128:(b + 1) * 128], xin_region[:, b, c * 128:(c + 1) * 128], ident)
            p.copy(xT[:, c, tt * 512:(tt + 1) * 512], ps_, eng='dve' if c % 2 == 0 else 'act')

    lin_ctr = [0, 0]
    pend_store = []

    def linear(W, Wb, K, M, rhs_fn, evac_fn, first, hook=None):
        nkg = K // 512
        nmg = M // 512
        for mg in range(nmg):
            nm = 4
            banks = PS[0:4] if lin_ctr[0] % 2 == 0 else PS[4:8]
            lin_ctr[0] += 1
            for kg in range(nkg):
                wt = wbf[lin_ctr[1] % NWB]
                key = (Wb.name, int(Wb.offset), kg, mg)
                wsrc = Wb[mg * nkg + kg].rearrange("p (kc m) -> p kc m", kc=4)
                if first:
                    for hf in range(2):
                        ws = wst[hf]
                        r0 = kg * 512 + hf * 256
                        p.dma(ws, W[r0:r0 + 256, mg * 512:(mg + 1) * 512].rearrange("(kc p) m -> p kc m", p=128), q='sp')
                        p.copy(wt[:, hf * 2:hf * 2 + 2, :], ws, eng='dve' if hf == 0 else 'act')
                    if pend_store:
                        pend_store.pop()()
                    pend_store.append(lambda wsrc=wsrc, wt=wt, key=key: p.dma(wsrc, wt, q='sp', wkeys=[key]))
                else:
                    p.dma(wt, wsrc, q='sp', rkeys=[key])
                lin_ctr[1] += 1
                for kc in range(4):
                    for m in range(nm):
                        p.mm(banks[m], wt[:, kc, m * 128:(m + 1) * 128], rhs_fn(kg * 4 + kc),
                             start=(kg == 0 and kc == 0), stop=(kg == nkg - 1 and kc == 3))
            for m in range(nm):
                evac_fn(mg * 4 + m, banks[m])
            if hook is not None and mg == 0:
                hook()
        if pend_store:
            pend_store.pop()()

    def rmsnorm_tile(tt, gcol0, dst=None):
        dst = hT if dst is None else dst
        ts_ = slice(tt * 512, (tt + 1) * 512)
        for c in range(8):
            sqb = sqr[c % 2]
            p.act(sqb.bitcast(F32R), xT[:, c, ts_], AF.Square)
            p.mm(PS[7], onesr.bitcast(F32R), sqb.bitcast(F32R), start=(c == 0), stop=(c == 7))
        p.act(st1, PS[7], AF.Sqrt, bias=1e-6, scale=1.0 / D)
        p.recip(st1, st1)
        for c in range(8):
            p.stt(dst[:, c, :], xT[:, c, ts_], col(gcol0 + c), st1, ALU.mult, ALU.mult)

    prenormed = set()

    def mlp_tile(tt, layer, mid=None):
        ts_ = slice(tt * 512, (tt + 1) * 512)
        rmsnorm_tile(tt, 16 + layer * 8)
        h1 = bfv(0, 8192).rearrange("p (c t) -> p c t", t=512)

        def ev1(mc, ps_):
            p.act(h1[:, mc, :], ps_, AF.Relu)
            p.tt(h1[:, mc, :], h1[:, mc, :], h1[:, mc, :], ALU.mult, eng='pool')
        linear(w1[layer], w1_b[layer], D, 4096, lambda kc: hT[:, kc, :], ev1, tt == 0)

        def ev2(mc, ps_):
            p.tt(xT[:, mc, ts_], ps_, xT[:, mc, ts_], ALU.add)
        linear(w2[layer], w2_b[layer], 4096, D, lambda kc: h1[:, kc, :], ev2, tt == 0, hook=mid)

    o = 0
    pl = scr[:, o:o + 14 * 512].rearrange("p (c t) -> p c t", t=512); o += 14 * 512
    ycat = bfv(o, o + 2048).rearrange("p (c t) -> p c t", t=512); o += 2048
    prawb = [scr[:, o:o + 513], scr[:, o + 544:o + 544 + 513]]; o += 1088
    TL = bfv(o, o + 256); o += 512
    SG = bfv(o, o + 256); o += 512
    SLOT0 = o
    assert SLOT0 + 12 * 512 == 17472, SLOT0
    slot = [scr[:, o + i * 512:o + (i + 1) * 512] for i in range(16)]
    qT = bfv(o, o + 1024).rearrange("p (c t) -> p c t", t=512)
    vattT = slot[4]
    sTa, sTb, denb = slot[5], slot[6], slot[9]
    eTa = bfv(SLOT0 + 7 * 512, SLOT0 + 7 * 512 + 256)
    eTb = bfv(SLOT0 + 8 * 512, SLOT0 + 8 * 512 + 256)
    o += 16 * 512
    assert o <= NSCR

    def hybrid_tile(tt):
        ts_ = slice(tt * 512, (tt + 1) * 512)
        if ('hyb', tt) not in prenormed:
            rmsnorm_tile(tt, 0)

        def ev_in(mc, ps_):
            if mc < 14:
                pb = prawb[mc % 2]
                p.copy(pb[:, 0:1], carry[:, mc:mc + 1], eng='pool')
                p.copy(pb[:, 1:513], ps_, eng='act')
                p.copy(carry[:, mc:mc + 1], pb[:, 512:513], eng='pool')
                p.tt(pl[:, mc, :], pb[:, 0:512], pb[:, 1:513], ALU.subtract)
                p.stt(pl[:, mc, :], pl[:, mc, :], col(40 + mc), pb[:, 1:513], ALU.mult, ALU.add)
            elif mc < 18:
                p.copy(qT[:, mc - 14, :], ps_, eng='act')
            elif mc == 18:
                p.copy(kT[:, 128:640], ps_, eng='act')
            else:
                p.copy(vattT, ps_, eng='act')
        linear(w_in, w_in_b, D, 2560, lambda kc: hT[:, kc, :], ev_in, tt == 0)

        if 'att' not in stages:
            return
        for b in range(4):
            p.transpose(PS[6][:, b * 128:(b + 1) * 128], vattT[:, b * 128:(b + 1) * 128], ident)
        p.copy(vTM[:, 1:5, :], PS[6].rearrange("p (b d) -> p b d", d=128))
        for blk in range(4):
            first = (tt == 0 and blk == 0)
            for g in range(2):
                rows = slice(g * 64, (g + 1) * 64)
                pc_, pp_ = (PS[1], PS[0]) if g == 0 else (PS[3], PS[2])
                qv = qT[rows, :, blk * 128:(blk + 1) * 128]
                p.mm(pc_, kT[rows, 128 + blk * 128:256 + blk * 128], qv)
                if not first:
                    p.mm(pp_, kT[rows, blk * 128:128 + blk * 128], qv)
                p.stt(sTa.rearrange("p (h q) -> p h q", q=128), pc_.rearrange("p (h q) -> p h q", q=128),
                      0.125, Bcur[:, 4 * g:4 * g + 4, :], ALU.mult, ALU.add)
                p.act(eTa, sTa, AF.Exp)
                if not first:
                    p.stt(sTb.rearrange("p (h q) -> p h q", q=128), pp_.rearrange("p (h q) -> p h q", q=128),
                          0.125, Bprev[:, 4 * g:4 * g + 4, :], ALU.mult, ALU.add)
                    p.act(eTb, sTb, AF.Exp)
                p.mm(PS[4], vTM[:, blk + 1, :], eTa, start=True, stop=first)
                if not first:
                    p.mm(PS[4], vTM[:, blk, :], eTb, start=False, stop=True)
                p.mm(PS[5], onesb, eTa, start=True, stop=first)
                if not first:
                    p.mm(PS[5], onesb, eTb, start=False, stop=True)
                den3 = denb.rearrange("p (h q) -> p h q", q=128)
                p.tt(den3, PS[5].rearrange("p (h q) -> p h q", q=128),
                     esink[:, 4 * g:4 * g + 4].unsqueeze(2).to_broadcast([128, 4, 128]), ALU.add)
                p.recip(denb, denb)
                p.tt(ycat[rows, 4:8, blk * 128:(blk + 1) * 128],
                     PS[4][rows, :].rearrange("p (h q) -> p h q", q=128), den3[rows], ALU.mult)
        p.copy(kT[:, 0:128], kT[:, 512:640], eng='pool')
        p.copy(vTM[:, 0, :], vTM[:, 4, :], eng='pool')

        if 'rwkv' not in stages:
            dbg_out("ycat%d" % tt, ycat)
            return
        p.copy(TL[64:128, :], pl[64:128, 12, :], eng='pool')
        p.act(TL[0:64, :], pl[0:64, 12, :], AF.Tanh)
        p.act(SG, pl[:, 13, :], AF.Sigmoid)

        def sl(i, n=512):
            b8 = SLOT0 + i * 512
            return scr[:, b8:b8 + n]

        def bigv(i):
            a_ = sl(i)
            return (a_.rearrange("p (c two t) -> p c two t", two=2, t=64), a_.rearrange("p (c n) -> p c n", n=128))
        BIG = [[bigv(s_ * 4 + k_) for k_ in range(4)] for s_ in range(2)]
        PT = [scr[:, SLOT0 + 8 * 512 + i * 256:SLOT0 + 8 * 512 + (i + 1) * 256] for i in range(8)]
        YR = [sl(12), sl(13)]
        POSTT = [sl(14), sl(15), prawb[0][:, 0:512], prawb[1][:, 0:512]]

        def v3(a):
            return a.rearrange("p (c t) -> p c t", t=64)

        def prep_closures(u):
            hp, half = u // 2, u % 2
            par = u % 2
            cs = slice(hp * 128, (hp + 1) * 128)
            tok = slice(half * 256, (half + 1) * 256)
            rl, kl, vl = pl[:, hp, tok], pl[:, 4 + hp, tok], pl[:, 8 + hp, tok]
            sS, sEe, sEi, sEni, sA, sKK, sT, sB = PT
            sTb = sT.bitcast(BF16)[:, 0:256]
            (AR4, AR3), (BK4, BK3), (BKh4, BKh3), (XV4, XV3) = BIG[par]
            WCu, STu = WC2[par], ST2[par]
            d0h = d0[:, 0:256]

            def c1():
                p.mm(PS[1][:, 0:256], LUP[0:64, cs], TL[0:64, tok])
                p.act(sEe, PS[1][:, 0:256], AF.Sigmoid, bias=col(54 + hp))
                p.scan(sS, d0h, sEe, 0.0, ALU.mult, ALU.add)
                p.tt(sEe, sS, sEe, ALU.subtract)
                p.act(sEe, sEe, AF.Exp, scale=-C0)

            def c2():
                p.act(sEi, sS, AF.Exp, scale=-C0)
                p.act(sEni, sS, AF.Exp, scale=C0)
                p.copy(STu, v3(sS)[:, :, 63], eng='pool')
                p.act(WCu, STu, AF.Exp, scale=-C0)
                p.tt(v3(sS), STu.unsqueeze(2).to_broadcast([128, 4, 64]), v3(sS), ALU.subtract, eng='pool')
                p.act(sS, sS, AF.Exp, scale=-C0)

            def c3():
                p.ts(sKK, kl, col(62 + hp), ALU.mult)
                p.act(sTb, sKK, AF.Square)
                p.mm(PS[3][:, 0:256], LUP[64:128, cs], TL[64:128, tok])
                p.act(sA, PS[3][:, 0:256], AF.Sigmoid, bias=col(58 + hp))

            def c3b():
                p.mm(PS[5][:, 0:256], ones2b, sTb)
                p.act(sT, PS[5][:, 0:256], AF.Sqrt)

            def c4():
                p.ts(sT, sT, 1e-12, ALU.max)
                p.recip(sT, sT)
                p.tt(sKK, sKK, sT, ALU.mult)
                p.ts(sT, sA, col(66 + hp), ALU.mult, col(378 + hp), ALU.add)
                p.tt(kl, kl, sT, ALU.mult, eng='pool')
                p.tt(sB, sKK, sA, ALU.mult, eng='pool')

            def c5():
                p.stt(AR4[:, :, 0, :], v3(sKK), -1.0, v3(sEe), ALU.mult, ALU.mult)
                p.tt(AR4[:, :, 1, :], v3(rl), v3(sEi), ALU.mult, eng='pool')
                p.tt(BK4[:, :, 0, :], v3(sB), v3(sEni), ALU.mult)
                p.tt(BK4[:, :, 1, :], v3(kl), v3(sEni), ALU.mult, eng='pool')

            def c6():
                p.tt(BKh4[:, :, 0, :], v3(sB), v3(sS), ALU.mult)
                p.tt(BKh4[:, :, 1, :], v3(kl), v3(sS), ALU.mult, eng='pool')
                p.memset(XV4[:, :, 0, :], 0.0, eng='pool')
                p.copy(XV4[:, :, 1, :], v3(vl), eng='pool')
            return [c1, c2, c3, c3b, c4, c5, c6]

        def pre_closures(u):
            par = u % 2
            (AR4, AR3), (BK4, BK3), _, _ = BIG[par]
            M2h = M2b[par]
            Pfin = Pb[par]
            st = []

            def stageA():
                for ci in range(4):
                    for h in range(2):
                        rows = slice(h * 64, (h + 1) * 64)
                        bank = PS[h * 2 + ci // 2]
                        o_ = (ci % 2) * 256
                        p.mm(bank[:, o_:o_ + 128], BK3[rows, ci, :], AR3[rows, ci, :])
                        p.mm(bank[:, o_ + 128:o_ + 192], AR3[rows, ci, :], BK4[rows, ci, 0, :])
                for h in range(2):
                    for pr in range(2):
                        bank = PS[h * 2 + pr].rearrange("p (c n) -> p c n", n=256)
                        p.tt(M2h[:, pr * 2:pr * 2 + 2, h, :], bank[:, :, 0:128],
                             maskG.unsqueeze(1).to_broadcast([128, 2, 128]), ALU.mult)
                        p.tt(Xtn[0][:, pr * 2:pr * 2 + 2, h * 64:(h + 1) * 64], bank[0:64, :, 128:192],
                             maskNt2[:, 0:64].unsqueeze(1).to_broadcast([64, 2, 64]), ALU.mult)
                p.tt(Pa.rearrange("p c (h t) -> p (c h) t", t=64),
                     M2h[0:64].rearrange("p c h n -> p (c h) n")[:, :, 0:64],
                     ident2[:, 0:64].unsqueeze(1).to_broadcast([64, 8, 64]), ALU.add, eng='pool')
                p.copy(Xn[0].rearrange("p c (h t) -> p (c h) t", t=64),
                       M2h[0:64].rearrange("p c h n -> p (c h) n")[:, :, 0:64], eng='pool')
                p.copy(Psh[0].rearrange("p c n -> p (c n)"), Pa.rearrange("p c n -> p (c n)"), eng='act')
            st.append(stageA)

            def mk_round(k, sb):
                cis = (0, 1) if sb == 0 else (2, 3)
                bX, bXt = (PS[0], PS[1]) if sb == 0 else (PS[2], PS[3])
                c0 = cis[0]

                def sq():
                    nx = (k + 1) % 2
                    for ci in cis:
                        for h in range(2):
                            hs = slice(h * 64, (h + 1) * 64)
                            xc = Xn[k % 2][:, ci, hs]
                            xtc = Xtn[k % 2][:, ci, hs]
                            uo = ((ci - c0) * 2 + h) * 64
                            if k < 4:
                                p.mm(bX[0:64, uo:uo + 64], xtc, xc)
                            p.mm(bXt[0:64, uo:uo + 64], xc, xtc)
                    if k < 4:
                        p.copy(Xn[nx][:, c0:c0 + 2, :].rearrange("p c n -> p (c n)"), bX[0:64, 0:256], eng='act')
                    p.copy(Xtn[nx][:, c0:c0 + 2, :].rearrange("p c n -> p (c n)"), bXt[0:64, 0:256], eng='dve')

                def pu():
                    nx = (k + 1) % 2
                    Pcur = Pa if k % 2 == 0 else Pfin
                    Pnxt = Pfin if k % 2 == 0 else Pa
                    for ci in cis:
                        for h in range(2):
                            hs = slice(h * 64, (h + 1) * 64)
                            uo = 256 + ((ci - c0) * 2 + h) * 64
                            p.mm(bX[0:64, uo:uo + 64], Xtn[nx][:, ci, hs], Psh[k % 2][:, ci, hs])
                    p.tt(Pnxt[:, c0:c0 + 2, :].rearrange("p c n -> p (c n)"), bX[0:64, 256:512],
                         Pcur[:, c0:c0 + 2, :].rearrange("p c n -> p (c n)"), ALU.add)
                    if k < 4:
                        p.copy(Psh[nx][:, c0:c0 + 2, :].rearrange("p c n -> p (c n)"),
                               Pnxt[:, c0:c0 + 2, :].rearrange("p c n -> p (c n)"), eng='act')
                return sq, pu
            for k in range(5):
                sq0, pu0 = mk_round(k, 0)
                sq1, pu1 = mk_round(k, 1)
                st += [sq0, sq1, pu0, pu1]
            return st

        def serial_closures(u):
            hp, half = u // 2, u % 2
            par = u % 2
            (AR4, AR3), _, (BKh4, BKh3), (XV4, XV3) = BIG[par]
            yraw = YR[hp % 2]
            out_ = []
            for ci in range(4):
                def mk(ci=ci):
                    ARc = AR4[:, ci]
                    M2 = M2b[par][:, ci]
                    Tt = Pb[par][:, ci, :]
                    c = half * 4 + ci

                    def s1():
                        p.transpose(PS[6][:, 0:128], BKh3[:, ci, :], ident)
                        p.transpose(PS[6][:, 128:256], XV3[:, ci, :], ident)
                        p.copy(BKhat, PS[6][:, 0:128], eng='act')
                        for h in range(2):
                            src = PS[6][64:128, 128 + h * 64:128 + (h + 1) * 64]
                            p.copy(UV[64:128, h, h * 64:(h + 1) * 64], src, eng='act')
                            p.copy(ZV[64:128, h, :], src, eng='act')

                    def s2():
                        for h in range(2):
                            zo = PS[4][0:64, h * 64:(h + 1) * 64]
                            p.mm(zo, ARc[:, 0, :], Hbd[:, hp, h * 64:(h + 1) * 64], start=True, stop=False)
                            p.mm(zo, M2[:, h, 0:64], ZV[:, h, :], start=False, stop=True)
                        p.copy(Zs, PS[4][0:64, 0:128], eng='dve')
                        for h in range(2):
                            p.mm(PS[4][0:64, 256 + h * 64:256 + (h + 1) * 64], Tt[:, h * 64:(h + 1) * 64], Zs[:, h * 64:(h + 1) * 64])
                        for h in range(2):
                            p.copy(UV[0:64, h, h * 64:(h + 1) * 64], PS[4][0:64, 256 + h * 64:256 + (h + 1) * 64], eng='dve')

                    def s3():
                        p.mm(PS[7][:, 0:64], Hbd[:, hp, :], ARc[:, 1, :], start=True, stop=False)
                        p.mm(PS[7][:, 0:64], UV[:, 0, :], M2[:, 0, 64:128], start=False, stop=False)
                        p.mm(PS[7][:, 0:64], UV[:, 1, :], M2[:, 1, 64:128], start=False, stop=True)
                        p.mm(PS[7][:, 128:256], BKhat, UV[:, 0, :], start=True, stop=False)
                        p.mm(PS[7][:, 128:256], BKhat, UV[:, 1, :], start=False, stop=True)
                        p.copy(yraw[:, c * 64:(c + 1) * 64], PS[7][:, 0:64], eng='dve')
                        for h in range(2):
                            rows = slice(h * 64, (h + 1) * 64)
                            hb = Hbd[rows, hp, h * 64:(h + 1) * 64]
                            p.stt(hb, hb, WC2[par][rows, ci:ci + 1], PS[7][rows, 128 + h * 64:128 + (h + 1) * 64], ALU.mult, ALU.add)
                    return [s1, s2, s3]
                out_ += mk()
            return out_

        def post_closures(hp):
            cs = slice(hp * 128, (hp + 1) * 128)
            rl, kl, vl = pl[:, hp, :], pl[:, 4 + hp, :], pl[:, 8 + hp, :]
            yraw = YR[hp % 2]
            t1, t2, t3, t4 = POSTT

            sqb_ = t3.bitcast(BF16)[:, 0:512]
            yrb_ = t3.bitcast(BF16)[:, 512:1024]
            t4b_ = t4.bitcast(BF16)[:, 0:512]

            def q1():
                p.act(sqb_, yraw, AF.Square)
                p.copy(yrb_, yraw, eng='pool')
                p.mm(PS[5], ones2b, yrb_)
                p.ts(t2, PS[5], 1.0 / 64, ALU.mult)

            def q1b():
                p.mm(PS[5], ones2b, sqb_)
                p.tt(t3, t2, t2, ALU.mult, eng='pool')
                p.stt(t3, PS[5], 1.0 / 64, t3, ALU.mult, ALU.subtract)
                p.stt(t4b_, rl, col(70 + hp), kl, ALU.mult, ALU.mult)

            def q1c():
                p.mm(PS[5], ones2b, t4b_)
                p.act(t3, t3, AF.Sqrt, bias=64e-5)
                p.tt(t4, PS[5], vl, ALU.mult)
                p.tt(t1, yraw, t2, ALU.subtract, eng='pool')

            def q2():
                p.recip(t3, t3)
                p.mm(PS[5], GUP[:, cs], SG)
                p.tt(t1, t1, t3, ALU.mult)
                p.ts(t1, t1, col(74 + hp), ALU.mult, col(78 + hp), ALU.add)

            def q3():
                p.tt(t1, t1, t4, ALU.add, eng='pool')
                p.tt(ycat[:, hp, :], PS[5], t1, ALU.mult)
            return [q1, q1b, q1c, q2, q3]

        def interleave(fg, bg):
            nf, nb = len(fg), len(bg)
            bi = 0
            for i, f_ in enumerate(fg):
                f_()
                tgt = ((i + 1) * nb) // max(nf, 1)
                while bi < tgt:
                    bg[bi]()
                    bi += 1
            while bi < nb:
                bg[bi]()
                bi += 1

        NU = 2 * nhp
        for f_ in prep_closures(0) + pre_closures(0):
            f_()
        for u in range(NU):
            bg = []
            if u % 2 == 0 and u >= 2:
                bg += post_closures(u // 2 - 1)
            if u + 1 < NU:
                bg += prep_closures(u + 1) + pre_closures(u + 1)
            interleave(serial_closures(u), bg)
        for f_ in post_closures(nhp - 1):
            f_()
        dbg_out("ycat%d" % tt, ycat)
        if 'wout' not in stages:
            return

        def ev_out(mc, ps_):
            p.tt(xT[:, mc, ts_], ps_, xT[:, mc, ts_], ALU.add)
        linear(w_out, w_out_b, D, D, lambda kc: ycat[:, kc, :], ev_out, tt == 0)

    o = 0
    U_ = bfv(o, o + 2176).rearrange("p (c t) -> p c t", t=544); o += 2176
    acc = scr[:, o:o + 8 * 512].rearrange("p (c t) -> p c t", t=512); o += 8 * 512
    SQ2 = o
    vT = bfv(o, o + 2048).rearrange("p (c t) -> p c t", t=512); o += 2048
    gsig = [scr[:, o:o + 512], scr[:, o + 512:o + 1024]]; o += 1024
    cm, cr, ctmp = scr[:, o:o + 512], scr[:, o + 512:o + 1024], scr[:, o + 1024:o + 1536]; o += 1536
    Dg = [bfv(o, o + 1984).rearrange("p (j q) -> p j q", q=128), bfv(o + 1984, o + 3968).rearrange("p (j q) -> p j q", q=128)]
    o += 3968
    assert o <= 17472

    def build_diag(c):
        p.tt(Dg[c % 2], ident.unsqueeze(1).to_broadcast([128, 31, 128]),
             pcol[:, 130 + c * 31:130 + (c + 1) * 31].unsqueeze(2).to_broadcast([128, 31, 128]), ALU.mult,
             eng='pool' if c % 2 == 0 else 'dve')

    def conv_tile(tt):
        ts_ = slice(tt * 512, (tt + 1) * 512)
        if ('conv', tt) not in prenormed:
            rmsnorm_tile(tt, 8)
        if tt == 0:
            p.memset(U_[:, :, 0:30], 0.0, eng='pool')
        else:
            p.copy(U_[:, :, 0:30], chalo, eng='pool')
        build_diag(0)
        build_diag(1)

        def ev_pw1(mc, ps_):
            if mc < 8:
                p.act(acc[:, mc, :], ps_, AF.Identity, bias=col(82 + mc))
            else:
                m = mc - 8
                gb = gsig[m % 2]
                p.act(gb, ps_, AF.Sigmoid, bias=col(82 + mc))
                p.tt(U_[:, m, 30:542], acc[:, m, :], gb, ALU.mult, eng='pool')
        linear(pw1, pw1_b, D, 2048, lambda kc: hT[:, kc, :], ev_pw1, tt == 0)
        p.copy(chalo, U_[:, :, 512:542], eng='pool')
        for c in range(8):
            ps_ = PS[4 + (c % 4)]
            for j in range(31):
                p.mm(ps_, Dg[c % 2][:, j, :], U_[:, c, j:j + 512], start=(j == 0), stop=(j == 30))
            p.act(acc[:, c, :], ps_, AF.Identity, bias=col(98 + c))
            if c + 2 < 8:
                build_diag(c + 2)
        for c in range(8):
            p.mm(PS[6], ones, acc[:, c, :], start=(c == 0), stop=(c == 7))
        for c in range(8):
            sqb = sqr[c % 2]
            p.act(sqb.bitcast(F32R), acc[:, c, :], AF.Square)
            p.mm(PS[7], onesr.bitcast(F32R), sqb.bitcast(F32R), start=(c == 0), stop=(c == 7))
        p.ts(cm, PS[6], 1.0 / D, ALU.mult)
        p.tt(ctmp, cm, cm, ALU.mult, eng='pool')
        p.stt(cr, PS[7], 1.0 / D, ctmp, ALU.mult, ALU.subtract)
        p.act(cr, cr, AF.Sqrt, bias=1e-5)
        p.recip(cr, cr)
        for c in range(8):
            p.tt(acc[:, c, :], acc[:, c, :], cm, ALU.subtract)
            p.tt(acc[:, c, :], acc[:, c, :], cr, ALU.mult, eng='pool')
            p.act(vT[:, c, :], acc[:, c, :], AF.Silu, bias=col(114 + c), scale=col(106 + c))
        dbg_out("vT%d" % tt, vT)

        def ev_pw2(mc, ps_):
            p.stt(xT[:, mc, ts_], ps_, col(122 + mc), xT[:, mc, ts_], ALU.add, ALU.add)
        linear(pw2, pw2_b, D, D, lambda kc: vT[:, kc, :], ev_pw2, tt == 0)

    for tt in range(NT):
        hybrid_tile(tt)
        dbg_out("xmix%d" % tt, xT[:, :, tt * 512:(tt + 1) * 512])
        if 'mlp0' in stages:
            def mid0(tt=tt):
                if tt + 1 < NT:
                    rmsnorm_tile(tt + 1, 0)
                    prenormed.add(('hyb', tt + 1))
                elif 'conv' in stages:
                    rmsnorm_tile(0, 8)
                    prenormed.add(('conv', 0))
            mlp_tile(tt, 0, mid0)
    dbg_out("xl0", xT)
    for tt in range(NT):
        if 'conv' in stages:
            conv_tile(tt)
        if 'mlp1' in stages:
            def mid1(tt=tt):
                if tt + 1 < NT and 'conv' in stages:
                    rmsnorm_tile(tt + 1, 8)
                    prenormed.add(('conv', tt + 1))
            mlp_tile(tt, 1, mid1)
    yo = scr[:, 0:4 * D].rearrange("p (b d) -> p b d", d=D)
    for tt in range(NT):
        rmsnorm_tile(tt, 32, dst=hTf)
        for b in range(4):
            for half in range(2):
                ps_ = PS[4 + ((b * 2 + half) % 4)]
                for cc in range(4):
                    c = half * 4 + cc
                    p.transpose(ps_[:, cc * 128:(cc + 1) * 128], hTf[:, c, b * 128:(b + 1) * 128], ident)
                p.copy(yo[:, b, half * 512:(half + 1) * 512], ps_, eng='dve' if half == 0 else 'act')
        p.dma(out[tt * 512:(tt + 1) * 512, :].rearrange("(b p) d -> p b d", p=128), yo,
              q='sp')
    nc = p.build()
    return nc, p


def host_weights(inp):
    w_in = np.asarray(inp['hy_w_in'][0], np.float32)
    qcols = 1792 + np.concatenate([np.concatenate([np.arange(j * 64, (j + 1) * 64), np.arange((j + 4) * 64, (j + 5) * 64)])
                                   for j in range(4)])
    perm = np.concatenate([np.arange(1792), qcols, np.arange(2304, 2560)])
    w_in_p = np.ascontiguousarray(w_in[:, perm])
    w_out = np.asarray(inp['hy_w_out'][0], np.float32)
    rperm = np.concatenate([np.arange(512), 512 + (qcols - 1792)])
    w_out_p = np.ascontiguousarray(w_out[rperm, :])
    lup = np.ascontiguousarray(np.concatenate([inp['hy_w_up'][0], inp['hy_a_up'][0]], 0).astype(np.float32))
    gup = np.ascontiguousarray(np.asarray(inp['hy_g_up'][0], np.float32))
    sinkb = np.ascontiguousarray(np.broadcast_to(np.asarray(inp['hy_sinks'][0], np.float32)[None, :], (128, 8)))
    return dict(w_in=w_in_p, w_out=w_out_p, pw1=np.ascontiguousarray(inp['cv_pw1_w'][0]),
                pw2=np.ascontiguousarray(inp['cv_pw2_w'][0]), w1=np.ascontiguousarray(inp['mlp_w1']),
                w2=np.ascontiguousarray(inp['mlp_w2']), lup=lup, gup=gup, sinkb=sinkb,
                cst=host_consts(), pcol=host_pcol(inp))


_CACHE = {}


def kernel(**inputs):
    inputs = {k: np.asarray(v) for k, v in inputs.items()}
    x = np.asarray(inputs['x'], np.float32)
    B, T, _ = x.shape
    shared = host_weights(inputs)
    nc, _ = build(T)
    in_maps = [dict(shared, x=np.ascontiguousarray(x[b])) for b in range(B)]
    res = run_bass_kernel_spmd(nc, in_maps, core_ids=list(range(B)))
    return np.stack([np.asarray(r["out"], np.float32) for r in res.results], 0)
```

```python
import numpy as np
from contextlib import ExitStack
import concourse.bass as bass
import concourse.mybir as mybir
from concourse.bass_utils import run_bass_kernel_spmd

F32 = mybir.dt.float32
BF16 = mybir.dt.bfloat16
ALU = mybir.AluOpType
AF = mybir.ActivationFunctionType
AX = mybir.AxisListType

NDMA = 12
BLK = 256
DTSZ = {mybir.dt.float32: 4, mybir.dt.bfloat16: 2, mybir.dt.float32r: 4}


class Prog:
    def __init__(self):
        self.nc = bass.Bass("TRN2", target_bir_lowering=False)
        self.es = ExitStack()
        self.ops = []
        self.tinfo = {}
        self.psum_names = set()
        self.ndma = 0

    def dram(self, name, shape, kind, dtype=F32):
        return self.nc.dram_tensor(name, list(shape), dtype, kind=kind).ap()

    def sb(self, name, shape, dtype=F32):
        t = self.es.enter_context(self.nc.sbuf_tensor(name, list(shape), dtype))
        ap = t[:]
        self.tinfo[ap.name] = int(np.prod(shape[1:])) * DTSZ[dtype]
        return ap

    def ps(self, name, shape, dtype=F32):
        t = self.es.enter_context(self.nc.psum_tensor(name, list(shape), dtype))
        ap = t[:]
        self.tinfo[ap.name] = int(np.prod(shape[1:]))
        self.psum_names.add(ap.name)
        return ap

    def blocks(self, ap):
        name = ap.name
        if name not in self.tinfo:
            return []
        if name in self.psum_names:
            return [(name, 0)]
        fs = self.tinfo[name]
        sz = DTSZ[ap.dtype]
        off = (int(ap.offset) * sz) % fs
        ext = 1
        for (st, cn) in ap.ap[1:]:
            ext += (cn - 1) * abs(st)
        ext *= sz
        b0 = off // BLK
        b1 = (off + ext - 1) // BLK
        return [(name, b) for b in range(b0, b1 + 1)]

    @staticmethod
    def _nfree(ap):
        n = 1
        for (st, cn) in ap.ap[1:]:
            n *= cn
        return n

    def _dur(self, eng, reads, writes, dma):
        n = self._nfree(writes[0]) if writes else 1
        if dma:
            return 0.1
        if eng == 'pe':
            return 0.06
        if eng == 'dve':
            return 0.08 + n / 960.0
        if eng == 'act':
            return 0.2 + n / 1200.0
        if eng == 'pool':
            return 0.3 + n / 480.0
        return 0.1

    def op(self, eng, fn, reads, writes, dma=False, rkeys=(), wkeys=()):
        rb = list(rkeys)
        for a in reads:
            if a is not None and not isinstance(a, (int, float)):
                rb += self.blocks(a)
        wb = list(wkeys)
        for a in writes:
            wb += self.blocks(a)
        wb += [b for b in rb if b[0] in self.psum_names]
        self.ops.append(dict(eng=eng, fn=fn, rb=rb, wb=wb, dma=dma, dur=self._dur(eng, reads, writes, dma)))

    @staticmethod
    def _cls(n):
        return 32 if n <= 32 else (64 if n <= 64 else 128)

    def _pemode(self, lhsT, tr):
        m = 1
        for (st, cn) in lhsT.ap[1:]:
            m *= cn
        return (int(lhsT.base_partition()), self._cls(int(lhsT.partition_size())), self._cls(m), tr)

    def mm(self, out, lhsT, rhs, start=True, stop=True):
        self.op('pe', lambda e: e.matmul(out, lhsT, rhs, start=start, stop=stop),
                [lhsT, rhs], [out])
        self.ops[-1]['pemode'] = (out.name, self._pemode(lhsT, False))
        n = self._nfree(rhs)
        base = max(0.055, n / 2200.0)
        dt_ = lhsT.dtype
        self.ops[-1]['dur'] = base * (4.0 if dt_ == mybir.dt.float32 else (2.0 if dt_ == mybir.dt.float32r else 1.0))

    def transpose(self, out, in_, ident):
        self.op('pe', lambda e: e.transpose(out, in_, ident), [in_, ident], [out])
        self.ops[-1]['pemode'] = (out.name, self._pemode(in_, True))
        self.ops[-1]['dur'] = 0.12

    def act(self, out, in_, func, bias=0.0, scale=1.0, eng='act'):
        self.op(eng, lambda e: e.activation(out, in_, func, bias=bias, scale=scale),
                [in_, bias, scale], [out])

    def tt(self, out, in0, in1, op, eng='dve'):
        self.op(eng, lambda e: e.tensor_tensor(out, in0, in1, op), [in0, in1], [out])

    def ts(self, out, in0, s1, op0, s2=None, op1=None, eng='dve'):
        if op1 is None:
            self.op(eng, lambda e: e.tensor_scalar(out, in0, s1, None, op0), [in0, s1], [out])
        else:
            self.op(eng, lambda e: e.tensor_scalar(out, in0, s1, s2, op0, op1), [in0, s1, s2], [out])

    def stt(self, out, in0, scalar, in1, op0, op1, eng='dve'):
        self.op(eng, lambda e: e.scalar_tensor_tensor(out, in0, scalar, in1, op0, op1),
                [in0, scalar, in1], [out])

    def copy(self, out, in_, eng='dve'):
        if eng == 'act':
            self.op(eng, lambda e: e.copy(out, in_), [in_], [out])
        else:
            self.op(eng, lambda e: e.tensor_copy(out, in_), [in_], [out])

    def memset(self, out, val, eng='dve'):
        self.op(eng, lambda e: e.memset(out, val), [], [out])

    def recip(self, out, in_):
        self.op('dve', lambda e: e.reciprocal(out, in_), [in_], [out])
        self.ops[-1]['dur'] = 0.1 + self._nfree(out) / 155.0

    def scan(self, out, d0, d1, init, op0, op1):
        self.op('dve', lambda e: e.tensor_tensor_scan(out, d0, d1, init, op0, op1), [d0, d1, init], [out])

    def dma(self, out, in_, q='sp', slow=False, rkeys=(), wkeys=()):
        if slow:
            self.op(q, lambda e: e.dma_start(out=out, in_=in_, allow_slow_non_contiguous=True), [in_], [out], dma=True,
                    rkeys=rkeys, wkeys=wkeys)
        else:
            self.op(q, lambda e: e.dma_start(out=out, in_=in_), [in_], [out], dma=True, rkeys=rkeys, wkeys=wkeys)

    def reschedule(self, window=256, lat=0.25, dma_lat=3.0, slack=0.0):
        ops = self.ops
        n = len(ops)
        last_w, readers = {}, {}
        preds = [None] * n
        for i, o in enumerate(ops):
            d = set()
            for b in o['rb']:
                w = last_w.get(b)
                if w is not None:
                    d.add(w)
            for b in o['wb']:
                w = last_w.get(b)
                if w is not None:
                    d.add(w)
                d.update(readers.get(b, ()))
            d.discard(i)
            preds[i] = d
            for b in o['rb']:
                readers.setdefault(b, []).append(i)
            for b in o['wb']:
                last_w[b] = i
                readers[b] = []
        queues = {}
        for i, o in enumerate(ops):
            queues.setdefault(o['eng'], []).append(i)
        dmas = [i for i, o in enumerate(ops) if o['dma']]
        for k in range(NDMA, len(dmas)):
            preds[dmas[k]].add(dmas[k - NDMA])
        done = [None] * n
        start = [None] * n
        free = {e: 0.0 for e in queues}
        head = {e: 0 for e in queues}
        issued = [False] * n
        remaining = n
        INF = 1e30

        def ready_time(i):
            t = 0.0
            for p_ in preds[i]:
                if not issued[p_]:
                    return INF
                dp = done[p_] + (0.0 if ops[p_]['eng'] == ops[i]['eng'] and not ops[p_]['dma'] else lat)
                if dp > t:
                    t = dp
            return t
        while remaining:
            best = None
            for e, q in queues.items():
                h = head[e]
                while h < len(q) and issued[q[h]]:
                    h += 1
                head[e] = h
                if h >= len(q):
                    continue
                hi = q[h]
                hr = ready_time(hi)
                hstart = max(free[e], hr) if hr < INF else INF
                cand = (hstart, hi)
                if hstart > free[e] + 1e-9:
                    cnt = 0
                    j = h + 1
                    while j < len(q) and cnt < window:
                        oj = q[j]
                        j += 1
                        if issued[oj]:
                            continue
                        cnt += 1
                        if ops[oj]['dma']:
                            continue
                        r = ready_time(oj)
                        if r >= INF:
                            continue
                        s = max(free[e], r)
                        if s + ops[oj]['dur'] <= hstart + slack + 1e-9 and s < cand[0]:
                            cand = (s, oj)
                if cand[0] < INF and (best is None or cand < best[0:2]):
                    best = (cand[0], cand[1], e)
            assert best is not None, "scheduler deadlock"
            s, i, e = best
            o = ops[i]
            start[i] = s
            issued[i] = True
            remaining -= 1
            free[e] = s + o['dur']
            done[i] = s + o['dur'] + (dma_lat if o['dma'] else 0.0)
        order = sorted(range(n), key=lambda i: (start[i], i))
        self.ops = [ops[i] for i in order]
        self.sim_time = max(done)

    def build(self, resched=True):
        nc = self.nc
        if resched:
            self.reschedule()
        ENG = ['pe', 'dve', 'act', 'pool', 'sp']
        engobj = {'pe': nc.tensor, 'dve': nc.vector, 'act': nc.scalar, 'pool': nc.gpsimd, 'sp': nc.sync}
        last_w = {}
        readers = {}
        bank_mode = {}
        known = {e: {} for e in ENG}
        eidx = {e: 0 for e in ENG}
        slot_last = [None] * NDMA
        slot_cnt = [0] * NDMA
        ndma = 0
        ops = self.ops
        for i, o in enumerate(ops):
            e = o['eng']
            deps = set()
            for b in o['rb']:
                w = last_w.get(b)
                if w is not None:
                    deps.add(w)
            for b in o['wb']:
                w = last_w.get(b)
                if w is not None:
                    deps.add(w)
                for r in readers.get(b, ()):
                    deps.add(r)
            forced = set()
            if 'pemode' in o:
                bank, mode = o['pemode']
                lm = bank_mode.get(bank)
                if lm is not None and lm[0] != mode:
                    deps.add(lm[1])
                    forced.add(lm[1])
                bank_mode[bank] = (mode, i)
            if o['dma']:
                k = ndma % NDMA
                ndma += 1
                o['slot'] = k
                if slot_last[k] is not None:
                    deps.add(slot_last[k])
                slot_cnt[k] += 1
                o['dval'] = 16 * slot_cnt[k]
                slot_last[k] = i
                o['key'] = ('dma', k)
                o['kidx'] = slot_cnt[k]
            else:
                eidx[e] += 1
                o['key'] = ('eng', e)
                o['kidx'] = eidx[e]
            waits = []
            vc = {}
            kn = known[e]
            for d in sorted(deps, reverse=True):
                if d == i:
                    continue
                p = ops[d]
                if p['key'] == ('eng', 'pe') and e == 'pe' and not o['dma'] and d not in forced:
                    continue
                if kn.get(p['key'], 0) >= p['kidx']:
                    continue
                waits.append(d)
                p['signal'] = True
                for k2, v2 in p['vc'].items():
                    if kn.get(k2, 0) < v2:
                        kn[k2] = v2
            o['waits'] = waits
            vc = dict(kn)
            vc[o['key']] = o['kidx']
            o['vc'] = vc
            o.setdefault('signal', False)
            if o['dma']:
                o['signal'] = True
            for b in o['rb']:
                readers.setdefault(b, []).append(i)
            for b in o['wb']:
                last_w[b] = i
                readers[b] = []
        self.final_dma = [(k, 16 * slot_cnt[k]) for k in range(NDMA) if slot_cnt[k]]
        sems = {e: self.es.enter_context(nc.semaphore("s_" + e)) for e in ENG}
        dsems = [self.es.enter_context(nc.semaphore("d_%d" % k)) for k in range(NDMA)]
        sig = {e: 0 for e in ENG}
        for o in ops:
            if o['dma']:
                o['sem'] = dsems[o['slot']]
                o['sval'] = o['dval']
            elif o['signal']:
                sig[o['eng']] += 1
                o['sem'] = sems[o['eng']]
                o['sval'] = sig[o['eng']]
        self.nwaits = sum(len(o['waits']) for o in ops)
        self.nsig = dict(sig)
        per = {e: [o for o in ops if o['eng'] == e] for e in ENG}
        block = self.es.enter_context(nc.Block())

        def emit(eng_name):
            def f(eo):
                for o in per[eng_name]:
                    for d in o['waits']:
                        p = ops[d]
                        eo.wait_ge(p['sem'], p['sval'])
                    ins = o['fn'](eo)
                    if o['dma']:
                        ins.then_inc(o['sem'], 16)
                    elif o['signal']:
                        ins.then_inc(o['sem'], 1)
                if eng_name == 'sp':
                    for k, v in self.final_dma:
                        eo.wait_ge(dsems[k], v)
            return f
        block.tensor(emit('pe'))
        block.vector(emit('dve'))
        block.scalar(emit('act'))
        block.gpsimd(emit('pool'))
        block.sync(emit('sp'))
        self.es.close()
        return nc


D = 1024
C0 = float(np.exp(-0.5))
NPC = 384
F32R = mybir.dt.float32r
RDT = BF16
O_ID, O_ONES2, O_MG, O_MNT, O_ID2, O_D0, O_BC, O_BP, O_SK, NCST = 0, 128, 256, 384, 512, 640, 1152, 2176, 3200, 3208


def host_consts():
    c = np.zeros((128, NCST), np.float32)
    c[:, O_ID:O_ID + 128] = np.eye(128)
    c[0:64, O_ONES2:O_ONES2 + 64] = 1.0
    c[64:128, O_ONES2 + 64:O_ONES2 + 128] = 1.0
    j = np.arange(64)[:, None]
    i = np.arange(64)[None, :]
    lt = (j < i).astype(np.float32)
    le = (j <= i).astype(np.float32)
    mg = np.zeros((128, 128), np.float32)
    mg[0:64, 0:64] = lt
    mg[0:64, 64:128] = le
    mg[64:128, 0:64] = lt
    mg[64:128, 64:128] = le
    c[:, O_MG:O_MG + 128] = mg
    c[0:64, O_MNT:O_MNT + 64] = (i.T > j.T).astype(np.float32).T * 0 + (np.arange(64)[:, None] > np.arange(64)[None, :])
    c[0:64, O_MNT + 64:O_MNT + 128] = c[0:64, O_MNT:O_MNT + 64]
    c[0:64, O_ID2:O_ID2 + 64] = np.eye(64)
    c[0:64, O_ID2 + 64:O_ID2 + 128] = np.eye(64)
    d0 = np.ones((512,), np.float32)
    d0[0::64] = 0.0
    c[:, O_D0:O_D0 + 512] = d0[None, :]
    key = np.arange(128)[:, None]
    q = np.arange(128)[None, :]
    for h in range(8):
        slope = 2.0 ** (-(h + 1))
        cur = np.where(q >= key, -slope * (q - key), -30000.0)
        prv = np.where(key > q, -slope * (q + 128 - key), -30000.0)
        c[:, O_BC + h * 128:O_BC + (h + 1) * 128] = cur
        c[:, O_BP + h * 128:O_BP + (h + 1) * 128] = prv
    return c


def host_pcol(inp):
    pc = np.zeros((128, NPC), np.float32)

    def put(col, vec):
        v = np.asarray(vec, np.float32).reshape(-1, 128)
        pc[:, col:col + v.shape[0]] = v.T
    put(0, inp['norm_mix_g'])
    put(16, inp['norm_ffn_g'])
    put(32, inp['final_norm_g'])
    put(40, inp['hy_mu'][0])
    put(54, inp['hy_w0'][0])
    put(58, inp['hy_a0'][0])
    put(62, inp['hy_k_k'][0])
    put(66, inp['hy_k_a'][0])
    put(70, inp['hy_r_k'][0])
    put(74, inp['hy_gn_g'][0])
    put(78, inp['hy_gn_b'][0])
    put(82, inp['cv_pw1_b'][0])
    put(98, inp['cv_dw_b'][0])
    put(106, inp['cv_ln_g'][0])
    put(114, inp['cv_ln_b'][0])
    put(122, inp['cv_pw2_b'][0])
    dw = np.asarray(inp['cv_dw_w'][0], np.float32)
    pc[:, 130:130 + 248] = dw.T.reshape(8, 128, 31).transpose(1, 0, 2).reshape(128, 248)
    return pc


def build(T=2048, dbg=(), stages=('att', 'rwkv', 'wout', 'mlp0', 'conv', 'mlp1'), nchunks=8, nhp=4, cut=99):
    p = Prog()
    NT = T // 512
    x = p.dram("x", [T, D], "ExternalInput")
    out = p.dram("out", [T, D], "ExternalOutput")
    w_in = p.dram("w_in", [D, 2560], "ExternalInput")
    w_out = p.dram("w_out", [D, D], "ExternalInput")
    pw1 = p.dram("pw1", [D, 2048], "ExternalInput")
    pw2 = p.dram("pw2", [D, D], "ExternalInput")
    w1 = p.dram("w1", [2, D, 4096], "ExternalInput")
    w2 = p.dram("w2", [2, 4096, D], "ExternalInput")
    lupd = p.dram("lup", [128, 512], "ExternalInput")
    gupd = p.dram("gup", [128, 512], "ExternalInput")
    cstd = p.dram("cst", [128, NCST], "ExternalInput")
    pcold = p.dram("pcol", [128, NPC], "ExternalInput")
    sinkd = p.dram("sinkb", [128, 8], "ExternalInput")
    dbg_outs = {}
    def wscr(name, K, M, lead=None):
        nt = (K // 256) * (M // 512)
        return p.dram(name, ([lead] if lead else []) + [nt, 128, 1024], "Internal", dtype=BF16)
    w_in_b = wscr("w_in_b", D, 2560)
    w_out_b = wscr("w_out_b", D, D)
    pw1_b = wscr("pw1_b", D, 2048)
    pw2_b = wscr("pw2_b", D, D)
    w1_b = wscr("w1_b", D, 4096, 2)
    w2_b = wscr("w2_b", 4096, D, 2)

    def dbg_out(name, ap):
        if name in dbg:
            shp = list(ap.shape)
            dt_ = p.dram("dbg_" + name, shp, "ExternalOutput", dtype=ap.dtype)
            p.dma(dt_, ap)

    cst = p.sb("cstsb", [128, NCST])
    pcol = p.sb("pcolsb", [128, NPC])
    ones = p.sb("ones", [128, 128])
    onesr = p.sb("onesr", [128, 128])
    sqr = [p.sb("sqr%d" % i, [128, 512]) for i in range(2)]
    esink = p.sb("esink", [128, 8])
    xT = p.sb("xT", [128, 8, T])
    wst = [p.sb("wst%d" % i, [128, 2, 512]) for i in range(2)]
    NWB = 4
    wbf = [p.sb("wbf%d" % i, [128, 2, 512], dtype=BF16) for i in range(NWB)]
    LUP = p.sb("LUP", [128, 512], dtype=BF16)
    GUP = p.sb("GUP", [128, 512], dtype=BF16)
    ones2b = p.sb("ones2b", [128, 128], dtype=BF16)
    Hbd = p.sb("Hbd", [128, 4, 128])
    UV = p.sb("UV", [128, 2, 128])
    ZV = p.sb("ZV", [128, 2, 64])
    Xn = [p.sb("Xn%d" % i, [64, 4, 128], dtype=RDT) for i in range(2)]
    Xtn = [p.sb("Xtn%d" % i, [64, 4, 128], dtype=RDT) for i in range(2)]
    Psh = [p.sb("Psh%d" % i, [64, 4, 128], dtype=RDT) for i in range(2)]
    Pa = p.sb("Pa", [64, 4, 128])
    Pb = [p.sb("Pb%d" % i, [64, 4, 128]) for i in range(2)]
    Zs = p.sb("Zs", [64, 128])
    BKhat = p.sb("BKhat", [128, 128])
    M2b = [p.sb("M2b%d" % i, [128, 4, 2, 128]) for i in range(2)]
    WC2 = [p.sb("WC2_%d" % i, [128, 4]) for i in range(2)]
    ST2 = [p.sb("ST2_%d" % i, [128, 4]) for i in range(2)]
    carry = p.sb("carry", [128, 16])
    kT = p.sb("kT", [128, 640], dtype=BF16)
    vTM = p.sb("vTM", [128, 5, 128], dtype=BF16)
    onesb = p.sb("onesb", [128, 128], dtype=BF16)
    st1 = p.sb("st1", [128, 512])
    NSCR = 19520
    scr = p.sb("scr", [128, NSCR])

    def bfv(a, b):
        return scr[:, a:b].bitcast(BF16)
    hT = bfv(17472, 17472 + 2048).rearrange("p (c t) -> p c t", t=512)
    hTf = scr[:, 8192:8192 + 4096].rearrange("p (c t) -> p c t", t=512)
    chalo = p.sb("chalo", [128, 8, 30], dtype=BF16)
    PS = [p.ps("ps%d" % i, [128, 512]) for i in range(8)]

    ident = cst[:, O_ID:O_ID + 128]
    ones2 = cst[:, O_ONES2:O_ONES2 + 128]
    maskG = cst[:, O_MG:O_MG + 128]
    maskNt2 = cst[0:64, O_MNT:O_MNT + 128]
    ident2 = cst[0:64, O_ID2:O_ID2 + 128]
    d0 = cst[:, O_D0:O_D0 + 512]
    Bcur = cst[:, O_BC:O_BC + 1024].rearrange("p (h q) -> p h q", q=128)
    Bprev = cst[:, O_BP:O_BP + 1024].rearrange("p (h q) -> p h q", q=128)

    def col(i):
        return pcol[:, i:i + 1]

    p.dma(cst, cstd)
    p.dma(pcol, pcold)
    p.dma(esink, sinkd)
    p.memset(ones, 1.0)
    p.copy(onesb, ones)
    p.copy(onesr.bitcast(F32R), ones)
    p.memset(Hbd, 0.0, eng='pool')
    p.memset(UV, 0.0, eng='pool')
    p.memset(ZV, 0.0, eng='pool')
    p.memset(carry, 0.0, eng='pool')
    p.act(esink, esink, AF.Exp)
    p.ts(pcol[:, 378:382], pcol[:, 66:70], -1.0, ALU.mult, 1.0, ALU.add)

    p.dma(scr[:, 8192:8704], lupd)
    p.dma(scr[:, 8704:9216], gupd)
    p.copy(LUP, scr[:, 8192:8704])
    p.copy(GUP, scr[:, 8704:9216], eng='act')
    p.copy(ones2b, ones2)
    xin_region = scr[:, 0:4 * D].rearrange("p (b d) -> p b d", d=D)
    for tt in range(NT):
        p.dma(xin_region, x[tt * 512:(tt + 1) * 512, :].rearrange("(b p) d -> p b d", p=128),
              q='sp')
        for c in range(8):
            ps_ = PS[4 + (c % 4)]
            for b in range(4):
                p.transpose(ps_[:, b * 128:(b + 1) * 128], xin_region[:, b, c * 128:(c + 1) * 128], ident)
            p.copy(xT[:, c, tt * 512:(tt + 1) * 512], ps_, eng='dve' if c % 2 == 0 else 'act')

    lin_ctr = [0, 0]
    pend_store = []

    def linear(W, Wb, K, M, rhs_fn, evac_fn, first, hook=None):
        nkg = K // 256
        nmg = (M + 511) // 512
        for mg in range(nmg):
            mw = min(512, M - mg * 512)
            nm = mw // 128
            banks = PS[0:4] if lin_ctr[0] % 2 == 0 else PS[4:8]
            lin_ctr[0] += 1
            for kg in range(nkg):
                wt = wbf[lin_ctr[1] % NWB]
                q = 'sp'
                key = (Wb.name, int(Wb.offset), kg, mg)
                assert mw == 512
                wsrc = Wb[mg * nkg + kg].rearrange("p (kc m) -> p kc m", kc=2)
                if first:
                    ws = wst[lin_ctr[1] % 2]
                    ce = 'dve' if lin_ctr[1] % 2 == 0 else 'act'
                    p.dma(ws[:, :, 0:mw],
                          W[kg * 256:(kg + 1) * 256, mg * 512:mg * 512 + mw].rearrange("(kc p) m -> p kc m", p=128), q=q)
                    p.copy(wt[:, :, 0:mw], ws[:, :, 0:mw], eng=ce)
                    if pend_store:
                        pend_store.pop()()
                    pend_store.append(lambda wsrc=wsrc, wt=wt, mw=mw, key=key: p.dma(wsrc, wt[:, :, 0:mw], q='sp', wkeys=[key]))
                else:
                    p.dma(wt[:, :, 0:mw], wsrc, q=q, rkeys=[key])
                lin_ctr[1] += 1
                for kc in range(2):
                    for m in range(nm):
                        p.mm(banks[m], wt[:, kc, m * 128:(m + 1) * 128], rhs_fn(kg * 2 + kc),
                             start=(kg == 0 and kc == 0), stop=(kg == nkg - 1 and kc == 1))
            for m in range(nm):
                evac_fn(mg * 4 + m, banks[m])
            if hook is not None and mg == 0:
                hook()
        if pend_store:
            pend_store.pop()()

    def rmsnorm_tile(tt, gcol0, dst=None):
        dst = hT if dst is None else dst
        ts_ = slice(tt * 512, (tt + 1) * 512)
        for c in range(8):
            sqb = sqr[c % 2]
            p.act(sqb.bitcast(F32R), xT[:, c, ts_], AF.Square)
            p.mm(PS[7], onesr.bitcast(F32R), sqb.bitcast(F32R), start=(c == 0), stop=(c == 7))
        p.act(st1, PS[7], AF.Sqrt, bias=1e-6, scale=1.0 / D)
        p.recip(st1, st1)
        for c in range(8):
            p.stt(dst[:, c, :], xT[:, c, ts_], col(gcol0 + c), st1, ALU.mult, ALU.mult)

    prenormed = set()

    def mlp_tile(tt, layer, mid=None):
        ts_ = slice(tt * 512, (tt + 1) * 512)
        rmsnorm_tile(tt, 16 + layer * 8)
        h1 = bfv(0, 8192).rearrange("p (c t) -> p c t", t=512)

        def ev1(mc, ps_):
            p.act(h1[:, mc, :], ps_, AF.Relu)
            p.tt(h1[:, mc, :], h1[:, mc, :], h1[:, mc, :], ALU.mult, eng='pool')
        linear(w1[layer], w1_b[layer], D, 4096, lambda kc: hT[:, kc, :], ev1, tt == 0)

        def ev2(mc, ps_):
            p.tt(xT[:, mc, ts_], ps_, xT[:, mc, ts_], ALU.add)
        linear(w2[layer], w2_b[layer], 4096, D, lambda kc: h1[:, kc, :], ev2, tt == 0, hook=mid)

    o = 0
    pl = scr[:, o:o + 14 * 512].rearrange("p (c t) -> p c t", t=512); o += 14 * 512
    ycat = bfv(o, o + 2048).rearrange("p (c t) -> p c t", t=512); o += 2048
    prawb = [scr[:, o:o + 513], scr[:, o + 544:o + 544 + 513]]; o += 1088
    TL = bfv(o, o + 256); o += 512
    SG = bfv(o, o + 256); o += 512
    SLOT0 = o
    assert SLOT0 + 12 * 512 == 17472, SLOT0
    slot = [scr[:, o + i * 512:o + (i + 1) * 512] for i in range(16)]
    qT = bfv(o, o + 1024).rearrange("p (c t) -> p c t", t=512)
    vattT = slot[4]
    sTa, sTb, denb = slot[5], slot[6], slot[9]
    eTa = bfv(SLOT0 + 7 * 512, SLOT0 + 7 * 512 + 256)
    eTb = bfv(SLOT0 + 8 * 512, SLOT0 + 8 * 512 + 256)
    o += 16 * 512
    assert o <= NSCR

    def hybrid_tile(tt):
        ts_ = slice(tt * 512, (tt + 1) * 512)
        if ('hyb', tt) not in prenormed:
            rmsnorm_tile(tt, 0)

        def ev_in(mc, ps_):
            if mc < 14:
                pb = prawb[mc % 2]
                p.copy(pb[:, 0:1], carry[:, mc:mc + 1], eng='pool')
                p.copy(pb[:, 1:513], ps_, eng='act')
                p.copy(carry[:, mc:mc + 1], pb[:, 512:513], eng='pool')
                p.tt(pl[:, mc, :], pb[:, 0:512], pb[:, 1:513], ALU.subtract)
                p.stt(pl[:, mc, :], pl[:, mc, :], col(40 + mc), pb[:, 1:513], ALU.mult, ALU.add)
            elif mc < 18:
                p.copy(qT[:, mc - 14, :], ps_, eng='act')
            elif mc == 18:
                p.copy(kT[:, 128:640], ps_, eng='act')
            else:
                p.copy(vattT, ps_, eng='act')
        linear(w_in, w_in_b, D, 2560, lambda kc: hT[:, kc, :], ev_in, tt == 0)

        if 'att' not in stages:
            return
        for b in range(4):
            p.transpose(PS[6][:, b * 128:(b + 1) * 128], vattT[:, b * 128:(b + 1) * 128], ident)
        p.copy(vTM[:, 1:5, :], PS[6].rearrange("p (b d) -> p b d", d=128))
        for blk in range(4):
            first = (tt == 0 and blk == 0)
            for g in range(2):
                rows = slice(g * 64, (g + 1) * 64)
                pc_, pp_ = (PS[1], PS[0]) if g == 0 else (PS[3], PS[2])
                qv = qT[rows, :, blk * 128:(blk + 1) * 128]
                p.mm(pc_, kT[rows, 128 + blk * 128:256 + blk * 128], qv)
                if not first:
                    p.mm(pp_, kT[rows, blk * 128:128 + blk * 128], qv)
                p.stt(sTa.rearrange("p (h q) -> p h q", q=128), pc_.rearrange("p (h q) -> p h q", q=128),
                      0.125, Bcur[:, 4 * g:4 * g + 4, :], ALU.mult, ALU.add)
                p.act(eTa, sTa, AF.Exp)
                if not first:
                    p.stt(sTb.rearrange("p (h q) -> p h q", q=128), pp_.rearrange("p (h q) -> p h q", q=128),
                          0.125, Bprev[:, 4 * g:4 * g + 4, :], ALU.mult, ALU.add)
                    p.act(eTb, sTb, AF.Exp)
                p.mm(PS[4], vTM[:, blk + 1, :], eTa, start=True, stop=first)
                if not first:
                    p.mm(PS[4], vTM[:, blk, :], eTb, start=False, stop=True)
                p.mm(PS[5], onesb, eTa, start=True, stop=first)
                if not first:
                    p.mm(PS[5], onesb, eTb, start=False, stop=True)
                den3 = denb.rearrange("p (h q) -> p h q", q=128)
                p.tt(den3, PS[5].rearrange("p (h q) -> p h q", q=128),
                     esink[:, 4 * g:4 * g + 4].unsqueeze(2).to_broadcast([128, 4, 128]), ALU.add)
                p.recip(denb, denb)
                p.tt(ycat[rows, 4:8, blk * 128:(blk + 1) * 128],
                     PS[4][rows, :].rearrange("p (h q) -> p h q", q=128), den3[rows], ALU.mult)
        p.copy(kT[:, 0:128], kT[:, 512:640], eng='pool')
        p.copy(vTM[:, 0, :], vTM[:, 4, :], eng='pool')

        if 'rwkv' not in stages:
            dbg_out("ycat%d" % tt, ycat)
            return
        p.copy(TL[64:128, :], pl[64:128, 12, :], eng='pool')
        p.act(TL[0:64, :], pl[0:64, 12, :], AF.Tanh)
        p.act(SG, pl[:, 13, :], AF.Sigmoid)

        def sl(i, n=512):
            b8 = SLOT0 + i * 512
            return scr[:, b8:b8 + n]

        def bigv(i):
            a_ = sl(i)
            return (a_.rearrange("p (c two t) -> p c two t", two=2, t=64), a_.rearrange("p (c n) -> p c n", n=128))
        BIG = [[bigv(s_ * 4 + k_) for k_ in range(4)] for s_ in range(2)]
        PT = [scr[:, SLOT0 + 8 * 512 + i * 256:SLOT0 + 8 * 512 + (i + 1) * 256] for i in range(8)]
        YR = [sl(12), sl(13)]
        POSTT = [sl(14), sl(15), prawb[0][:, 0:512], prawb[1][:, 0:512]]

        def v3(a):
            return a.rearrange("p (c t) -> p c t", t=64)

        def prep_closures(u):
            hp, half = u // 2, u % 2
            par = u % 2
            cs = slice(hp * 128, (hp + 1) * 128)
            tok = slice(half * 256, (half + 1) * 256)
            rl, kl, vl = pl[:, hp, tok], pl[:, 4 + hp, tok], pl[:, 8 + hp, tok]
            sS, sEe, sEi, sEni, sA, sKK, sT, sB = PT
            sTb = sT.bitcast(BF16)[:, 0:256]
            (AR4, AR3), (BK4, BK3), (BKh4, BKh3), (XV4, XV3) = BIG[par]
            WCu, STu = WC2[par], ST2[par]
            d0h = d0[:, 0:256]

            def c1():
                p.mm(PS[1][:, 0:256], LUP[0:64, cs], TL[0:64, tok])
                p.act(sEe, PS[1][:, 0:256], AF.Sigmoid, bias=col(54 + hp))
                p.scan(sS, d0h, sEe, 0.0, ALU.mult, ALU.add)
                p.tt(sEe, sS, sEe, ALU.subtract)
                p.act(sEe, sEe, AF.Exp, scale=-C0)

            def c2():
                p.act(sEi, sS, AF.Exp, scale=-C0)
                p.act(sEni, sS, AF.Exp, scale=C0)
                p.copy(STu, v3(sS)[:, :, 63], eng='pool')
                p.act(WCu, STu, AF.Exp, scale=-C0)
                p.tt(v3(sS), STu.unsqueeze(2).to_broadcast([128, 4, 64]), v3(sS), ALU.subtract, eng='pool')
                p.act(sS, sS, AF.Exp, scale=-C0)

            def c3():
                p.ts(sKK, kl, col(62 + hp), ALU.mult)
                p.act(sTb, sKK, AF.Square)
                p.mm(PS[3][:, 0:256], LUP[64:128, cs], TL[64:128, tok])
                p.act(sA, PS[3][:, 0:256], AF.Sigmoid, bias=col(58 + hp))

            def c3b():
                p.mm(PS[5][:, 0:256], ones2b, sTb)
                p.act(sT, PS[5][:, 0:256], AF.Sqrt)

            def c4():
                p.ts(sT, sT, 1e-12, ALU.max)
                p.recip(sT, sT)
                p.tt(sKK, sKK, sT, ALU.mult)
                p.ts(sT, sA, col(66 + hp), ALU.mult, col(378 + hp), ALU.add)
                p.tt(kl, kl, sT, ALU.mult, eng='pool')
                p.tt(sB, sKK, sA, ALU.mult, eng='pool')

            def c5():
                p.stt(AR4[:, :, 0, :], v3(sKK), -1.0, v3(sEe), ALU.mult, ALU.mult)
                p.tt(AR4[:, :, 1, :], v3(rl), v3(sEi), ALU.mult, eng='pool')
                p.tt(BK4[:, :, 0, :], v3(sB), v3(sEni), ALU.mult)
                p.tt(BK4[:, :, 1, :], v3(kl), v3(sEni), ALU.mult, eng='pool')

            def c6():
                p.tt(BKh4[:, :, 0, :], v3(sB), v3(sS), ALU.mult)
                p.tt(BKh4[:, :, 1, :], v3(kl), v3(sS), ALU.mult, eng='pool')
                p.memset(XV4[:, :, 0, :], 0.0, eng='pool')
                p.copy(XV4[:, :, 1, :], v3(vl), eng='pool')
            return [c1, c2, c3, c3b, c4, c5, c6]

        def pre_closures(u):
            par = u % 2
            (AR4, AR3), (BK4, BK3), _, _ = BIG[par]
            M2h = M2b[par]
            Pfin = Pb[par]
            st = []

            def stageA():
                for ci in range(4):
                    for h in range(2):
                        rows = slice(h * 64, (h + 1) * 64)
                        bank = PS[h * 2 + ci // 2]
                        o_ = (ci % 2) * 256
                        p.mm(bank[:, o_:o_ + 128], BK3[rows, ci, :], AR3[rows, ci, :])
                        p.mm(bank[:, o_ + 128:o_ + 192], AR3[rows, ci, :], BK4[rows, ci, 0, :])
                for h in range(2):
                    for pr in range(2):
                        bank = PS[h * 2 + pr].rearrange("p (c n) -> p c n", n=256)
                        p.tt(M2h[:, pr * 2:pr * 2 + 2, h, :], bank[:, :, 0:128],
                             maskG.unsqueeze(1).to_broadcast([128, 2, 128]), ALU.mult)
                        p.tt(Xtn[0][:, pr * 2:pr * 2 + 2, h * 64:(h + 1) * 64], bank[0:64, :, 128:192],
                             maskNt2[:, 0:64].unsqueeze(1).to_broadcast([64, 2, 64]), ALU.mult)
                p.tt(Pa.rearrange("p c (h t) -> p (c h) t", t=64),
                     M2h[0:64].rearrange("p c h n -> p (c h) n")[:, :, 0:64],
                     ident2[:, 0:64].unsqueeze(1).to_broadcast([64, 8, 64]), ALU.add, eng='pool')
                p.copy(Xn[0].rearrange("p c (h t) -> p (c h) t", t=64),
                       M2h[0:64].rearrange("p c h n -> p (c h) n")[:, :, 0:64], eng='pool')
                p.copy(Psh[0].rearrange("p c n -> p (c n)"), Pa.rearrange("p c n -> p (c n)"), eng='act')
            st.append(stageA)

            def mk_round(k, sb):
                cis = (0, 1) if sb == 0 else (2, 3)
                bX, bXt = (PS[0], PS[1]) if sb == 0 else (PS[2], PS[3])
                c0 = cis[0]

                def sq():
                    nx = (k + 1) % 2
                    for ci in cis:
                        for h in range(2):
                            hs = slice(h * 64, (h + 1) * 64)
                            xc = Xn[k % 2][:, ci, hs]
                            xtc = Xtn[k % 2][:, ci, hs]
                            uo = ((ci - c0) * 2 + h) * 64
                            if k < 4:
                                p.mm(bX[0:64, uo:uo + 64], xtc, xc)
                            p.mm(bXt[0:64, uo:uo + 64], xc, xtc)
                    if k < 4:
                        p.copy(Xn[nx][:, c0:c0 + 2, :].rearrange("p c n -> p (c n)"), bX[0:64, 0:256], eng='act')
                    p.copy(Xtn[nx][:, c0:c0 + 2, :].rearrange("p c n -> p (c n)"), bXt[0:64, 0:256], eng='dve')

                def pu():
                    nx = (k + 1) % 2
                    Pcur = Pa if k % 2 == 0 else Pfin
                    Pnxt = Pfin if k % 2 == 0 else Pa
                    for ci in cis:
                        for h in range(2):
                            hs = slice(h * 64, (h + 1) * 64)
                            uo = 256 + ((ci - c0) * 2 + h) * 64
                            p.mm(bX[0:64, uo:uo + 64], Xtn[nx][:, ci, hs], Psh[k % 2][:, ci, hs])
                    p.tt(Pnxt[:, c0:c0 + 2, :].rearrange("p c n -> p (c n)"), bX[0:64, 256:512],
                         Pcur[:, c0:c0 + 2, :].rearrange("p c n -> p (c n)"), ALU.add)
                    if k < 4:
                        p.copy(Psh[nx][:, c0:c0 + 2, :].rearrange("p c n -> p (c n)"),
                               Pnxt[:, c0:c0 + 2, :].rearrange("p c n -> p (c n)"), eng='act')
                return sq, pu
            for k in range(5):
                sq0, pu0 = mk_round(k, 0)
                sq1, pu1 = mk_round(k, 1)
                st += [sq0, sq1, pu0, pu1]
            return st

        def serial_closures(u):
            hp, half = u // 2, u % 2
            par = u % 2
            (AR4, AR3), _, (BKh4, BKh3), (XV4, XV3) = BIG[par]
            yraw = YR[hp % 2]
            out_ = []
            for ci in range(4):
                def mk(ci=ci):
                    ARc = AR4[:, ci]
                    M2 = M2b[par][:, ci]
                    Tt = Pb[par][:, ci, :]
                    c = half * 4 + ci

                    def s1():
                        p.transpose(PS[6][:, 0:128], BKh3[:, ci, :], ident)
                        p.transpose(PS[6][:, 128:256], XV3[:, ci, :], ident)
                        p.copy(BKhat, PS[6][:, 0:128], eng='act')
                        for h in range(2):
                            src = PS[6][64:128, 128 + h * 64:128 + (h + 1) * 64]
                            p.copy(UV[64:128, h, h * 64:(h + 1) * 64], src, eng='act')
                            p.copy(ZV[64:128, h, :], src, eng='act')

                    def s2():
                        for h in range(2):
                            zo = PS[4][0:64, h * 64:(h + 1) * 64]
                            p.mm(zo, ARc[:, 0, :], Hbd[:, hp, h * 64:(h + 1) * 64], start=True, stop=False)
                            p.mm(zo, M2[:, h, 0:64], ZV[:, h, :], start=False, stop=True)
                        p.copy(Zs, PS[4][0:64, 0:128], eng='dve')
                        for h in range(2):
                            p.mm(PS[4][0:64, 256 + h * 64:256 + (h + 1) * 64], Tt[:, h * 64:(h + 1) * 64], Zs[:, h * 64:(h + 1) * 64])
                        for h in range(2):
                            p.copy(UV[0:64, h, h * 64:(h + 1) * 64], PS[4][0:64, 256 + h * 64:256 + (h + 1) * 64], eng='dve')

                    def s3():
                        p.mm(PS[7][:, 0:64], Hbd[:, hp, :], ARc[:, 1, :], start=True, stop=False)
                        p.mm(PS[7][:, 0:64], UV[:, 0, :], M2[:, 0, 64:128], start=False, stop=False)
                        p.mm(PS[7][:, 0:64], UV[:, 1, :], M2[:, 1, 64:128], start=False, stop=True)
                        p.mm(PS[7][:, 128:256], BKhat, UV[:, 0, :], start=True, stop=False)
                        p.mm(PS[7][:, 128:256], BKhat, UV[:, 1, :], start=False, stop=True)
                        p.copy(yraw[:, c * 64:(c + 1) * 64], PS[7][:, 0:64], eng='dve')
                        for h in range(2):
                            rows = slice(h * 64, (h + 1) * 64)
                            hb = Hbd[rows, hp, h * 64:(h + 1) * 64]
                            p.stt(hb, hb, WC2[par][rows, ci:ci + 1], PS[7][rows, 128 + h * 64:128 + (h + 1) * 64], ALU.mult, ALU.add)
                    return [s1, s2, s3]
                out_ += mk()
            return out_

        def post_closures(hp):
            cs = slice(hp * 128, (hp + 1) * 128)
            rl, kl, vl = pl[:, hp, :], pl[:, 4 + hp, :], pl[:, 8 + hp, :]
            yraw = YR[hp % 2]
            t1, t2, t3, t4 = POSTT

            sqb_ = t3.bitcast(BF16)[:, 0:512]
            yrb_ = t3.bitcast(BF16)[:, 512:1024]
            t4b_ = t4.bitcast(BF16)[:, 0:512]

            def q1():
                p.act(sqb_, yraw, AF.Square)
                p.copy(yrb_, yraw, eng='pool')
                p.mm(PS[5], ones2b, yrb_)
                p.ts(t2, PS[5], 1.0 / 64, ALU.mult)

            def q1b():
                p.mm(PS[5], ones2b, sqb_)
                p.tt(t3, t2, t2, ALU.mult, eng='pool')
                p.stt(t3, PS[5], 1.0 / 64, t3, ALU.mult, ALU.subtract)
                p.stt(t4b_, rl, col(70 + hp), kl, ALU.mult, ALU.mult)

            def q1c():
                p.mm(PS[5], ones2b, t4b_)
                p.act(t3, t3, AF.Sqrt, bias=64e-5)
                p.tt(t4, PS[5], vl, ALU.mult)
                p.tt(t1, yraw, t2, ALU.subtract, eng='pool')

            def q2():
                p.recip(t3, t3)
                p.mm(PS[5], GUP[:, cs], SG)
                p.tt(t1, t1, t3, ALU.mult)
                p.ts(t1, t1, col(74 + hp), ALU.mult, col(78 + hp), ALU.add)

            def q3():
                p.tt(t1, t1, t4, ALU.add, eng='pool')
                p.tt(ycat[:, hp, :], PS[5], t1, ALU.mult)
            return [q1, q1b, q1c, q2, q3]

        def interleave(fg, bg):
            nf, nb = len(fg), len(bg)
            bi = 0
            for i, f_ in enumerate(fg):
                f_()
                tgt = ((i + 1) * nb) // max(nf, 1)
                while bi < tgt:
                    bg[bi]()
                    bi += 1
            while bi < nb:
                bg[bi]()
                bi += 1

        NU = 2 * nhp
        for f_ in prep_closures(0) + pre_closures(0):
            f_()
        for u in range(NU):
            bg = []
            if u % 2 == 0 and u >= 2:
                bg += post_closures(u // 2 - 1)
            if u + 1 < NU:
                bg += prep_closures(u + 1) + pre_closures(u + 1)
            interleave(serial_closures(u), bg)
        for f_ in post_closures(nhp - 1):
            f_()
        dbg_out("ycat%d" % tt, ycat)
        if 'wout' not in stages:
            return

        def ev_out(mc, ps_):
            p.tt(xT[:, mc, ts_], ps_, xT[:, mc, ts_], ALU.add)
        linear(w_out, w_out_b, D, D, lambda kc: ycat[:, kc, :], ev_out, tt == 0)

    o = 0
    U_ = bfv(o, o + 2176).rearrange("p (c t) -> p c t", t=544); o += 2176
    acc = scr[:, o:o + 8 * 512].rearrange("p (c t) -> p c t", t=512); o += 8 * 512
    SQ2 = o
    vT = bfv(o, o + 2048).rearrange("p (c t) -> p c t", t=512); o += 2048
    gsig = [scr[:, o:o + 512], scr[:, o + 512:o + 1024]]; o += 1024
    cm, cr, ctmp = scr[:, o:o + 512], scr[:, o + 512:o + 1024], scr[:, o + 1024:o + 1536]; o += 1536
    Dg = [bfv(o, o + 1984).rearrange("p (j q) -> p j q", q=128), bfv(o + 1984, o + 3968).rearrange("p (j q) -> p j q", q=128)]
    o += 3968
    assert o <= 17472

    def build_diag(c):
        p.tt(Dg[c % 2], ident.unsqueeze(1).to_broadcast([128, 31, 128]),
             pcol[:, 130 + c * 31:130 + (c + 1) * 31].unsqueeze(2).to_broadcast([128, 31, 128]), ALU.mult,
             eng='pool' if c % 2 == 0 else 'dve')

    def conv_tile(tt):
        ts_ = slice(tt * 512, (tt + 1) * 512)
        if ('conv', tt) not in prenormed:
            rmsnorm_tile(tt, 8)
        if tt == 0:
            p.memset(U_[:, :, 0:30], 0.0, eng='pool')
        else:
            p.copy(U_[:, :, 0:30], chalo, eng='pool')
        build_diag(0)
        build_diag(1)

        def ev_pw1(mc, ps_):
            if mc < 8:
                p.act(acc[:, mc, :], ps_, AF.Identity, bias=col(82 + mc))
            else:
                m = mc - 8
                gb = gsig[m % 2]
                p.act(gb, ps_, AF.Sigmoid, bias=col(82 + mc))
                p.tt(U_[:, m, 30:542], acc[:, m, :], gb, ALU.mult, eng='pool')
        linear(pw1, pw1_b, D, 2048, lambda kc: hT[:, kc, :], ev_pw1, tt == 0)
        p.copy(chalo, U_[:, :, 512:542], eng='pool')
        for c in range(8):
            ps_ = PS[4 + (c % 4)]
            for j in range(31):
                p.mm(ps_, Dg[c % 2][:, j, :], U_[:, c, j:j + 512], start=(j == 0), stop=(j == 30))
            p.act(acc[:, c, :], ps_, AF.Identity, bias=col(98 + c))
            if c + 2 < 8:
                build_diag(c + 2)
        for c in range(8):
            p.mm(PS[6], ones, acc[:, c, :], start=(c == 0), stop=(c == 7))
        for c in range(8):
            sqb = sqr[c % 2]
            p.act(sqb.bitcast(F32R), acc[:, c, :], AF.Square)
            p.mm(PS[7], onesr.bitcast(F32R), sqb.bitcast(F32R), start=(c == 0), stop=(c == 7))
        p.ts(cm, PS[6], 1.0 / D, ALU.mult)
        p.tt(ctmp, cm, cm, ALU.mult, eng='pool')
        p.stt(cr, PS[7], 1.0 / D, ctmp, ALU.mult, ALU.subtract)
        p.act(cr, cr, AF.Sqrt, bias=1e-5)
        p.recip(cr, cr)
        for c in range(8):
            p.tt(acc[:, c, :], acc[:, c, :], cm, ALU.subtract)
            p.tt(acc[:, c, :], acc[:, c, :], cr, ALU.mult, eng='pool')
            p.act(vT[:, c, :], acc[:, c, :], AF.Silu, bias=col(114 + c), scale=col(106 + c))
        dbg_out("vT%d" % tt, vT)

        def ev_pw2(mc, ps_):
            p.stt(xT[:, mc, ts_], ps_, col(122 + mc), xT[:, mc, ts_], ALU.add, ALU.add)
        linear(pw2, pw2_b, D, D, lambda kc: vT[:, kc, :], ev_pw2, tt == 0)

    for tt in range(NT):
        hybrid_tile(tt)
        dbg_out("xmix%d" % tt, xT[:, :, tt * 512:(tt + 1) * 512])
        if 'mlp0' in stages:
            def mid0(tt=tt):
                if tt + 1 < NT:
                    rmsnorm_tile(tt + 1, 0)
                    prenormed.add(('hyb', tt + 1))
                elif 'conv' in stages:
                    rmsnorm_tile(0, 8)
                    prenormed.add(('conv', 0))
            mlp_tile(tt, 0, mid0)
    dbg_out("xl0", xT)
    for tt in range(NT):
        if 'conv' in stages:
            conv_tile(tt)
        if 'mlp1' in stages:
            def mid1(tt=tt):
                if tt + 1 < NT and 'conv' in stages:
                    rmsnorm_tile(tt + 1, 8)
                    prenormed.add(('conv', tt + 1))
            mlp_tile(tt, 1, mid1)
    yo = scr[:, 0:4 * D].rearrange("p (b d) -> p b d", d=D)
    for tt in range(NT):
        rmsnorm_tile(tt, 32, dst=hTf)
        for b in range(4):
            for half in range(2):
                ps_ = PS[4 + ((b * 2 + half) % 4)]
                for cc in range(4):
                    c = half * 4 + cc
                    p.transpose(ps_[:, cc * 128:(cc + 1) * 128], hTf[:, c, b * 128:(b + 1) * 128], ident)
                p.copy(yo[:, b, half * 512:(half + 1) * 512], ps_, eng='dve' if half == 0 else 'act')
        p.dma(out[tt * 512:(tt + 1) * 512, :].rearrange("(b p) d -> p b d", p=128), yo,
              q='sp')
    nc = p.build()
    return nc, p


def host_weights(inp):
    w_in = np.asarray(inp['hy_w_in'][0], np.float32)
    qcols = 1792 + np.concatenate([np.concatenate([np.arange(j * 64, (j + 1) * 64), np.arange((j + 4) * 64, (j + 5) * 64)])
                                   for j in range(4)])
    perm = np.concatenate([np.arange(1792), qcols, np.arange(2304, 2560)])
    w_in_p = np.ascontiguousarray(w_in[:, perm])
    w_out = np.asarray(inp['hy_w_out'][0], np.float32)
    rperm = np.concatenate([np.arange(512), 512 + (qcols - 1792)])
    w_out_p = np.ascontiguousarray(w_out[rperm, :])
    lup = np.ascontiguousarray(np.concatenate([inp['hy_w_up'][0], inp['hy_a_up'][0]], 0).astype(np.float32))
    gup = np.ascontiguousarray(np.asarray(inp['hy_g_up'][0], np.float32))
    sinkb = np.ascontiguousarray(np.broadcast_to(np.asarray(inp['hy_sinks'][0], np.float32)[None, :], (128, 8)))
    return dict(w_in=w_in_p, w_out=w_out_p, pw1=np.ascontiguousarray(inp['cv_pw1_w'][0]),
                pw2=np.ascontiguousarray(inp['cv_pw2_w'][0]), w1=np.ascontiguousarray(inp['mlp_w1']),
                w2=np.ascontiguousarray(inp['mlp_w2']), lup=lup, gup=gup, sinkb=sinkb,
                cst=host_consts(), pcol=host_pcol(inp))


_CACHE = {}


def kernel(**inputs):
    inputs = {k: np.asarray(v) for k, v in inputs.items()}
    x = np.asarray(inputs['x'], np.float32)
    B, T, _ = x.shape
    shared = host_weights(inputs)
    nc, _ = build(T)
    in_maps = [dict(shared, x=np.ascontiguousarray(x[b])) for b in range(B)]
    res = run_bass_kernel_spmd(nc, in_maps, core_ids=list(range(B)))
    return np.stack([np.asarray(r["out"], np.float32) for r in res.results], 0)
```

```python
import numpy as np
from contextlib import ExitStack
import concourse.bass as bass
import concourse.mybir as mybir
from concourse.bass_utils import run_bass_kernel_spmd

F32 = mybir.dt.float32
BF16 = mybir.dt.bfloat16
ALU = mybir.AluOpType
AF = mybir.ActivationFunctionType
AX = mybir.AxisListType

NDMA = 12
BLK = 256
DTSZ = {mybir.dt.float32: 4, mybir.dt.bfloat16: 2, mybir.dt.float32r: 4}


class Prog:
    def __init__(self):
        self.nc = bass.Bass("TRN2", target_bir_lowering=False)
        self.es = ExitStack()
        self.ops = []
        self.tinfo = {}
        self.psum_names = set()
        self.ndma = 0

    def dram(self, name, shape, kind, dtype=F32):
        return self.nc.dram_tensor(name, list(shape), dtype, kind=kind).ap()

    def sb(self, name, shape, dtype=F32):
        t = self.es.enter_context(self.nc.sbuf_tensor(name, list(shape), dtype))
        ap = t[:]
        self.tinfo[ap.name] = int(np.prod(shape[1:])) * DTSZ[dtype]
        return ap

    def ps(self, name, shape, dtype=F32):
        t = self.es.enter_context(self.nc.psum_tensor(name, list(shape), dtype))
        ap = t[:]
        self.tinfo[ap.name] = int(np.prod(shape[1:]))
        self.psum_names.add(ap.name)
        return ap

    def blocks(self, ap):
        name = ap.name
        if name not in self.tinfo:
            return []
        if name in self.psum_names:
            return [(name, 0)]
        fs = self.tinfo[name]
        sz = DTSZ[ap.dtype]
        off = (int(ap.offset) * sz) % fs
        ext = 1
        for (st, cn) in ap.ap[1:]:
            ext += (cn - 1) * abs(st)
        ext *= sz
        b0 = off // BLK
        b1 = (off + ext - 1) // BLK
        return [(name, b) for b in range(b0, b1 + 1)]

    @staticmethod
    def _nfree(ap):
        n = 1
        for (st, cn) in ap.ap[1:]:
            n *= cn
        return n

    def _dur(self, eng, reads, writes, dma):
        n = self._nfree(writes[0]) if writes else 1
        if dma:
            return 0.1
        if eng == 'pe':
            return 0.06
        if eng == 'dve':
            return 0.08 + n / 960.0
        if eng == 'act':
            return 0.2 + n / 1200.0
        if eng == 'pool':
            return 0.3 + n / 480.0
        return 0.1

    def op(self, eng, fn, reads, writes, dma=False, rkeys=(), wkeys=()):
        rb = list(rkeys)
        for a in reads:
            if a is not None and not isinstance(a, (int, float)):
                rb += self.blocks(a)
        wb = list(wkeys)
        for a in writes:
            wb += self.blocks(a)
        wb += [b for b in rb if b[0] in self.psum_names]
        self.ops.append(dict(eng=eng, fn=fn, rb=rb, wb=wb, dma=dma, dur=self._dur(eng, reads, writes, dma)))

    @staticmethod
    def _cls(n):
        return 32 if n <= 32 else (64 if n <= 64 else 128)

    def _pemode(self, lhsT, tr):
        m = 1
        for (st, cn) in lhsT.ap[1:]:
            m *= cn
        return (int(lhsT.base_partition()), self._cls(int(lhsT.partition_size())), self._cls(m), tr)

    def mm(self, out, lhsT, rhs, start=True, stop=True):
        self.op('pe', lambda e: e.matmul(out, lhsT, rhs, start=start, stop=stop),
                [lhsT, rhs], [out])
        self.ops[-1]['pemode'] = (out.name, self._pemode(lhsT, False))
        n = self._nfree(rhs)
        base = max(0.055, n / 2200.0)
        dt_ = lhsT.dtype
        self.ops[-1]['dur'] = base * (4.0 if dt_ == mybir.dt.float32 else (2.0 if dt_ == mybir.dt.float32r else 1.0))

    def transpose(self, out, in_, ident):
        self.op('pe', lambda e: e.transpose(out, in_, ident), [in_, ident], [out])
        self.ops[-1]['pemode'] = (out.name, self._pemode(in_, True))
        self.ops[-1]['dur'] = 0.12

    def act(self, out, in_, func, bias=0.0, scale=1.0, eng='act'):
        self.op(eng, lambda e: e.activation(out, in_, func, bias=bias, scale=scale),
                [in_, bias, scale], [out])

    def tt(self, out, in0, in1, op, eng='dve'):
        self.op(eng, lambda e: e.tensor_tensor(out, in0, in1, op), [in0, in1], [out])

    def ts(self, out, in0, s1, op0, s2=None, op1=None, eng='dve'):
        if op1 is None:
            self.op(eng, lambda e: e.tensor_scalar(out, in0, s1, None, op0), [in0, s1], [out])
        else:
            self.op(eng, lambda e: e.tensor_scalar(out, in0, s1, s2, op0, op1), [in0, s1, s2], [out])

    def stt(self, out, in0, scalar, in1, op0, op1, eng='dve'):
        self.op(eng, lambda e: e.scalar_tensor_tensor(out, in0, scalar, in1, op0, op1),
                [in0, scalar, in1], [out])

    def copy(self, out, in_, eng='dve'):
        if eng == 'act':
            self.op(eng, lambda e: e.copy(out, in_), [in_], [out])
        else:
            self.op(eng, lambda e: e.tensor_copy(out, in_), [in_], [out])

    def memset(self, out, val, eng='dve'):
        self.op(eng, lambda e: e.memset(out, val), [], [out])

    def recip(self, out, in_):
        self.op('dve', lambda e: e.reciprocal(out, in_), [in_], [out])
        self.ops[-1]['dur'] = 0.1 + self._nfree(out) / 155.0

    def scan(self, out, d0, d1, init, op0, op1):
        self.op('dve', lambda e: e.tensor_tensor_scan(out, d0, d1, init, op0, op1), [d0, d1, init], [out])

    def dma(self, out, in_, q='sp', slow=False, rkeys=(), wkeys=()):
        if slow:
            self.op(q, lambda e: e.dma_start(out=out, in_=in_, allow_slow_non_contiguous=True), [in_], [out], dma=True,
                    rkeys=rkeys, wkeys=wkeys)
        else:
            self.op(q, lambda e: e.dma_start(out=out, in_=in_), [in_], [out], dma=True, rkeys=rkeys, wkeys=wkeys)

    def reschedule(self, window=256, lat=0.25, dma_lat=3.0, slack=0.2):
        ops = self.ops
        n = len(ops)
        last_w, readers = {}, {}
        preds = [None] * n
        for i, o in enumerate(ops):
            d = set()
            for b in o['rb']:
                w = last_w.get(b)
                if w is not None:
                    d.add(w)
            for b in o['wb']:
                w = last_w.get(b)
                if w is not None:
                    d.add(w)
                d.update(readers.get(b, ()))
            d.discard(i)
            preds[i] = d
            for b in o['rb']:
                readers.setdefault(b, []).append(i)
            for b in o['wb']:
                last_w[b] = i
                readers[b] = []
        queues = {}
        for i, o in enumerate(ops):
            queues.setdefault(o['eng'], []).append(i)
        dmas = [i for i, o in enumerate(ops) if o['dma']]
        for k in range(NDMA, len(dmas)):
            preds[dmas[k]].add(dmas[k - NDMA])
        done = [None] * n
        start = [None] * n
        free = {e: 0.0 for e in queues}
        head = {e: 0 for e in queues}
        issued = [False] * n
        remaining = n
        INF = 1e30

        def ready_time(i):
            t = 0.0
            for p_ in preds[i]:
                if not issued[p_]:
                    return INF
                dp = done[p_] + (0.0 if ops[p_]['eng'] == ops[i]['eng'] and not ops[p_]['dma'] else lat)
                if dp > t:
                    t = dp
            return t
        while remaining:
            best = None
            for e, q in queues.items():
                h = head[e]
                while h < len(q) and issued[q[h]]:
                    h += 1
                head[e] = h
                if h >= len(q):
                    continue
                hi = q[h]
                hr = ready_time(hi)
                hstart = max(free[e], hr) if hr < INF else INF
                cand = (hstart, hi)
                if hstart > free[e] + 1e-9:
                    cnt = 0
                    j = h + 1
                    while j < len(q) and cnt < window:
                        oj = q[j]
                        j += 1
                        if issued[oj]:
                            continue
                        cnt += 1
                        if ops[oj]['dma']:
                            continue
                        r = ready_time(oj)
                        if r >= INF:
                            continue
                        s = max(free[e], r)
                        if s + ops[oj]['dur'] <= hstart + slack + 1e-9 and s < cand[0]:
                            cand = (s, oj)
                if cand[0] < INF and (best is None or cand < best[0:2]):
                    best = (cand[0], cand[1], e)
            assert best is not None, "scheduler deadlock"
            s, i, e = best
            o = ops[i]
            start[i] = s
            issued[i] = True
            remaining -= 1
            free[e] = s + o['dur']
            done[i] = s + o['dur'] + (dma_lat if o['dma'] else 0.0)
        order = sorted(range(n), key=lambda i: (start[i], i))
        self.ops = [ops[i] for i in order]
        self.sim_time = max(done)

    def build(self, resched=True):
        nc = self.nc
        if resched:
            self.reschedule()
        ENG = ['pe', 'dve', 'act', 'pool', 'sp']
        engobj = {'pe': nc.tensor, 'dve': nc.vector, 'act': nc.scalar, 'pool': nc.gpsimd, 'sp': nc.sync}
        last_w = {}
        readers = {}
        bank_mode = {}
        known = {e: {} for e in ENG}
        eidx = {e: 0 for e in ENG}
        slot_last = [None] * NDMA
        slot_cnt = [0] * NDMA
        ndma = 0
        ops = self.ops
        for i, o in enumerate(ops):
            e = o['eng']
            deps = set()
            for b in o['rb']:
                w = last_w.get(b)
                if w is not None:
                    deps.add(w)
            for b in o['wb']:
                w = last_w.get(b)
                if w is not None:
                    deps.add(w)
                for r in readers.get(b, ()):
                    deps.add(r)
            forced = set()
            if 'pemode' in o:
                bank, mode = o['pemode']
                lm = bank_mode.get(bank)
                if lm is not None and lm[0] != mode:
                    deps.add(lm[1])
                    forced.add(lm[1])
                bank_mode[bank] = (mode, i)
            if o['dma']:
                k = ndma % NDMA
                ndma += 1
                o['slot'] = k
                if slot_last[k] is not None:
                    deps.add(slot_last[k])
                slot_cnt[k] += 1
                o['dval'] = 16 * slot_cnt[k]
                slot_last[k] = i
                o['key'] = ('dma', k)
                o['kidx'] = slot_cnt[k]
            else:
                eidx[e] += 1
                o['key'] = ('eng', e)
                o['kidx'] = eidx[e]
            waits = []
            vc = {}
            kn = known[e]
            for d in sorted(deps, reverse=True):
                if d == i:
                    continue
                p = ops[d]
                if p['key'] == ('eng', 'pe') and e == 'pe' and not o['dma'] and d not in forced:
                    continue
                if kn.get(p['key'], 0) >= p['kidx']:
                    continue
                waits.append(d)
                p['signal'] = True
                for k2, v2 in p['vc'].items():
                    if kn.get(k2, 0) < v2:
                        kn[k2] = v2
            o['waits'] = waits
            vc = dict(kn)
            vc[o['key']] = o['kidx']
            o['vc'] = vc
            o.setdefault('signal', False)
            if o['dma']:
                o['signal'] = True
            for b in o['rb']:
                readers.setdefault(b, []).append(i)
            for b in o['wb']:
                last_w[b] = i
                readers[b] = []
        self.final_dma = [(k, 16 * slot_cnt[k]) for k in range(NDMA) if slot_cnt[k]]
        sems = {e: self.es.enter_context(nc.semaphore("s_" + e)) for e in ENG}
        dsems = [self.es.enter_context(nc.semaphore("d_%d" % k)) for k in range(NDMA)]
        sig = {e: 0 for e in ENG}
        for o in ops:
            if o['dma']:
                o['sem'] = dsems[o['slot']]
                o['sval'] = o['dval']
            elif o['signal']:
                sig[o['eng']] += 1
                o['sem'] = sems[o['eng']]
                o['sval'] = sig[o['eng']]
        self.nwaits = sum(len(o['waits']) for o in ops)
        self.nsig = dict(sig)
        per = {e: [o for o in ops if o['eng'] == e] for e in ENG}
        block = self.es.enter_context(nc.Block())

        def emit(eng_name):
            def f(eo):
                for o in per[eng_name]:
                    for d in o['waits']:
                        p = ops[d]
                        eo.wait_ge(p['sem'], p['sval'])
                    ins = o['fn'](eo)
                    if o['dma']:
                        ins.then_inc(o['sem'], 16)
                    elif o['signal']:
                        ins.then_inc(o['sem'], 1)
                if eng_name == 'sp':
                    for k, v in self.final_dma:
                        eo.wait_ge(dsems[k], v)
            return f
        block.tensor(emit('pe'))
        block.vector(emit('dve'))
        block.scalar(emit('act'))
        block.gpsimd(emit('pool'))
        block.sync(emit('sp'))
        self.es.close()
        return nc


D = 1024
C0 = float(np.exp(-0.5))
NPC = 384
F32R = mybir.dt.float32r
RDT = BF16
O_ID, O_ONES2, O_MG, O_MNT, O_ID2, O_D0, O_BC, O_BP, O_SK, NCST = 0, 128, 256, 384, 512, 640, 1152, 2176, 3200, 3208


def host_consts():
    c = np.zeros((128, NCST), np.float32)
    c[:, O_ID:O_ID + 128] = np.eye(128)
    c[0:64, O_ONES2:O_ONES2 + 64] = 1.0
    c[64:128, O_ONES2 + 64:O_ONES2 + 128] = 1.0
    j = np.arange(64)[:, None]
    i = np.arange(64)[None, :]
    lt = (j < i).astype(np.float32)
    le = (j <= i).astype(np.float32)
    mg = np.zeros((128, 128), np.float32)
    mg[0:64, 0:64] = lt
    mg[0:64, 64:128] = le
    mg[64:128, 0:64] = lt
    mg[64:128, 64:128] = le
    c[:, O_MG:O_MG + 128] = mg
    c[0:64, O_MNT:O_MNT + 64] = (i.T > j.T).astype(np.float32).T * 0 + (np.arange(64)[:, None] > np.arange(64)[None, :])
    c[0:64, O_MNT + 64:O_MNT + 128] = c[0:64, O_MNT:O_MNT + 64]
    c[0:64, O_ID2:O_ID2 + 64] = np.eye(64)
    c[0:64, O_ID2 + 64:O_ID2 + 128] = np.eye(64)
    d0 = np.ones((512,), np.float32)
    d0[0::64] = 0.0
    c[:, O_D0:O_D0 + 512] = d0[None, :]
    key = np.arange(128)[:, None]
    q = np.arange(128)[None, :]
    for h in range(8):
        slope = 2.0 ** (-(h + 1))
        cur = np.where(q >= key, -slope * (q - key), -30000.0)
        prv = np.where(key > q, -slope * (q + 128 - key), -30000.0)
        c[:, O_BC + h * 128:O_BC + (h + 1) * 128] = cur
        c[:, O_BP + h * 128:O_BP + (h + 1) * 128] = prv
    return c


def host_pcol(inp):
    pc = np.zeros((128, NPC), np.float32)

    def put(col, vec):
        v = np.asarray(vec, np.float32).reshape(-1, 128)
        pc[:, col:col + v.shape[0]] = v.T
    put(0, inp['norm_mix_g'])
    put(16, inp['norm_ffn_g'])
    put(32, inp['final_norm_g'])
    put(40, inp['hy_mu'][0])
    put(54, inp['hy_w0'][0])
    put(58, inp['hy_a0'][0])
    put(62, inp['hy_k_k'][0])
    put(66, inp['hy_k_a'][0])
    put(70, inp['hy_r_k'][0])
    put(74, inp['hy_gn_g'][0])
    put(78, inp['hy_gn_b'][0])
    put(82, inp['cv_pw1_b'][0])
    put(98, inp['cv_dw_b'][0])
    put(106, inp['cv_ln_g'][0])
    put(114, inp['cv_ln_b'][0])
    put(122, inp['cv_pw2_b'][0])
    dw = np.asarray(inp['cv_dw_w'][0], np.float32)
    pc[:, 130:130 + 248] = dw.T.reshape(8, 128, 31).transpose(1, 0, 2).reshape(128, 248)
    return pc


def build(T=2048, dbg=(), stages=('att', 'rwkv', 'wout', 'mlp0', 'conv', 'mlp1'), nchunks=8, nhp=4, cut=99):
    p = Prog()
    NT = T // 512
    x = p.dram("x", [T, D], "ExternalInput")
    out = p.dram("out", [T, D], "ExternalOutput")
    w_in = p.dram("w_in", [D, 2560], "ExternalInput")
    w_out = p.dram("w_out", [D, D], "ExternalInput")
    pw1 = p.dram("pw1", [D, 2048], "ExternalInput")
    pw2 = p.dram("pw2", [D, D], "ExternalInput")
    w1 = p.dram("w1", [2, D, 4096], "ExternalInput")
    w2 = p.dram("w2", [2, 4096, D], "ExternalInput")
    lupd = p.dram("lup", [128, 512], "ExternalInput")
    gupd = p.dram("gup", [128, 512], "ExternalInput")
    cstd = p.dram("cst", [128, NCST], "ExternalInput")
    pcold = p.dram("pcol", [128, NPC], "ExternalInput")
    sinkd = p.dram("sinkb", [128, 8], "ExternalInput")
    dbg_outs = {}
    def wscr(name, K, M, lead=None):
        nt = (K // 256) * (M // 512)
        return p.dram(name, ([lead] if lead else []) + [nt, 128, 1024], "Internal", dtype=BF16)
    w_in_b = wscr("w_in_b", D, 2560)
    w_out_b = wscr("w_out_b", D, D)
    pw1_b = wscr("pw1_b", D, 2048)
    pw2_b = wscr("pw2_b", D, D)
    w1_b = wscr("w1_b", D, 4096, 2)
    w2_b = wscr("w2_b", 4096, D, 2)

    def dbg_out(name, ap):
        if name in dbg:
            shp = list(ap.shape)
            dt_ = p.dram("dbg_" + name, shp, "ExternalOutput", dtype=ap.dtype)
            p.dma(dt_, ap)

    cst = p.sb("cstsb", [128, NCST])
    pcol = p.sb("pcolsb", [128, NPC])
    ones = p.sb("ones", [128, 128])
    onesr = p.sb("onesr", [128, 128])
    sqr = [p.sb("sqr%d" % i, [128, 512]) for i in range(2)]
    esink = p.sb("esink", [128, 8])
    xT = p.sb("xT", [128, 8, T])
    wst = [p.sb("wst%d" % i, [128, 2, 512]) for i in range(2)]
    NWB = 4
    wbf = [p.sb("wbf%d" % i, [128, 2, 512], dtype=BF16) for i in range(NWB)]
    LUP = p.sb("LUP", [128, 512], dtype=BF16)
    GUP = p.sb("GUP", [128, 512], dtype=BF16)
    ones2b = p.sb("ones2b", [128, 128], dtype=BF16)
    Hbd = p.sb("Hbd", [128, 4, 128])
    UV = p.sb("UV", [128, 2, 128])
    ZV = p.sb("ZV", [128, 2, 64])
    Xn = [p.sb("Xn%d" % i, [64, 4, 128], dtype=RDT) for i in range(2)]
    Xtn = [p.sb("Xtn%d" % i, [64, 4, 128], dtype=RDT) for i in range(2)]
    Psh = [p.sb("Psh%d" % i, [64, 4, 128], dtype=RDT) for i in range(2)]
    Pa = p.sb("Pa", [64, 4, 128])
    Pb = [p.sb("Pb%d" % i, [64, 4, 128]) for i in range(2)]
    Zs = p.sb("Zs", [64, 128])
    BKhat = p.sb("BKhat", [128, 128])
    M2b = [p.sb("M2b%d" % i, [128, 4, 2, 128]) for i in range(2)]
    WC2 = [p.sb("WC2_%d" % i, [128, 4]) for i in range(2)]
    ST2 = [p.sb("ST2_%d" % i, [128, 4]) for i in range(2)]
    carry = p.sb("carry", [128, 16])
    kT = p.sb("kT", [128, 640], dtype=BF16)
    vTM = p.sb("vTM", [128, 5, 128], dtype=BF16)
    onesb = p.sb("onesb", [128, 128], dtype=BF16)
    st1 = p.sb("st1", [128, 512])
    NSCR = 19520
    scr = p.sb("scr", [128, NSCR])

    def bfv(a, b):
        return scr[:, a:b].bitcast(BF16)
    hT = bfv(17472, 17472 + 2048).rearrange("p (c t) -> p c t", t=512)
    hTf = scr[:, 8192:8192 + 4096].rearrange("p (c t) -> p c t", t=512)
    chalo = p.sb("chalo", [128, 8, 30], dtype=BF16)
    PS = [p.ps("ps%d" % i, [128, 512]) for i in range(8)]

    ident = cst[:, O_ID:O_ID + 128]
    ones2 = cst[:, O_ONES2:O_ONES2 + 128]
    maskG = cst[:, O_MG:O_MG + 128]
    maskNt2 = cst[0:64, O_MNT:O_MNT + 128]
    ident2 = cst[0:64, O_ID2:O_ID2 + 128]
    d0 = cst[:, O_D0:O_D0 + 512]
    Bcur = cst[:, O_BC:O_BC + 1024].rearrange("p (h q) -> p h q", q=128)
    Bprev = cst[:, O_BP:O_BP + 1024].rearrange("p (h q) -> p h q", q=128)

    def col(i):
        return pcol[:, i:i + 1]

    p.dma(cst, cstd)
    p.dma(pcol, pcold)
    p.dma(esink, sinkd)
    p.memset(ones, 1.0)
    p.copy(onesb, ones)
    p.copy(onesr.bitcast(F32R), ones)
    p.memset(Hbd, 0.0, eng='pool')
    p.memset(UV, 0.0, eng='pool')
    p.memset(ZV, 0.0, eng='pool')
    p.memset(carry, 0.0, eng='pool')
    p.act(esink, esink, AF.Exp)
    p.ts(pcol[:, 378:382], pcol[:, 66:70], -1.0, ALU.mult, 1.0, ALU.add)

    p.dma(scr[:, 8192:8704], lupd)
    p.dma(scr[:, 8704:9216], gupd)
    p.copy(LUP, scr[:, 8192:8704])
    p.copy(GUP, scr[:, 8704:9216], eng='act')
    p.copy(ones2b, ones2)
    xin_region = scr[:, 0:4 * D].rearrange("p (b d) -> p b d", d=D)
    for tt in range(NT):
        p.dma(xin_region, x[tt * 512:(tt + 1) * 512, :].rearrange("(b p) d -> p b d", p=128),
              q='sp')
        for c in range(8):
            ps_ = PS[4 + (c % 4)]
            for b in range(4):
                p.transpose(ps_[:, b * 128:(b + 1) * 128], xin_region[:, b, c * 128:(c + 1) * 128], ident)
            p.copy(xT[:, c, tt * 512:(tt + 1) * 512], ps_, eng='dve' if c % 2 == 0 else 'act')

    lin_ctr = [0, 0]
    pend_store = []

    def linear(W, Wb, K, M, rhs_fn, evac_fn, first, hook=None):
        nkg = K // 256
        nmg = (M + 511) // 512
        for mg in range(nmg):
            mw = min(512, M - mg * 512)
            nm = mw // 128
            banks = PS[0:4] if lin_ctr[0] % 2 == 0 else PS[4:8]
            lin_ctr[0] += 1
            for kg in range(nkg):
                wt = wbf[lin_ctr[1] % NWB]
                q = 'sp'
                key = (Wb.name, int(Wb.offset), kg, mg)
                assert mw == 512
                wsrc = Wb[mg * nkg + kg].rearrange("p (kc m) -> p kc m", kc=2)
                if first:
                    ws = wst[lin_ctr[1] % 2]
                    ce = 'dve' if lin_ctr[1] % 2 == 0 else 'act'
                    p.dma(ws[:, :, 0:mw],
                          W[kg * 256:(kg + 1) * 256, mg * 512:mg * 512 + mw].rearrange("(kc p) m -> p kc m", p=128), q=q)
                    p.copy(wt[:, :, 0:mw], ws[:, :, 0:mw], eng=ce)
                    if pend_store:
                        pend_store.pop()()
                    pend_store.append(lambda wsrc=wsrc, wt=wt, mw=mw, key=key: p.dma(wsrc, wt[:, :, 0:mw], q='sp', wkeys=[key]))
                else:
                    p.dma(wt[:, :, 0:mw], wsrc, q=q, rkeys=[key])
                lin_ctr[1] += 1
                for kc in range(2):
                    for m in range(nm):
                        p.mm(banks[m], wt[:, kc, m * 128:(m + 1) * 128], rhs_fn(kg * 2 + kc),
                             start=(kg == 0 and kc == 0), stop=(kg == nkg - 1 and kc == 1))
            for m in range(nm):
                evac_fn(mg * 4 + m, banks[m])
            if hook is not None and mg == 0:
                hook()
        if pend_store:
            pend_store.pop()()

    def rmsnorm_tile(tt, gcol0, dst=None):
        dst = hT if dst is None else dst
        ts_ = slice(tt * 512, (tt + 1) * 512)
        for c in range(8):
            sqb = sqr[c % 2]
            p.act(sqb.bitcast(F32R), xT[:, c, ts_], AF.Square)
            p.mm(PS[7], onesr.bitcast(F32R), sqb.bitcast(F32R), start=(c == 0), stop=(c == 7))
        p.act(st1, PS[7], AF.Sqrt, bias=1e-6, scale=1.0 / D)
        p.recip(st1, st1)
        for c in range(8):
            p.stt(dst[:, c, :], xT[:, c, ts_], col(gcol0 + c), st1, ALU.mult, ALU.mult)

    prenormed = set()

    def mlp_tile(tt, layer, mid=None):
        ts_ = slice(tt * 512, (tt + 1) * 512)
        rmsnorm_tile(tt, 16 + layer * 8)
        h1 = bfv(0, 8192).rearrange("p (c t) -> p c t", t=512)

        def ev1(mc, ps_):
            p.act(h1[:, mc, :], ps_, AF.Relu)
            p.tt(h1[:, mc, :], h1[:, mc, :], h1[:, mc, :], ALU.mult, eng='pool')
        linear(w1[layer], w1_b[layer], D, 4096, lambda kc: hT[:, kc, :], ev1, tt == 0)

        def ev2(mc, ps_):
            p.tt(xT[:, mc, ts_], ps_, xT[:, mc, ts_], ALU.add)
        linear(w2[layer], w2_b[layer], 4096, D, lambda kc: h1[:, kc, :], ev2, tt == 0, hook=mid)

    o = 0
    pl = scr[:, o:o + 14 * 512].rearrange("p (c t) -> p c t", t=512); o += 14 * 512
    ycat = bfv(o, o + 2048).rearrange("p (c t) -> p c t", t=512); o += 2048
    prawb = [scr[:, o:o + 513], scr[:, o + 544:o + 544 + 513]]; o += 1088
    TL = bfv(o, o + 256); o += 512
    SG = bfv(o, o + 256); o += 512
    SLOT0 = o
    assert SLOT0 + 12 * 512 == 17472, SLOT0
    slot = [scr[:, o + i * 512:o + (i + 1) * 512] for i in range(16)]
    qT = bfv(o, o + 1024).rearrange("p (c t) -> p c t", t=512)
    vattT = slot[4]
    sTa, sTb, denb = slot[5], slot[6], slot[9]
    eTa = bfv(SLOT0 + 7 * 512, SLOT0 + 7 * 512 + 256)
    eTb = bfv(SLOT0 + 8 * 512, SLOT0 + 8 * 512 + 256)
    o += 16 * 512
    assert o <= NSCR

    def hybrid_tile(tt):
        ts_ = slice(tt * 512, (tt + 1) * 512)
        if ('hyb', tt) not in prenormed:
            rmsnorm_tile(tt, 0)

        def ev_in(mc, ps_):
            if mc < 14:
                pb = prawb[mc % 2]
                p.copy(pb[:, 0:1], carry[:, mc:mc + 1], eng='pool')
                p.copy(pb[:, 1:513], ps_, eng='act')
                p.copy(carry[:, mc:mc + 1], pb[:, 512:513], eng='pool')
                p.tt(pl[:, mc, :], pb[:, 0:512], pb[:, 1:513], ALU.subtract)
                p.stt(pl[:, mc, :], pl[:, mc, :], col(40 + mc), pb[:, 1:513], ALU.mult, ALU.add)
            elif mc < 18:
                p.copy(qT[:, mc - 14, :], ps_, eng='act')
            elif mc == 18:
                p.copy(kT[:, 128:640], ps_, eng='act')
            else:
                p.copy(vattT, ps_, eng='act')
        linear(w_in, w_in_b, D, 2560, lambda kc: hT[:, kc, :], ev_in, tt == 0)

        if 'att' not in stages:
            return
        for b in range(4):
            p.transpose(PS[6][:, b * 128:(b + 1) * 128], vattT[:, b * 128:(b + 1) * 128], ident)
        p.copy(vTM[:, 1:5, :], PS[6].rearrange("p (b d) -> p b d", d=128))
        for blk in range(4):
            first = (tt == 0 and blk == 0)
            for g in range(2):
                rows = slice(g * 64, (g + 1) * 64)
                pc_, pp_ = (PS[1], PS[0]) if g == 0 else (PS[3], PS[2])
                qv = qT[rows, :, blk * 128:(blk + 1) * 128]
                p.mm(pc_, kT[rows, 128 + blk * 128:256 + blk * 128], qv)
                if not first:
                    p.mm(pp_, kT[rows, blk * 128:128 + blk * 128], qv)
                p.stt(sTa.rearrange("p (h q) -> p h q", q=128), pc_.rearrange("p (h q) -> p h q", q=128),
                      0.125, Bcur[:, 4 * g:4 * g + 4, :], ALU.mult, ALU.add)
                p.act(eTa, sTa, AF.Exp)
                if not first:
                    p.stt(sTb.rearrange("p (h q) -> p h q", q=128), pp_.rearrange("p (h q) -> p h q", q=128),
                          0.125, Bprev[:, 4 * g:4 * g + 4, :], ALU.mult, ALU.add)
                    p.act(eTb, sTb, AF.Exp)
                p.mm(PS[4], vTM[:, blk + 1, :], eTa, start=True, stop=first)
                if not first:
                    p.mm(PS[4], vTM[:, blk, :], eTb, start=False, stop=True)
                p.mm(PS[5], onesb, eTa, start=True, stop=first)
                if not first:
                    p.mm(PS[5], onesb, eTb, start=False, stop=True)
                den3 = denb.rearrange("p (h q) -> p h q", q=128)
                p.tt(den3, PS[5].rearrange("p (h q) -> p h q", q=128),
                     esink[:, 4 * g:4 * g + 4].unsqueeze(2).to_broadcast([128, 4, 128]), ALU.add)
                p.recip(denb, denb)
                p.tt(ycat[rows, 4:8, blk * 128:(blk + 1) * 128],
                     PS[4][rows, :].rearrange("p (h q) -> p h q", q=128), den3[rows], ALU.mult)
        p.copy(kT[:, 0:128], kT[:, 512:640], eng='pool')
        p.copy(vTM[:, 0, :], vTM[:, 4, :], eng='pool')

        if 'rwkv' not in stages:
            dbg_out("ycat%d" % tt, ycat)
            return
        p.copy(TL[64:128, :], pl[64:128, 12, :], eng='pool')
        p.act(TL[0:64, :], pl[0:64, 12, :], AF.Tanh)
        p.act(SG, pl[:, 13, :], AF.Sigmoid)

        def sl(i, n=512):
            b8 = SLOT0 + i * 512
            return scr[:, b8:b8 + n]

        def bigv(i):
            a_ = sl(i)
            return (a_.rearrange("p (c two t) -> p c two t", two=2, t=64), a_.rearrange("p (c n) -> p c n", n=128))
        BIG = [[bigv(s_ * 4 + k_) for k_ in range(4)] for s_ in range(2)]
        PT = [scr[:, SLOT0 + 8 * 512 + i * 256:SLOT0 + 8 * 512 + (i + 1) * 256] for i in range(8)]
        YR = [sl(12), sl(13)]
        POSTT = [sl(14), sl(15), prawb[0][:, 0:512], prawb[1][:, 0:512]]

        def v3(a):
            return a.rearrange("p (c t) -> p c t", t=64)

        def prep_closures(u):
            hp, half = u // 2, u % 2
            par = u % 2
            cs = slice(hp * 128, (hp + 1) * 128)
            tok = slice(half * 256, (half + 1) * 256)
            rl, kl, vl = pl[:, hp, tok], pl[:, 4 + hp, tok], pl[:, 8 + hp, tok]
            sS, sEe, sEi, sEni, sA, sKK, sT, sB = PT
            sTb = sT.bitcast(BF16)[:, 0:256]
            (AR4, AR3), (BK4, BK3), (BKh4, BKh3), (XV4, XV3) = BIG[par]
            WCu, STu = WC2[par], ST2[par]
            d0h = d0[:, 0:256]

            def c1():
                p.mm(PS[1][:, 0:256], LUP[0:64, cs], TL[0:64, tok])
                p.act(sEe, PS[1][:, 0:256], AF.Sigmoid, bias=col(54 + hp))
                p.scan(sS, d0h, sEe, 0.0, ALU.mult, ALU.add)
                p.tt(sEe, sS, sEe, ALU.subtract)
                p.act(sEe, sEe, AF.Exp, scale=-C0)

            def c2():
                p.act(sEi, sS, AF.Exp, scale=-C0)
                p.act(sEni, sS, AF.Exp, scale=C0)
                p.copy(STu, v3(sS)[:, :, 63], eng='pool')
                p.act(WCu, STu, AF.Exp, scale=-C0)
                p.tt(v3(sS), STu.unsqueeze(2).to_broadcast([128, 4, 64]), v3(sS), ALU.subtract, eng='pool')
                p.act(sS, sS, AF.Exp, scale=-C0)

            def c3():
                p.ts(sKK, kl, col(62 + hp), ALU.mult)
                p.act(sTb, sKK, AF.Square)
                p.mm(PS[3][:, 0:256], LUP[64:128, cs], TL[64:128, tok])
                p.act(sA, PS[3][:, 0:256], AF.Sigmoid, bias=col(58 + hp))

            def c3b():
                p.mm(PS[5][:, 0:256], ones2b, sTb)
                p.act(sT, PS[5][:, 0:256], AF.Sqrt)

            def c4():
                p.ts(sT, sT, 1e-12, ALU.max)
                p.recip(sT, sT)
                p.tt(sKK, sKK, sT, ALU.mult)
                p.ts(sT, sA, col(66 + hp), ALU.mult, col(378 + hp), ALU.add)
                p.tt(kl, kl, sT, ALU.mult, eng='pool')
                p.tt(sB, sKK, sA, ALU.mult, eng='pool')

            def c5():
                p.stt(AR4[:, :, 0, :], v3(sKK), -1.0, v3(sEe), ALU.mult, ALU.mult)
                p.tt(AR4[:, :, 1, :], v3(rl), v3(sEi), ALU.mult, eng='pool')
                p.tt(BK4[:, :, 0, :], v3(sB), v3(sEni), ALU.mult)
                p.tt(BK4[:, :, 1, :], v3(kl), v3(sEni), ALU.mult, eng='pool')

            def c6():
                p.tt(BKh4[:, :, 0, :], v3(sB), v3(sS), ALU.mult)
                p.tt(BKh4[:, :, 1, :], v3(kl), v3(sS), ALU.mult, eng='pool')
                p.memset(XV4[:, :, 0, :], 0.0, eng='pool')
                p.copy(XV4[:, :, 1, :], v3(vl), eng='pool')
            return [c1, c2, c3, c3b, c4, c5, c6]

        def pre_closures(u):
            par = u % 2
            (AR4, AR3), (BK4, BK3), _, _ = BIG[par]
            M2h = M2b[par]
            Pfin = Pb[par]
            st = []

            def stageA():
                for ci in range(4):
                    for h in range(2):
                        rows = slice(h * 64, (h + 1) * 64)
                        bank = PS[h * 2 + ci // 2]
                        o_ = (ci % 2) * 256
                        p.mm(bank[:, o_:o_ + 128], BK3[rows, ci, :], AR3[rows, ci, :])
                        p.mm(bank[:, o_ + 128:o_ + 192], AR3[rows, ci, :], BK4[rows, ci, 0, :])
                for h in range(2):
                    for pr in range(2):
                        bank = PS[h * 2 + pr].rearrange("p (c n) -> p c n", n=256)
                        p.tt(M2h[:, pr * 2:pr * 2 + 2, h, :], bank[:, :, 0:128],
                             maskG.unsqueeze(1).to_broadcast([128, 2, 128]), ALU.mult)
                        p.tt(Xtn[0][:, pr * 2:pr * 2 + 2, h * 64:(h + 1) * 64], bank[0:64, :, 128:192],
                             maskNt2[:, 0:64].unsqueeze(1).to_broadcast([64, 2, 64]), ALU.mult)
                p.tt(Pa.rearrange("p c (h t) -> p (c h) t", t=64),
                     M2h[0:64].rearrange("p c h n -> p (c h) n")[:, :, 0:64],
                     ident2[:, 0:64].unsqueeze(1).to_broadcast([64, 8, 64]), ALU.add, eng='pool')
                p.copy(Xn[0].rearrange("p c (h t) -> p (c h) t", t=64),
                       M2h[0:64].rearrange("p c h n -> p (c h) n")[:, :, 0:64], eng='pool')
                p.copy(Psh[0].rearrange("p c n -> p (c n)"), Pa.rearrange("p c n -> p (c n)"), eng='act')
            st.append(stageA)

            def mk_round(k, sb):
                cis = (0, 1) if sb == 0 else (2, 3)
                bX, bXt = (PS[0], PS[1]) if sb == 0 else (PS[2], PS[3])
                c0 = cis[0]

                def sq():
                    nx = (k + 1) % 2
                    for ci in cis:
                        for h in range(2):
                            hs = slice(h * 64, (h + 1) * 64)
                            xc = Xn[k % 2][:, ci, hs]
                            xtc = Xtn[k % 2][:, ci, hs]
                            uo = ((ci - c0) * 2 + h) * 64
                            if k < 4:
                                p.mm(bX[0:64, uo:uo + 64], xtc, xc)
                            p.mm(bXt[0:64, uo:uo + 64], xc, xtc)
                    if k < 4:
                        p.copy(Xn[nx][:, c0:c0 + 2, :].rearrange("p c n -> p (c n)"), bX[0:64, 0:256], eng='act')
                    p.copy(Xtn[nx][:, c0:c0 + 2, :].rearrange("p c n -> p (c n)"), bXt[0:64, 0:256], eng='dve')

                def pu():
                    nx = (k + 1) % 2
                    Pcur = Pa if k % 2 == 0 else Pfin
                    Pnxt = Pfin if k % 2 == 0 else Pa
                    for ci in cis:
                        for h in range(2):
                            hs = slice(h * 64, (h + 1) * 64)
                            uo = 256 + ((ci - c0) * 2 + h) * 64
                            p.mm(bX[0:64, uo:uo + 64], Xtn[nx][:, ci, hs], Psh[k % 2][:, ci, hs])
                    p.tt(Pnxt[:, c0:c0 + 2, :].rearrange("p c n -> p (c n)"), bX[0:64, 256:512],
                         Pcur[:, c0:c0 + 2, :].rearrange("p c n -> p (c n)"), ALU.add)
                    if k < 4:
                        p.copy(Psh[nx][:, c0:c0 + 2, :].rearrange("p c n -> p (c n)"),
                               Pnxt[:, c0:c0 + 2, :].rearrange("p c n -> p (c n)"), eng='act')
                return sq, pu
            for k in range(5):
                sq0, pu0 = mk_round(k, 0)
                sq1, pu1 = mk_round(k, 1)
                st += [sq0, sq1, pu0, pu1]
            return st

        def serial_closures(u):
            hp, half = u // 2, u % 2
            par = u % 2
            (AR4, AR3), _, (BKh4, BKh3), (XV4, XV3) = BIG[par]
            yraw = YR[hp % 2]
            out_ = []
            for ci in range(4):
                def mk(ci=ci):
                    ARc = AR4[:, ci]
                    M2 = M2b[par][:, ci]
                    Tt = Pb[par][:, ci, :]
                    c = half * 4 + ci

                    def s1():
                        p.transpose(PS[6][:, 0:128], BKh3[:, ci, :], ident)
                        p.transpose(PS[6][:, 128:256], XV3[:, ci, :], ident)
                        p.copy(BKhat, PS[6][:, 0:128], eng='act')
                        for h in range(2):
                            src = PS[6][64:128, 128 + h * 64:128 + (h + 1) * 64]
                            p.copy(UV[64:128, h, h * 64:(h + 1) * 64], src, eng='act')
                            p.copy(ZV[64:128, h, :], src, eng='act')

                    def s2():
                        for h in range(2):
                            zo = PS[4][0:64, h * 64:(h + 1) * 64]
                            p.mm(zo, ARc[:, 0, :], Hbd[:, hp, h * 64:(h + 1) * 64], start=True, stop=False)
                            p.mm(zo, M2[:, h, 0:64], ZV[:, h, :], start=False, stop=True)
                        p.copy(Zs, PS[4][0:64, 0:128], eng='dve')
                        for h in range(2):
                            p.mm(PS[4][0:64, 256 + h * 64:256 + (h + 1) * 64], Tt[:, h * 64:(h + 1) * 64], Zs[:, h * 64:(h + 1) * 64])
                        for h in range(2):
                            p.copy(UV[0:64, h, h * 64:(h + 1) * 64], PS[4][0:64, 256 + h * 64:256 + (h + 1) * 64], eng='dve')

                    def s3():
                        p.mm(PS[7][:, 0:64], Hbd[:, hp, :], ARc[:, 1, :], start=True, stop=False)
                        p.mm(PS[7][:, 0:64], UV[:, 0, :], M2[:, 0, 64:128], start=False, stop=False)
                        p.mm(PS[7][:, 0:64], UV[:, 1, :], M2[:, 1, 64:128], start=False, stop=True)
                        p.mm(PS[7][:, 128:256], BKhat, UV[:, 0, :], start=True, stop=False)
                        p.mm(PS[7][:, 128:256], BKhat, UV[:, 1, :], start=False, stop=True)
                        p.copy(yraw[:, c * 64:(c + 1) * 64], PS[7][:, 0:64], eng='dve')
                        for h in range(2):
                            rows = slice(h * 64, (h + 1) * 64)
                            hb = Hbd[rows, hp, h * 64:(h + 1) * 64]
                            p.stt(hb, hb, WC2[par][rows, ci:ci + 1], PS[7][rows, 128 + h * 64:128 + (h + 1) * 64], ALU.mult, ALU.add)
                    return [s1, s2, s3]
                out_ += mk()
            return out_

        def post_closures(hp):
            cs = slice(hp * 128, (hp + 1) * 128)
            rl, kl, vl = pl[:, hp, :], pl[:, 4 + hp, :], pl[:, 8 + hp, :]
            yraw = YR[hp % 2]
            t1, t2, t3, t4 = POSTT

            sqb_ = t3.bitcast(BF16)[:, 0:512]
            yrb_ = t3.bitcast(BF16)[:, 512:1024]
            t4b_ = t4.bitcast(BF16)[:, 0:512]

            def q1():
                p.act(sqb_, yraw, AF.Square)
                p.copy(yrb_, yraw, eng='pool')
                p.mm(PS[5], ones2b, yrb_)
                p.ts(t2, PS[5], 1.0 / 64, ALU.mult)

            def q1b():
                p.mm(PS[5], ones2b, sqb_)
                p.tt(t3, t2, t2, ALU.mult, eng='pool')
                p.stt(t3, PS[5], 1.0 / 64, t3, ALU.mult, ALU.subtract)
                p.stt(t4b_, rl, col(70 + hp), kl, ALU.mult, ALU.mult)

            def q1c():
                p.mm(PS[5], ones2b, t4b_)
                p.act(t3, t3, AF.Sqrt, bias=64e-5)
                p.tt(t4, PS[5], vl, ALU.mult)
                p.tt(t1, yraw, t2, ALU.subtract, eng='pool')

            def q2():
                p.recip(t3, t3)
                p.mm(PS[5], GUP[:, cs], SG)
                p.tt(t1, t1, t3, ALU.mult)
                p.ts(t1, t1, col(74 + hp), ALU.mult, col(78 + hp), ALU.add)

            def q3():
                p.tt(t1, t1, t4, ALU.add, eng='pool')
                p.tt(ycat[:, hp, :], PS[5], t1, ALU.mult)
            return [q1, q1b, q1c, q2, q3]

        def interleave(fg, bg):
            nf, nb = len(fg), len(bg)
            bi = 0
            for i, f_ in enumerate(fg):
                f_()
                tgt = ((i + 1) * nb) // max(nf, 1)
                while bi < tgt:
                    bg[bi]()
                    bi += 1
            while bi < nb:
                bg[bi]()
                bi += 1

        NU = 2 * nhp
        for f_ in prep_closures(0) + pre_closures(0):
            f_()
        for u in range(NU):
            bg = []
            if u % 2 == 0 and u >= 2:
                bg += post_closures(u // 2 - 1)
            if u + 1 < NU:
                bg += prep_closures(u + 1) + pre_closures(u + 1)
            interleave(serial_closures(u), bg)
        for f_ in post_closures(nhp - 1):
            f_()
        dbg_out("ycat%d" % tt, ycat)
        if 'wout' not in stages:
            return

        def ev_out(mc, ps_):
            p.tt(xT[:, mc, ts_], ps_, xT[:, mc, ts_], ALU.add)
        linear(w_out, w_out_b, D, D, lambda kc: ycat[:, kc, :], ev_out, tt == 0)

    o = 0
    U_ = bfv(o, o + 2176).rearrange("p (c t) -> p c t", t=544); o += 2176
    acc = scr[:, o:o + 8 * 512].rearrange("p (c t) -> p c t", t=512); o += 8 * 512
    SQ2 = o
    vT = bfv(o, o + 2048).rearrange("p (c t) -> p c t", t=512); o += 2048
    gsig = [scr[:, o:o + 512], scr[:, o + 512:o + 1024]]; o += 1024
    cm, cr, ctmp = scr[:, o:o + 512], scr[:, o + 512:o + 1024], scr[:, o + 1024:o + 1536]; o += 1536
    Dg = [bfv(o, o + 1984).rearrange("p (j q) -> p j q", q=128), bfv(o + 1984, o + 3968).rearrange("p (j q) -> p j q", q=128)]
    o += 3968
    assert o <= 17472

    def build_diag(c):
        p.tt(Dg[c % 2], ident.unsqueeze(1).to_broadcast([128, 31, 128]),
             pcol[:, 130 + c * 31:130 + (c + 1) * 31].unsqueeze(2).to_broadcast([128, 31, 128]), ALU.mult,
             eng='pool' if c % 2 == 0 else 'dve')

    def conv_tile(tt):
        ts_ = slice(tt * 512, (tt + 1) * 512)
        if ('conv', tt) not in prenormed:
            rmsnorm_tile(tt, 8)
        if tt == 0:
            p.memset(U_[:, :, 0:30], 0.0, eng='pool')
        else:
            p.copy(U_[:, :, 0:30], chalo, eng='pool')
        build_diag(0)
        build_diag(1)

        def ev_pw1(mc, ps_):
            if mc < 8:
                p.act(acc[:, mc, :], ps_, AF.Identity, bias=col(82 + mc))
            else:
                m = mc - 8
                gb = gsig[m % 2]
                p.act(gb, ps_, AF.Sigmoid, bias=col(82 + mc))
                p.tt(U_[:, m, 30:542], acc[:, m, :], gb, ALU.mult, eng='pool')
        linear(pw1, pw1_b, D, 2048, lambda kc: hT[:, kc, :], ev_pw1, tt == 0)
        p.copy(chalo, U_[:, :, 512:542], eng='pool')
        for c in range(8):
            ps_ = PS[4 + (c % 4)]
            for j in range(31):
                p.mm(ps_, Dg[c % 2][:, j, :], U_[:, c, j:j + 512], start=(j == 0), stop=(j == 30))
            p.act(acc[:, c, :], ps_, AF.Identity, bias=col(98 + c))
            if c + 2 < 8:
                build_diag(c + 2)
        for c in range(8):
            p.mm(PS[6], ones, acc[:, c, :], start=(c == 0), stop=(c == 7))
        for c in range(8):
            sqb = sqr[c % 2]
            p.act(sqb.bitcast(F32R), acc[:, c, :], AF.Square)
            p.mm(PS[7], onesr.bitcast(F32R), sqb.bitcast(F32R), start=(c == 0), stop=(c == 7))
        p.ts(cm, PS[6], 1.0 / D, ALU.mult)
        p.tt(ctmp, cm, cm, ALU.mult, eng='pool')
        p.stt(cr, PS[7], 1.0 / D, ctmp, ALU.mult, ALU.subtract)
        p.act(cr, cr, AF.Sqrt, bias=1e-5)
        p.recip(cr, cr)
        for c in range(8):
            p.tt(acc[:, c, :], acc[:, c, :], cm, ALU.subtract)
            p.tt(acc[:, c, :], acc[:, c, :], cr, ALU.mult, eng='pool')
            p.act(vT[:, c, :], acc[:, c, :], AF.Silu, bias=col(114 + c), scale=col(106 + c))
        dbg_out("vT%d" % tt, vT)

        def ev_pw2(mc, ps_):
            p.stt(xT[:, mc, ts_], ps_, col(122 + mc), xT[:, mc, ts_], ALU.add, ALU.add)
        linear(pw2, pw2_b, D, D, lambda kc: vT[:, kc, :], ev_pw2, tt == 0)

    for tt in range(NT):
        hybrid_tile(tt)
        dbg_out("xmix%d" % tt, xT[:, :, tt * 512:(tt + 1) * 512])
        if 'mlp0' in stages:
            def mid0(tt=tt):
                if tt + 1 < NT:
                    rmsnorm_tile(tt + 1, 0)
                    prenormed.add(('hyb', tt + 1))
                elif 'conv' in stages:
                    rmsnorm_tile(0, 8)
                    prenormed.add(('conv', 0))
            mlp_tile(tt, 0, mid0)
    dbg_out("xl0", xT)
    for tt in range(NT):
        if 'conv' in stages:
            conv_tile(tt)
        if 'mlp1' in stages:
            def mid1(tt=tt):
                if tt + 1 < NT and 'conv' in stages:
                    rmsnorm_tile(tt + 1, 8)
                    prenormed.add(('conv', tt + 1))
            mlp_tile(tt, 1, mid1)
    yo = scr[:, 0:4 * D].rearrange("p (b d) -> p b d", d=D)
    for tt in range(NT):
        rmsnorm_tile(tt, 32, dst=hTf)
        for b in range(4):
            for half in range(2):
                ps_ = PS[4 + ((b * 2 + half) % 4)]
                for cc in range(4):
                    c = half * 4 + cc
                    p.transpose(ps_[:, cc * 128:(cc + 1) * 128], hTf[:, c, b * 128:(b + 1) * 128], ident)
                p.copy(yo[:, b, half * 512:(half + 1) * 512], ps_, eng='dve' if half == 0 else 'act')
        p.dma(out[tt * 512:(tt + 1) * 512, :].rearrange("(b p) d -> p b d", p=128), yo,
              q='sp')
    nc = p.build()
    return nc, p


def host_weights(inp):
    w_in = np.asarray(inp['hy_w_in'][0], np.float32)
    qcols = 1792 + np.concatenate([np.concatenate([np.arange(j * 64, (j + 1) * 64), np.arange((j + 4) * 64, (j + 5) * 64)])
                                   for j in range(4)])
    perm = np.concatenate([np.arange(1792), qcols, np.arange(2304, 2560)])
    w_in_p = np.ascontiguousarray(w_in[:, perm])
    w_out = np.asarray(inp['hy_w_out'][0], np.float32)
    rperm = np.concatenate([np.arange(512), 512 + (qcols - 1792)])
    w_out_p = np.ascontiguousarray(w_out[rperm, :])
    lup = np.ascontiguousarray(np.concatenate([inp['hy_w_up'][0], inp['hy_a_up'][0]], 0).astype(np.float32))
    gup = np.ascontiguousarray(np.asarray(inp['hy_g_up'][0], np.float32))
    sinkb = np.ascontiguousarray(np.broadcast_to(np.asarray(inp['hy_sinks'][0], np.float32)[None, :], (128, 8)))
    return dict(w_in=w_in_p, w_out=w_out_p, pw1=np.ascontiguousarray(inp['cv_pw1_w'][0]),
                pw2=np.ascontiguousarray(inp['cv_pw2_w'][0]), w1=np.ascontiguousarray(inp['mlp_w1']),
                w2=np.ascontiguousarray(inp['mlp_w2']), lup=lup, gup=gup, sinkb=sinkb,
                cst=host_consts(), pcol=host_pcol(inp))


_CACHE = {}


def kernel(**inputs):
    inputs = {k: np.asarray(v) for k, v in inputs.items()}
    x = np.asarray(inputs['x'], np.float32)
    B, T, _ = x.shape
    shared = host_weights(inputs)
    nc, _ = build(T)
    in_maps = [dict(shared, x=np.ascontiguousarray(x[b])) for b in range(B)]
    res = run_bass_kernel_spmd(nc, in_maps, core_ids=list(range(B)))
    return np.stack([np.asarray(r["out"], np.float32) for r in res.results], 0)
```

```python
import numpy as np
from contextlib import ExitStack
import concourse.bass as bass
import concourse.mybir as mybir
from concourse.bass_utils import run_bass_kernel_spmd

F32 = mybir.dt.float32
BF16 = mybir.dt.bfloat16
ALU = mybir.AluOpType
AF = mybir.ActivationFunctionType
AX = mybir.AxisListType

NDMA = 12
BLK = 256
DTSZ = {mybir.dt.float32: 4, mybir.dt.bfloat16: 2, mybir.dt.float32r: 4}


class Prog:
    def __init__(self):
        self.nc = bass.Bass("TRN2", target_bir_lowering=False)
        self.es = ExitStack()
        self.ops = []
        self.tinfo = {}
        self.psum_names = set()
        self.ndma = 0

    def dram(self, name, shape, kind, dtype=F32):
        return self.nc.dram_tensor(name, list(shape), dtype, kind=kind).ap()

    def sb(self, name, shape, dtype=F32):
        t = self.es.enter_context(self.nc.sbuf_tensor(name, list(shape), dtype))
        ap = t[:]
        self.tinfo[ap.name] = int(np.prod(shape[1:])) * DTSZ[dtype]
        return ap

    def ps(self, name, shape, dtype=F32):
        t = self.es.enter_context(self.nc.psum_tensor(name, list(shape), dtype))
        ap = t[:]
        self.tinfo[ap.name] = int(np.prod(shape[1:]))
        self.psum_names.add(ap.name)
        return ap

    def blocks(self, ap):
        name = ap.name
        if name not in self.tinfo:
            return []
        if name in self.psum_names:
            return [(name, 0)]
        fs = self.tinfo[name]
        sz = DTSZ[ap.dtype]
        off = (int(ap.offset) * sz) % fs
        ext = 1
        for (st, cn) in ap.ap[1:]:
            ext += (cn - 1) * abs(st)
        ext *= sz
        b0 = off // BLK
        b1 = (off + ext - 1) // BLK
        return [(name, b) for b in range(b0, b1 + 1)]

    @staticmethod
    def _nfree(ap):
        n = 1
        for (st, cn) in ap.ap[1:]:
            n *= cn
        return n

    def _dur(self, eng, reads, writes, dma):
        n = self._nfree(writes[0]) if writes else 1
        if dma:
            return 0.1
        if eng == 'pe':
            return 0.06
        if eng == 'dve':
            return 0.08 + n / 960.0
        if eng == 'act':
            return 0.2 + n / 1200.0
        if eng == 'pool':
            return 0.3 + n / 480.0
        return 0.1

    def op(self, eng, fn, reads, writes, dma=False, rkeys=(), wkeys=()):
        rb = list(rkeys)
        for a in reads:
            if a is not None and not isinstance(a, (int, float)):
                rb += self.blocks(a)
        wb = list(wkeys)
        for a in writes:
            wb += self.blocks(a)
        wb += [b for b in rb if b[0] in self.psum_names]
        self.ops.append(dict(eng=eng, fn=fn, rb=rb, wb=wb, dma=dma, dur=self._dur(eng, reads, writes, dma)))

    @staticmethod
    def _cls(n):
        return 32 if n <= 32 else (64 if n <= 64 else 128)

    def _pemode(self, lhsT, tr):
        m = 1
        for (st, cn) in lhsT.ap[1:]:
            m *= cn
        return (int(lhsT.base_partition()), self._cls(int(lhsT.partition_size())), self._cls(m), tr)

    def mm(self, out, lhsT, rhs, start=True, stop=True):
        self.op('pe', lambda e: e.matmul(out, lhsT, rhs, start=start, stop=stop),
                [lhsT, rhs], [out])
        self.ops[-1]['pemode'] = (out.name, self._pemode(lhsT, False))
        n = self._nfree(rhs)
        base = max(0.055, n / 2200.0)
        dt_ = lhsT.dtype
        self.ops[-1]['dur'] = base * (4.0 if dt_ == mybir.dt.float32 else (2.0 if dt_ == mybir.dt.float32r else 1.0))

    def transpose(self, out, in_, ident):
        self.op('pe', lambda e: e.transpose(out, in_, ident), [in_, ident], [out])
        self.ops[-1]['pemode'] = (out.name, self._pemode(in_, True))
        self.ops[-1]['dur'] = 0.12

    def act(self, out, in_, func, bias=0.0, scale=1.0, eng='act'):
        self.op(eng, lambda e: e.activation(out, in_, func, bias=bias, scale=scale),
                [in_, bias, scale], [out])

    def tt(self, out, in0, in1, op, eng='dve'):
        self.op(eng, lambda e: e.tensor_tensor(out, in0, in1, op), [in0, in1], [out])

    def ts(self, out, in0, s1, op0, s2=None, op1=None, eng='dve'):
        if op1 is None:
            self.op(eng, lambda e: e.tensor_scalar(out, in0, s1, None, op0), [in0, s1], [out])
        else:
            self.op(eng, lambda e: e.tensor_scalar(out, in0, s1, s2, op0, op1), [in0, s1, s2], [out])

    def stt(self, out, in0, scalar, in1, op0, op1, eng='dve'):
        self.op(eng, lambda e: e.scalar_tensor_tensor(out, in0, scalar, in1, op0, op1),
                [in0, scalar, in1], [out])

    def copy(self, out, in_, eng='dve'):
        if eng == 'act':
            self.op(eng, lambda e: e.copy(out, in_), [in_], [out])
        else:
            self.op(eng, lambda e: e.tensor_copy(out, in_), [in_], [out])

    def memset(self, out, val, eng='dve'):
        self.op(eng, lambda e: e.memset(out, val), [], [out])

    def recip(self, out, in_):
        self.op('dve', lambda e: e.reciprocal(out, in_), [in_], [out])
        self.ops[-1]['dur'] = 0.1 + self._nfree(out) / 155.0

    def scan(self, out, d0, d1, init, op0, op1):
        self.op('dve', lambda e: e.tensor_tensor_scan(out, d0, d1, init, op0, op1), [d0, d1, init], [out])

    def dma(self, out, in_, q='sp', slow=False, rkeys=(), wkeys=()):
        if slow:
            self.op(q, lambda e: e.dma_start(out=out, in_=in_, allow_slow_non_contiguous=True), [in_], [out], dma=True,
                    rkeys=rkeys, wkeys=wkeys)
        else:
            self.op(q, lambda e: e.dma_start(out=out, in_=in_), [in_], [out], dma=True, rkeys=rkeys, wkeys=wkeys)

    def reschedule(self, window=256, lat=0.25, dma_lat=3.0, slack=0.2):
        ops = self.ops
        n = len(ops)
        last_w, readers = {}, {}
        preds = [None] * n
        for i, o in enumerate(ops):
            d = set()
            for b in o['rb']:
                w = last_w.get(b)
                if w is not None:
                    d.add(w)
            for b in o['wb']:
                w = last_w.get(b)
                if w is not None:
                    d.add(w)
                d.update(readers.get(b, ()))
            d.discard(i)
            preds[i] = d
            for b in o['rb']:
                readers.setdefault(b, []).append(i)
            for b in o['wb']:
                last_w[b] = i
                readers[b] = []
        queues = {}
        for i, o in enumerate(ops):
            queues.setdefault(o['eng'], []).append(i)
        dmas = [i for i, o in enumerate(ops) if o['dma']]
        for k in range(NDMA, len(dmas)):
            preds[dmas[k]].add(dmas[k - NDMA])
        done = [None] * n
        start = [None] * n
        free = {e: 0.0 for e in queues}
        head = {e: 0 for e in queues}
        issued = [False] * n
        remaining = n
        INF = 1e30

        def ready_time(i):
            t = 0.0
            for p_ in preds[i]:
                if not issued[p_]:
                    return INF
                dp = done[p_] + (0.0 if ops[p_]['eng'] == ops[i]['eng'] and not ops[p_]['dma'] else lat)
                if dp > t:
                    t = dp
            return t
        while remaining:
            best = None
            for e, q in queues.items():
                h = head[e]
                while h < len(q) and issued[q[h]]:
                    h += 1
                head[e] = h
                if h >= len(q):
                    continue
                hi = q[h]
                hr = ready_time(hi)
                hstart = max(free[e], hr) if hr < INF else INF
                cand = (hstart, hi)
                if hstart > free[e] + 1e-9:
                    cnt = 0
                    j = h + 1
                    while j < len(q) and cnt < window:
                        oj = q[j]
                        j += 1
                        if issued[oj]:
                            continue
                        cnt += 1
                        if ops[oj]['dma']:
                            continue
                        r = ready_time(oj)
                        if r >= INF:
                            continue
                        s = max(free[e], r)
                        if s + ops[oj]['dur'] <= hstart + slack + 1e-9 and s < cand[0]:
                            cand = (s, oj)
                if cand[0] < INF and (best is None or cand < best[0:2]):
                    best = (cand[0], cand[1], e)
            assert best is not None, "scheduler deadlock"
            s, i, e = best
            o = ops[i]
            start[i] = s
            issued[i] = True
            remaining -= 1
            free[e] = s + o['dur']
            done[i] = s + o['dur'] + (dma_lat if o['dma'] else 0.0)
        order = sorted(range(n), key=lambda i: (start[i], i))
        self.ops = [ops[i] for i in order]
        self.sim_time = max(done)

    def build(self, resched=True):
        nc = self.nc
        if resched:
            self.reschedule()
        ENG = ['pe', 'dve', 'act', 'pool', 'sp']
        engobj = {'pe': nc.tensor, 'dve': nc.vector, 'act': nc.scalar, 'pool': nc.gpsimd, 'sp': nc.sync}
        last_w = {}
        readers = {}
        bank_mode = {}
        known = {e: {} for e in ENG}
        eidx = {e: 0 for e in ENG}
        slot_last = [None] * NDMA
        slot_cnt = [0] * NDMA
        ndma = 0
        ops = self.ops
        for i, o in enumerate(ops):
            e = o['eng']
            deps = set()
            for b in o['rb']:
                w = last_w.get(b)
                if w is not None:
                    deps.add(w)
            for b in o['wb']:
                w = last_w.get(b)
                if w is not None:
                    deps.add(w)
                for r in readers.get(b, ()):
                    deps.add(r)
            forced = set()
            if 'pemode' in o:
                bank, mode = o['pemode']
                lm = bank_mode.get(bank)
                if lm is not None and lm[0] != mode:
                    deps.add(lm[1])
                    forced.add(lm[1])
                bank_mode[bank] = (mode, i)
            if o['dma']:
                k = ndma % NDMA
                ndma += 1
                o['slot'] = k
                if slot_last[k] is not None:
                    deps.add(slot_last[k])
                slot_cnt[k] += 1
                o['dval'] = 16 * slot_cnt[k]
                slot_last[k] = i
                o['key'] = ('dma', k)
                o['kidx'] = slot_cnt[k]
            else:
                eidx[e] += 1
                o['key'] = ('eng', e)
                o['kidx'] = eidx[e]
            waits = []
            vc = {}
            kn = known[e]
            for d in sorted(deps, reverse=True):
                if d == i:
                    continue
                p = ops[d]
                if p['key'] == ('eng', 'pe') and e == 'pe' and not o['dma'] and d not in forced:
                    continue
                if kn.get(p['key'], 0) >= p['kidx']:
                    continue
                waits.append(d)
                p['signal'] = True
                for k2, v2 in p['vc'].items():
                    if kn.get(k2, 0) < v2:
                        kn[k2] = v2
            o['waits'] = waits
            vc = dict(kn)
            vc[o['key']] = o['kidx']
            o['vc'] = vc
            o.setdefault('signal', False)
            if o['dma']:
                o['signal'] = True
            for b in o['rb']:
                readers.setdefault(b, []).append(i)
            for b in o['wb']:
                last_w[b] = i
                readers[b] = []
        self.final_dma = [(k, 16 * slot_cnt[k]) for k in range(NDMA) if slot_cnt[k]]
        sems = {e: self.es.enter_context(nc.semaphore("s_" + e)) for e in ENG}
        dsems = [self.es.enter_context(nc.semaphore("d_%d" % k)) for k in range(NDMA)]
        sig = {e: 0 for e in ENG}
        for o in ops:
            if o['dma']:
                o['sem'] = dsems[o['slot']]
                o['sval'] = o['dval']
            elif o['signal']:
                sig[o['eng']] += 1
                o['sem'] = sems[o['eng']]
                o['sval'] = sig[o['eng']]
        self.nwaits = sum(len(o['waits']) for o in ops)
        self.nsig = dict(sig)
        per = {e: [o for o in ops if o['eng'] == e] for e in ENG}
        block = self.es.enter_context(nc.Block())

        def emit(eng_name):
            def f(eo):
                for o in per[eng_name]:
                    for d in o['waits']:
                        p = ops[d]
                        eo.wait_ge(p['sem'], p['sval'])
                    ins = o['fn'](eo)
                    if o['dma']:
                        ins.then_inc(o['sem'], 16)
                    elif o['signal']:
                        ins.then_inc(o['sem'], 1)
                if eng_name == 'sp':
                    for k, v in self.final_dma:
                        eo.wait_ge(dsems[k], v)
            return f
        block.tensor(emit('pe'))
        block.vector(emit('dve'))
        block.scalar(emit('act'))
        block.gpsimd(emit('pool'))
        block.sync(emit('sp'))
        self.es.close()
        return nc


D = 1024
C0 = float(np.exp(-0.5))
NPC = 384
F32R = mybir.dt.float32r
RDT = BF16
O_ID, O_ONES2, O_MG, O_MNT, O_ID2, O_D0, O_BC, O_BP, O_SK, NCST = 0, 128, 256, 384, 512, 640, 1152, 2176, 3200, 3208


def host_consts():
    c = np.zeros((128, NCST), np.float32)
    c[:, O_ID:O_ID + 128] = np.eye(128)
    c[0:64, O_ONES2:O_ONES2 + 64] = 1.0
    c[64:128, O_ONES2 + 64:O_ONES2 + 128] = 1.0
    j = np.arange(64)[:, None]
    i = np.arange(64)[None, :]
    lt = (j < i).astype(np.float32)
    le = (j <= i).astype(np.float32)
    mg = np.zeros((128, 128), np.float32)
    mg[0:64, 0:64] = lt
    mg[0:64, 64:128] = le
    mg[64:128, 0:64] = lt
    mg[64:128, 64:128] = le
    c[:, O_MG:O_MG + 128] = mg
    c[0:64, O_MNT:O_MNT + 64] = (i.T > j.T).astype(np.float32).T * 0 + (np.arange(64)[:, None] > np.arange(64)[None, :])
    c[0:64, O_MNT + 64:O_MNT + 128] = c[0:64, O_MNT:O_MNT + 64]
    c[0:64, O_ID2:O_ID2 + 64] = np.eye(64)
    c[0:64, O_ID2 + 64:O_ID2 + 128] = np.eye(64)
    d0 = np.ones((512,), np.float32)
    d0[0::64] = 0.0
    c[:, O_D0:O_D0 + 512] = d0[None, :]
    key = np.arange(128)[:, None]
    q = np.arange(128)[None, :]
    for h in range(8):
        slope = 2.0 ** (-(h + 1))
        cur = np.where(q >= key, -slope * (q - key), -30000.0)
        prv = np.where(key > q, -slope * (q + 128 - key), -30000.0)
        c[:, O_BC + h * 128:O_BC + (h + 1) * 128] = cur
        c[:, O_BP + h * 128:O_BP + (h + 1) * 128] = prv
    return c


def host_pcol(inp):
    pc = np.zeros((128, NPC), np.float32)

    def put(col, vec):
        v = np.asarray(vec, np.float32).reshape(-1, 128)
        pc[:, col:col + v.shape[0]] = v.T
    put(0, inp['norm_mix_g'])
    put(16, inp['norm_ffn_g'])
    put(32, inp['final_norm_g'])
    put(40, inp['hy_mu'][0])
    put(54, inp['hy_w0'][0])
    put(58, inp['hy_a0'][0])
    put(62, inp['hy_k_k'][0])
    put(66, inp['hy_k_a'][0])
    put(70, inp['hy_r_k'][0])
    put(74, inp['hy_gn_g'][0])
    put(78, inp['hy_gn_b'][0])
    put(82, inp['cv_pw1_b'][0])
    put(98, inp['cv_dw_b'][0])
    put(106, inp['cv_ln_g'][0])
    put(114, inp['cv_ln_b'][0])
    put(122, inp['cv_pw2_b'][0])
    dw = np.asarray(inp['cv_dw_w'][0], np.float32)
    pc[:, 130:130 + 248] = dw.T.reshape(8, 128, 31).transpose(1, 0, 2).reshape(128, 248)
    return pc


def build(T=2048, dbg=(), stages=('att', 'rwkv', 'wout', 'mlp0', 'conv', 'mlp1'), nchunks=8, nhp=4, cut=99):
    p = Prog()
    NT = T // 512
    x = p.dram("x", [T, D], "ExternalInput")
    out = p.dram("out", [T, D], "ExternalOutput")
    w_in = p.dram("w_in", [D, 2560], "ExternalInput")
    w_out = p.dram("w_out", [D, D], "ExternalInput")
    pw1 = p.dram("pw1", [D, 2048], "ExternalInput")
    pw2 = p.dram("pw2", [D, D], "ExternalInput")
    w1 = p.dram("w1", [2, D, 4096], "ExternalInput")
    w2 = p.dram("w2", [2, 4096, D], "ExternalInput")
    lupd = p.dram("lup", [128, 512], "ExternalInput")
    gupd = p.dram("gup", [128, 512], "ExternalInput")
    cstd = p.dram("cst", [128, NCST], "ExternalInput")
    pcold = p.dram("pcol", [128, NPC], "ExternalInput")
    sinkd = p.dram("sinkb", [128, 8], "ExternalInput")
    dbg_outs = {}
    def wscr(name, K, M, lead=None):
        nt = (K // 256) * (M // 512)
        return p.dram(name, ([lead] if lead else []) + [nt, 128, 1024], "Internal", dtype=BF16)
    w_in_b = wscr("w_in_b", D, 2560)
    w_out_b = wscr("w_out_b", D, D)
    pw1_b = wscr("pw1_b", D, 2048)
    pw2_b = wscr("pw2_b", D, D)
    w1_b = wscr("w1_b", D, 4096, 2)
    w2_b = wscr("w2_b", 4096, D, 2)

    def dbg_out(name, ap):
        if name in dbg:
            shp = list(ap.shape)
            dt_ = p.dram("dbg_" + name, shp, "ExternalOutput", dtype=ap.dtype)
            p.dma(dt_, ap)

    cst = p.sb("cstsb", [128, NCST])
    pcol = p.sb("pcolsb", [128, NPC])
    ones = p.sb("ones", [128, 128])
    onesr = p.sb("onesr", [128, 128])
    sqr = [p.sb("sqr%d" % i, [128, 512]) for i in range(2)]
    esink = p.sb("esink", [128, 8])
    xT = p.sb("xT", [128, 8, T])
    wst = [p.sb("wst%d" % i, [128, 2, 512]) for i in range(2)]
    NWB = 4
    wbf = [p.sb("wbf%d" % i, [128, 2, 512], dtype=BF16) for i in range(NWB)]
    LUP = p.sb("LUP", [128, 512], dtype=BF16)
    GUP = p.sb("GUP", [128, 512], dtype=BF16)
    ones2b = p.sb("ones2b", [128, 128], dtype=BF16)
    Hbd = p.sb("Hbd", [128, 4, 128])
    UV = p.sb("UV", [128, 2, 128])
    ZV = p.sb("ZV", [128, 2, 64])
    Xn = [p.sb("Xn%d" % i, [64, 4, 128], dtype=RDT) for i in range(2)]
    Xtn = [p.sb("Xtn%d" % i, [64, 4, 128], dtype=RDT) for i in range(2)]
    Psh = [p.sb("Psh%d" % i, [64, 4, 128], dtype=RDT) for i in range(2)]
    Pa = p.sb("Pa", [64, 4, 128])
    Pb = [p.sb("Pb%d" % i, [64, 4, 128]) for i in range(2)]
    Zs = p.sb("Zs", [64, 128])
    BKhat = p.sb("BKhat", [128, 128])
    M2b = [p.sb("M2b%d" % i, [128, 4, 2, 128]) for i in range(2)]
    WC2 = [p.sb("WC2_%d" % i, [128, 4]) for i in range(2)]
    ST2 = [p.sb("ST2_%d" % i, [128, 4]) for i in range(2)]
    carry = p.sb("carry", [128, 16])
    kT = p.sb("kT", [128, 640], dtype=BF16)
    vTM = p.sb("vTM", [128, 5, 128], dtype=BF16)
    onesb = p.sb("onesb", [128, 128], dtype=BF16)
    st1 = p.sb("st1", [128, 512])
    NSCR = 19520
    scr = p.sb("scr", [128, NSCR])

    def bfv(a, b):
        return scr[:, a:b].bitcast(BF16)
    hT = bfv(17472, 17472 + 2048).rearrange("p (c t) -> p c t", t=512)
    hTf = scr[:, 8192:8192 + 4096].rearrange("p (c t) -> p c t", t=512)
    chalo = p.sb("chalo", [128, 8, 30], dtype=BF16)
    PS = [p.ps("ps%d" % i, [128, 512]) for i in range(8)]

    ident = cst[:, O_ID:O_ID + 128]
    ones2 = cst[:, O_ONES2:O_ONES2 + 128]
    maskG = cst[:, O_MG:O_MG + 128]
    maskNt2 = cst[0:64, O_MNT:O_MNT + 128]
    ident2 = cst[0:64, O_ID2:O_ID2 + 128]
    d0 = cst[:, O_D0:O_D0 + 512]
    Bcur = cst[:, O_BC:O_BC + 1024].rearrange("p (h q) -> p h q", q=128)
    Bprev = cst[:, O_BP:O_BP + 1024].rearrange("p (h q) -> p h q", q=128)

    def col(i):
        return pcol[:, i:i + 1]

    p.dma(cst, cstd)
    p.dma(pcol, pcold)
    p.dma(esink, sinkd)
    p.memset(ones, 1.0)
    p.copy(onesb, ones)
    p.copy(onesr.bitcast(F32R), ones)
    p.memset(Hbd, 0.0, eng='pool')
    p.memset(UV, 0.0, eng='pool')
    p.memset(ZV, 0.0, eng='pool')
    p.memset(carry, 0.0, eng='pool')
    p.act(esink, esink, AF.Exp)
    p.ts(pcol[:, 378:382], pcol[:, 66:70], -1.0, ALU.mult, 1.0, ALU.add)

    p.dma(scr[:, 8192:8704], lupd)
    p.dma(scr[:, 8704:9216], gupd)
    p.copy(LUP, scr[:, 8192:8704])
    p.copy(GUP, scr[:, 8704:9216], eng='act')
    p.copy(ones2b, ones2)
    xin_region = scr[:, 0:4 * D].rearrange("p (b d) -> p b d", d=D)
    for tt in range(NT):
        p.dma(xin_region, x[tt * 512:(tt + 1) * 512, :].rearrange("(b p) d -> p b d", p=128),
              q='sp')
        for c in range(8):
            ps_ = PS[4 + (c % 4)]
            for b in range(4):
                p.transpose(ps_[:, b * 128:(b + 1) * 128], xin_region[:, b, c * 128:(c + 1) * 128], ident)
            p.copy(xT[:, c, tt * 512:(tt + 1) * 512], ps_, eng='dve' if c % 2 == 0 else 'act')

    lin_ctr = [0, 0]
    pend_store = []

    def linear(W, Wb, K, M, rhs_fn, evac_fn, first, hook=None):
        nkg = K // 256
        nmg = (M + 511) // 512
        for mg in range(nmg):
            mw = min(512, M - mg * 512)
            nm = mw // 128
            banks = PS[0:4] if lin_ctr[0] % 2 == 0 else PS[4:8]
            lin_ctr[0] += 1
            for kg in range(nkg):
                wt = wbf[lin_ctr[1] % NWB]
                q = 'sp'
                key = (Wb.name, int(Wb.offset), kg, mg)
                assert mw == 512
                wsrc = Wb[mg * nkg + kg].rearrange("p (kc m) -> p kc m", kc=2)
                if first:
                    ws = wst[lin_ctr[1] % 2]
                    ce = 'dve' if lin_ctr[1] % 2 == 0 else 'act'
                    p.dma(ws[:, :, 0:mw],
                          W[kg * 256:(kg + 1) * 256, mg * 512:mg * 512 + mw].rearrange("(kc p) m -> p kc m", p=128), q=q)
                    p.copy(wt[:, :, 0:mw], ws[:, :, 0:mw], eng=ce)
                    if pend_store:
                        pend_store.pop()()
                    pend_store.append(lambda wsrc=wsrc, wt=wt, mw=mw, key=key: p.dma(wsrc, wt[:, :, 0:mw], q='sp', wkeys=[key]))
                else:
                    p.dma(wt[:, :, 0:mw], wsrc, q=q, rkeys=[key])
                lin_ctr[1] += 1
                for kc in range(2):
                    for m in range(nm):
                        p.mm(banks[m], wt[:, kc, m * 128:(m + 1) * 128], rhs_fn(kg * 2 + kc),
                             start=(kg == 0 and kc == 0), stop=(kg == nkg - 1 and kc == 1))
            for m in range(nm):
                evac_fn(mg * 4 + m, banks[m])
            if hook is not None and mg == 0:
                hook()
        if pend_store:
            pend_store.pop()()

    def rmsnorm_tile(tt, gcol0, dst=None):
        dst = hT if dst is None else dst
        ts_ = slice(tt * 512, (tt + 1) * 512)
        for c in range(8):
            sqb = sqr[c % 2]
            p.act(sqb.bitcast(F32R), xT[:, c, ts_], AF.Square)
            p.mm(PS[7], onesr.bitcast(F32R), sqb.bitcast(F32R), start=(c == 0), stop=(c == 7))
        p.act(st1, PS[7], AF.Sqrt, bias=1e-6, scale=1.0 / D)
        p.recip(st1, st1)
        for c in range(8):
            p.stt(dst[:, c, :], xT[:, c, ts_], col(gcol0 + c), st1, ALU.mult, ALU.mult)

    prenormed = set()

    def mlp_tile(tt, layer, mid=None):
        ts_ = slice(tt * 512, (tt + 1) * 512)
        rmsnorm_tile(tt, 16 + layer * 8)
        h1 = bfv(0, 8192).rearrange("p (c t) -> p c t", t=512)

        def ev1(mc, ps_):
            p.act(h1[:, mc, :], ps_, AF.Relu)
            p.tt(h1[:, mc, :], h1[:, mc, :], h1[:, mc, :], ALU.mult, eng='pool')
        linear(w1[layer], w1_b[layer], D, 4096, lambda kc: hT[:, kc, :], ev1, tt == 0)

        def ev2(mc, ps_):
            p.tt(xT[:, mc, ts_], ps_, xT[:, mc, ts_], ALU.add)
        linear(w2[layer], w2_b[layer], 4096, D, lambda kc: h1[:, kc, :], ev2, tt == 0, hook=mid)

    o = 0
    pl = scr[:, o:o + 14 * 512].rearrange("p (c t) -> p c t", t=512); o += 14 * 512
    ycat = bfv(o, o + 2048).rearrange("p (c t) -> p c t", t=512); o += 2048
    prawb = [scr[:, o:o + 513], scr[:, o + 544:o + 544 + 513]]; o += 1088
    TL = bfv(o, o + 256); o += 512
    SG = bfv(o, o + 256); o += 512
    SLOT0 = o
    assert SLOT0 + 12 * 512 == 17472, SLOT0
    slot = [scr[:, o + i * 512:o + (i + 1) * 512] for i in range(16)]
    qT = bfv(o, o + 1024).rearrange("p (c t) -> p c t", t=512)
    vattT = slot[4]
    sTa, sTb, denb = slot[5], slot[6], slot[9]
    eTa = bfv(SLOT0 + 7 * 512, SLOT0 + 7 * 512 + 256)
    eTb = bfv(SLOT0 + 8 * 512, SLOT0 + 8 * 512 + 256)
    o += 16 * 512
    assert o <= NSCR

    def hybrid_tile(tt):
        ts_ = slice(tt * 512, (tt + 1) * 512)
        if ('hyb', tt) not in prenormed:
            rmsnorm_tile(tt, 0)

        def ev_in(mc, ps_):
            if mc < 14:
                pb = prawb[mc % 2]
                p.copy(pb[:, 0:1], carry[:, mc:mc + 1], eng='pool')
                p.copy(pb[:, 1:513], ps_, eng='act')
                p.copy(carry[:, mc:mc + 1], pb[:, 512:513], eng='pool')
                p.tt(pl[:, mc, :], pb[:, 0:512], pb[:, 1:513], ALU.subtract)
                p.stt(pl[:, mc, :], pl[:, mc, :], col(40 + mc), pb[:, 1:513], ALU.mult, ALU.add)
            elif mc < 18:
                p.copy(qT[:, mc - 14, :], ps_, eng='act')
            elif mc == 18:
                p.copy(kT[:, 128:640], ps_, eng='act')
            else:
                p.copy(vattT, ps_, eng='act')
        linear(w_in, w_in_b, D, 2560, lambda kc: hT[:, kc, :], ev_in, tt == 0)

        if 'att' not in stages:
            return
        for b in range(4):
            p.transpose(PS[6][:, b * 128:(b + 1) * 128], vattT[:, b * 128:(b + 1) * 128], ident)
        p.copy(vTM[:, 1:5, :], PS[6].rearrange("p (b d) -> p b d", d=128))
        for blk in range(4):
            first = (tt == 0 and blk == 0)
            for g in range(2):
                rows = slice(g * 64, (g + 1) * 64)
                pc_, pp_ = (PS[1], PS[0]) if g == 0 else (PS[3], PS[2])
                qv = qT[rows, :, blk * 128:(blk + 1) * 128]
                p.mm(pc_, kT[rows, 128 + blk * 128:256 + blk * 128], qv)
                if not first:
                    p.mm(pp_, kT[rows, blk * 128:128 + blk * 128], qv)
                p.stt(sTa.rearrange("p (h q) -> p h q", q=128), pc_.rearrange("p (h q) -> p h q", q=128),
                      0.125, Bcur[:, 4 * g:4 * g + 4, :], ALU.mult, ALU.add)
                p.act(eTa, sTa, AF.Exp)
                if not first:
                    p.stt(sTb.rearrange("p (h q) -> p h q", q=128), pp_.rearrange("p (h q) -> p h q", q=128),
                          0.125, Bprev[:, 4 * g:4 * g + 4, :], ALU.mult, ALU.add)
                    p.act(eTb, sTb, AF.Exp)
                p.mm(PS[4], vTM[:, blk + 1, :], eTa, start=True, stop=first)
                if not first:
                    p.mm(PS[4], vTM[:, blk, :], eTb, start=False, stop=True)
                p.mm(PS[5], onesb, eTa, start=True, stop=first)
                if not first:
                    p.mm(PS[5], onesb, eTb, start=False, stop=True)
                den3 = denb.rearrange("p (h q) -> p h q", q=128)
                p.tt(den3, PS[5].rearrange("p (h q) -> p h q", q=128),
                     esink[:, 4 * g:4 * g + 4].unsqueeze(2).to_broadcast([128, 4, 128]), ALU.add)
                p.recip(denb, denb)
                p.tt(ycat[rows, 4:8, blk * 128:(blk + 1) * 128],
                     PS[4][rows, :].rearrange("p (h q) -> p h q", q=128), den3[rows], ALU.mult)
        p.copy(kT[:, 0:128], kT[:, 512:640], eng='pool')
        p.copy(vTM[:, 0, :], vTM[:, 4, :], eng='pool')

        if 'rwkv' not in stages:
            dbg_out("ycat%d" % tt, ycat)
            return
        p.copy(TL[64:128, :], pl[64:128, 12, :], eng='pool')
        p.act(TL[0:64, :], pl[0:64, 12, :], AF.Tanh)
        p.act(SG, pl[:, 13, :], AF.Sigmoid)

        def sl(i, n=512):
            b8 = SLOT0 + i * 512
            return scr[:, b8:b8 + n]

        def bigv(i):
            a_ = sl(i)
            return (a_.rearrange("p (c two t) -> p c two t", two=2, t=64), a_.rearrange("p (c n) -> p c n", n=128))
        BIG = [[bigv(s_ * 4 + k_) for k_ in range(4)] for s_ in range(2)]
        PT = [scr[:, SLOT0 + 8 * 512 + i * 256:SLOT0 + 8 * 512 + (i + 1) * 256] for i in range(8)]
        YR = [sl(12), sl(13)]
        POSTT = [sl(14), sl(15), prawb[0][:, 0:512], prawb[1][:, 0:512]]

        def v3(a):
            return a.rearrange("p (c t) -> p c t", t=64)

        def prep_closures(u):
            hp, half = u // 2, u % 2
            par = u % 2
            cs = slice(hp * 128, (hp + 1) * 128)
            tok = slice(half * 256, (half + 1) * 256)
            rl, kl, vl = pl[:, hp, tok], pl[:, 4 + hp, tok], pl[:, 8 + hp, tok]
            sS, sEe, sEi, sEni, sA, sKK, sT, sB = PT
            sTb = sT.bitcast(BF16)[:, 0:256]
            (AR4, AR3), (BK4, BK3), (BKh4, BKh3), (XV4, XV3) = BIG[par]
            WCu, STu = WC2[par], ST2[par]
            d0h = d0[:, 0:256]

            def c1():
                p.mm(PS[1][:, 0:256], LUP[0:64, cs], TL[0:64, tok])
                p.act(sEe, PS[1][:, 0:256], AF.Sigmoid, bias=col(54 + hp))
                p.scan(sS, d0h, sEe, 0.0, ALU.mult, ALU.add)
                p.tt(sEe, sS, sEe, ALU.subtract)
                p.act(sEe, sEe, AF.Exp, scale=-C0)

            def c2():
                p.act(sEi, sS, AF.Exp, scale=-C0)
                p.act(sEni, sS, AF.Exp, scale=C0)
                p.copy(STu, v3(sS)[:, :, 63], eng='pool')
                p.act(WCu, STu, AF.Exp, scale=-C0)
                p.tt(v3(sS), STu.unsqueeze(2).to_broadcast([128, 4, 64]), v3(sS), ALU.subtract, eng='pool')
                p.act(sS, sS, AF.Exp, scale=-C0)

            def c3():
                p.ts(sKK, kl, col(62 + hp), ALU.mult)
                p.act(sTb, sKK, AF.Square)
                p.mm(PS[3][:, 0:256], LUP[64:128, cs], TL[64:128, tok])
                p.act(sA, PS[3][:, 0:256], AF.Sigmoid, bias=col(58 + hp))

            def c3b():
                p.mm(PS[5][:, 0:256], ones2b, sTb)
                p.act(sT, PS[5][:, 0:256], AF.Sqrt)

            def c4():
                p.ts(sT, sT, 1e-12, ALU.max)
                p.recip(sT, sT)
                p.tt(sKK, sKK, sT, ALU.mult)
                p.ts(sT, sA, col(66 + hp), ALU.mult, col(378 + hp), ALU.add)
                p.tt(kl, kl, sT, ALU.mult, eng='pool')
                p.tt(sB, sKK, sA, ALU.mult, eng='pool')

            def c5():
                p.stt(AR4[:, :, 0, :], v3(sKK), -1.0, v3(sEe), ALU.mult, ALU.mult)
                p.tt(AR4[:, :, 1, :], v3(rl), v3(sEi), ALU.mult, eng='pool')
                p.tt(BK4[:, :, 0, :], v3(sB), v3(sEni), ALU.mult)
                p.tt(BK4[:, :, 1, :], v3(kl), v3(sEni), ALU.mult, eng='pool')

            def c6():
                p.tt(BKh4[:, :, 0, :], v3(sB), v3(sS), ALU.mult)
                p.tt(BKh4[:, :, 1, :], v3(kl), v3(sS), ALU.mult, eng='pool')
                p.memset(XV4[:, :, 0, :], 0.0, eng='pool')
                p.copy(XV4[:, :, 1, :], v3(vl), eng='pool')
            return [c1, c2, c3, c3b, c4, c5, c6]

        def pre_closures(u):
            par = u % 2
            (AR4, AR3), (BK4, BK3), _, _ = BIG[par]
            M2h = M2b[par]
            Pfin = Pb[par]
            st = []

            def stageA():
                for ci in range(4):
                    for h in range(2):
                        rows = slice(h * 64, (h + 1) * 64)
                        bank = PS[h * 2 + ci // 2]
                        o_ = (ci % 2) * 256
                        p.mm(bank[:, o_:o_ + 128], BK3[rows, ci, :], AR3[rows, ci, :])
                        p.mm(bank[:, o_ + 128:o_ + 192], AR3[rows, ci, :], BK4[rows, ci, 0, :])
                for h in range(2):
                    for pr in range(2):
                        bank = PS[h * 2 + pr].rearrange("p (c n) -> p c n", n=256)
                        p.tt(M2h[:, pr * 2:pr * 2 + 2, h, :], bank[:, :, 0:128],
                             maskG.unsqueeze(1).to_broadcast([128, 2, 128]), ALU.mult)
                        p.tt(Xtn[0][:, pr * 2:pr * 2 + 2, h * 64:(h + 1) * 64], bank[0:64, :, 128:192],
                             maskNt2[:, 0:64].unsqueeze(1).to_broadcast([64, 2, 64]), ALU.mult)
                p.tt(Pa.rearrange("p c (h t) -> p (c h) t", t=64),
                     M2h[0:64].rearrange("p c h n -> p (c h) n")[:, :, 0:64],
                     ident2[:, 0:64].unsqueeze(1).to_broadcast([64, 8, 64]), ALU.add, eng='pool')
                p.copy(Xn[0].rearrange("p c (h t) -> p (c h) t", t=64),
                       M2h[0:64].rearrange("p c h n -> p (c h) n")[:, :, 0:64], eng='pool')
                p.copy(Psh[0].rearrange("p c n -> p (c n)"), Pa.rearrange("p c n -> p (c n)"), eng='act')
            st.append(stageA)

            def mk_round(k, sb):
                cis = (0, 1) if sb == 0 else (2, 3)
                bX, bXt = (PS[0], PS[1]) if sb == 0 else (PS[2], PS[3])
                c0 = cis[0]

                def sq():
                    nx = (k + 1) % 2
                    for ci in cis:
                        for h in range(2):
                            hs = slice(h * 64, (h + 1) * 64)
                            xc = Xn[k % 2][:, ci, hs]
                            xtc = Xtn[k % 2][:, ci, hs]
                            uo = ((ci - c0) * 2 + h) * 64
                            if k < 4:
                                p.mm(bX[0:64, uo:uo + 64], xtc, xc)
                            p.mm(bXt[0:64, uo:uo + 64], xc, xtc)
                    if k < 4:
                        p.copy(Xn[nx][:, c0:c0 + 2, :].rearrange("p c n -> p (c n)"), bX[0:64, 0:256], eng='act')
                    p.copy(Xtn[nx][:, c0:c0 + 2, :].rearrange("p c n -> p (c n)"), bXt[0:64, 0:256], eng='dve')

                def pu():
                    nx = (k + 1) % 2
                    Pcur = Pa if k % 2 == 0 else Pfin
                    Pnxt = Pfin if k % 2 == 0 else Pa
                    for ci in cis:
                        for h in range(2):
                            hs = slice(h * 64, (h + 1) * 64)
                            uo = 256 + ((ci - c0) * 2 + h) * 64
                            p.mm(bX[0:64, uo:uo + 64], Xtn[nx][:, ci, hs], Psh[k % 2][:, ci, hs])
                    p.tt(Pnxt[:, c0:c0 + 2, :].rearrange("p c n -> p (c n)"), bX[0:64, 256:512],
                         Pcur[:, c0:c0 + 2, :].rearrange("p c n -> p (c n)"), ALU.add)
                    if k < 4:
                        p.copy(Psh[nx][:, c0:c0 + 2, :].rearrange("p c n -> p (c n)"),
                               Pnxt[:, c0:c0 + 2, :].rearrange("p c n -> p (c n)"), eng='act')
                return sq, pu
            for k in range(5):
                sq0, pu0 = mk_round(k, 0)
                sq1, pu1 = mk_round(k, 1)
                st += [sq0, sq1, pu0, pu1]
            return st

        def serial_closures(u):
            hp, half = u // 2, u % 2
            par = u % 2
            (AR4, AR3), _, (BKh4, BKh3), (XV4, XV3) = BIG[par]
            yraw = YR[hp % 2]
            out_ = []
            for ci in range(4):
                def mk(ci=ci):
                    ARc = AR4[:, ci]
                    M2 = M2b[par][:, ci]
                    Tt = Pb[par][:, ci, :]
                    c = half * 4 + ci

                    def s1():
                        p.transpose(PS[6][:, 0:128], BKh3[:, ci, :], ident)
                        p.transpose(PS[6][:, 128:256], XV3[:, ci, :], ident)
                        p.copy(BKhat, PS[6][:, 0:128], eng='act')
                        for h in range(2):
                            src = PS[6][64:128, 128 + h * 64:128 + (h + 1) * 64]
                            p.copy(UV[64:128, h, h * 64:(h + 1) * 64], src, eng='act')
                            p.copy(ZV[64:128, h, :], src, eng='act')

                    def s2():
                        for h in range(2):
                            zo = PS[4][0:64, h * 64:(h + 1) * 64]
                            p.mm(zo, ARc[:, 0, :], Hbd[:, hp, h * 64:(h + 1) * 64], start=True, stop=False)
                            p.mm(zo, M2[:, h, 0:64], ZV[:, h, :], start=False, stop=True)
                        p.copy(Zs, PS[4][0:64, 0:128], eng='dve')
                        for h in range(2):
                            p.mm(PS[4][0:64, 256 + h * 64:256 + (h + 1) * 64], Tt[:, h * 64:(h + 1) * 64], Zs[:, h * 64:(h + 1) * 64])
                        for h in range(2):
                            p.copy(UV[0:64, h, h * 64:(h + 1) * 64], PS[4][0:64, 256 + h * 64:256 + (h + 1) * 64], eng='dve')

                    def s3():
                        p.mm(PS[7][:, 0:64], Hbd[:, hp, :], ARc[:, 1, :], start=True, stop=False)
                        p.mm(PS[7][:, 0:64], UV[:, 0, :], M2[:, 0, 64:128], start=False, stop=False)
                        p.mm(PS[7][:, 0:64], UV[:, 1, :], M2[:, 1, 64:128], start=False, stop=True)
                        p.mm(PS[7][:, 128:256], BKhat, UV[:, 0, :], start=True, stop=False)
                        p.mm(PS[7][:, 128:256], BKhat, UV[:, 1, :], start=False, stop=True)
                        p.copy(yraw[:, c * 64:(c + 1) * 64], PS[7][:, 0:64], eng='dve')
                        for h in range(2):
                            rows = slice(h * 64, (h + 1) * 64)
                            hb = Hbd[rows, hp, h * 64:(h + 1) * 64]
                            p.stt(hb, hb, WC2[par][rows, ci:ci + 1], PS[7][rows, 128 + h * 64:128 + (h + 1) * 64], ALU.mult, ALU.add)
                    return [s1, s2, s3]
                out_ += mk()
            return out_

        def post_closures(hp):
            cs = slice(hp * 128, (hp + 1) * 128)
            rl, kl, vl = pl[:, hp, :], pl[:, 4 + hp, :], pl[:, 8 + hp, :]
            yraw = YR[hp % 2]
            t1, t2, t3, t4 = POSTT

            sqb_ = t3.bitcast(BF16)[:, 0:512]
            yrb_ = t3.bitcast(BF16)[:, 512:1024]
            t4b_ = t4.bitcast(BF16)[:, 0:512]

            def q1():
                p.act(sqb_, yraw, AF.Square)
                p.copy(yrb_, yraw, eng='pool')
                p.mm(PS[5], ones2b, yrb_)
                p.ts(t2, PS[5], 1.0 / 64, ALU.mult)

            def q1b():
                p.mm(PS[5], ones2b, sqb_)
                p.tt(t3, t2, t2, ALU.mult, eng='pool')
                p.stt(t3, PS[5], 1.0 / 64, t3, ALU.mult, ALU.subtract)
                p.stt(t4b_, rl, col(70 + hp), kl, ALU.mult, ALU.mult)

            def q1c():
                p.mm(PS[5], ones2b, t4b_)
                p.act(t3, t3, AF.Sqrt, bias=64e-5)
                p.tt(t4, PS[5], vl, ALU.mult)
                p.tt(t1, yraw, t2, ALU.subtract, eng='pool')

            def q2():
                p.recip(t3, t3)
                p.mm(PS[5], GUP[:, cs], SG)
                p.tt(t1, t1, t3, ALU.mult)
                p.ts(t1, t1, col(74 + hp), ALU.mult, col(78 + hp), ALU.add)

            def q3():
                p.tt(t1, t1, t4, ALU.add, eng='pool')
                p.tt(ycat[:, hp, :], PS[5], t1, ALU.mult)
            return [q1, q1b, q1c, q2, q3]

        def interleave(fg, bg):
            nf, nb = len(fg), len(bg)
            bi = 0
            for i, f_ in enumerate(fg):
                f_()
                tgt = ((i + 1) * nb) // max(nf, 1)
                while bi < tgt:
                    bg[bi]()
                    bi += 1
            while bi < nb:
                bg[bi]()
                bi += 1

        NU = 2 * nhp
        for f_ in prep_closures(0) + pre_closures(0):
            f_()
        for u in range(NU):
            bg = []
            if u % 2 == 0 and u >= 2:
                bg += post_closures(u // 2 - 1)
            if u + 1 < NU:
                bg += prep_closures(u + 1) + pre_closures(u + 1)
            interleave(serial_closures(u), bg)
        for f_ in post_closures(nhp - 1):
            f_()
        dbg_out("ycat%d" % tt, ycat)
        if 'wout' not in stages:
            return

        def ev_out(mc, ps_):
            p.tt(xT[:, mc, ts_], ps_, xT[:, mc, ts_], ALU.add)
        linear(w_out, w_out_b, D, D, lambda kc: ycat[:, kc, :], ev_out, tt == 0)

    o = 0
    U_ = bfv(o, o + 2176).rearrange("p (c t) -> p c t", t=544); o += 2176
    acc = scr[:, o:o + 8 * 512].rearrange("p (c t) -> p c t", t=512); o += 8 * 512
    SQ2 = o
    vT = bfv(o, o + 2048).rearrange("p (c t) -> p c t", t=512); o += 2048
    gsig = [scr[:, o:o + 512], scr[:, o + 512:o + 1024]]; o += 1024
    cm, cr, ctmp = scr[:, o:o + 512], scr[:, o + 512:o + 1024], scr[:, o + 1024:o + 1536]; o += 1536
    Dg = [bfv(o, o + 1984).rearrange("p (j q) -> p j q", q=128), bfv(o + 1984, o + 3968).rearrange("p (j q) -> p j q", q=128)]
    o += 3968
    assert o <= 17472

    def build_diag(c):
        p.tt(Dg[c % 2], ident.unsqueeze(1).to_broadcast([128, 31, 128]),
             pcol[:, 130 + c * 31:130 + (c + 1) * 31].unsqueeze(2).to_broadcast([128, 31, 128]), ALU.mult,
             eng='pool' if c % 2 == 0 else 'dve')

    def conv_tile(tt):
        ts_ = slice(tt * 512, (tt + 1) * 512)
        if ('conv', tt) not in prenormed:
            rmsnorm_tile(tt, 8)
        if tt == 0:
            p.memset(U_[:, :, 0:30], 0.0, eng='pool')
        else:
            p.copy(U_[:, :, 0:30], chalo, eng='pool')
        build_diag(0)
        build_diag(1)

        def ev_pw1(mc, ps_):
            if mc < 8:
                p.act(acc[:, mc, :], ps_, AF.Identity, bias=col(82 + mc))
            else:
                m = mc - 8
                gb = gsig[m % 2]
                p.act(gb, ps_, AF.Sigmoid, bias=col(82 + mc))
                p.tt(U_[:, m, 30:542], acc[:, m, :], gb, ALU.mult, eng='pool')
        linear(pw1, pw1_b, D, 2048, lambda kc: hT[:, kc, :], ev_pw1, tt == 0)
        p.copy(chalo, U_[:, :, 512:542], eng='pool')
        for c in range(8):
            ps_ = PS[4 + (c % 4)]
            for j in range(31):
                p.mm(ps_, Dg[c % 2][:, j, :], U_[:, c, j:j + 512], start=(j == 0), stop=(j == 30))
            p.act(acc[:, c, :], ps_, AF.Identity, bias=col(98 + c))
            if c + 2 < 8:
                build_diag(c + 2)
        for c in range(8):
            p.mm(PS[6], ones, acc[:, c, :], start=(c == 0), stop=(c == 7))
        for c in range(8):
            sqb = sqr[c % 2]
            p.act(sqb.bitcast(F32R), acc[:, c, :], AF.Square)
            p.mm(PS[7], onesr.bitcast(F32R), sqb.bitcast(F32R), start=(c == 0), stop=(c == 7))
        p.ts(cm, PS[6], 1.0 / D, ALU.mult)
        p.tt(ctmp, cm, cm, ALU.mult, eng='pool')
        p.stt(cr, PS[7], 1.0 / D, ctmp, ALU.mult, ALU.subtract)
        p.act(cr, cr, AF.Sqrt, bias=1e-5)
        p.recip(cr, cr)
        for c in range(8):
            p.tt(acc[:, c, :], acc[:, c, :], cm, ALU.subtract)
            p.tt(acc[:, c, :], acc[:, c, :], cr, ALU.mult, eng='pool')
            p.act(vT[:, c, :], acc[:, c, :], AF.Silu, bias=col(114 + c), scale=col(106 + c))
        dbg_out("vT%d" % tt, vT)

        def ev_pw2(mc, ps_):
            p.stt(xT[:, mc, ts_], ps_, col(122 + mc), xT[:, mc, ts_], ALU.add, ALU.add)
        linear(pw2, pw2_b, D, D, lambda kc: vT[:, kc, :], ev_pw2, tt == 0)

    for tt in range(NT):
        hybrid_tile(tt)
        dbg_out("xmix%d" % tt, xT[:, :, tt * 512:(tt + 1) * 512])
        if 'mlp0' in stages:
            def mid0(tt=tt):
                if tt + 1 < NT:
                    rmsnorm_tile(tt + 1, 0)
                    prenormed.add(('hyb', tt + 1))
                elif 'conv' in stages and NT > 1:
                    rmsnorm_tile(0, 8)
                    prenormed.add(('conv', 0))
            mlp_tile(tt, 0, mid0)
    dbg_out("xl0", xT)
    for tt in range(NT):
        if 'conv' in stages:
            conv_tile(tt)
        if 'mlp1' in stages:
            def mid1(tt=tt):
                if tt + 1 < NT and 'conv' in stages:
                    rmsnorm_tile(tt + 1, 8)
                    prenormed.add(('conv', tt + 1))
            mlp_tile(tt, 1, mid1)
    yo = scr[:, 0:4 * D].rearrange("p (b d) -> p b d", d=D)
    for tt in range(NT):
        rmsnorm_tile(tt, 32, dst=hTf)
        for b in range(4):
            for half in range(2):
                ps_ = PS[4 + ((b * 2 + half) % 4)]
                for cc in range(4):
                    c = half * 4 + cc
                    p.transpose(ps_[:, cc * 128:(cc + 1) * 128], hTf[:, c, b * 128:(b + 1) * 128], ident)
                p.copy(yo[:, b, half * 512:(half + 1) * 512], ps_, eng='dve' if half == 0 else 'act')
        p.dma(out[tt * 512:(tt + 1) * 512, :].rearrange("(b p) d -> p b d", p=128), yo,
              q='sp')
    nc = p.build()
    return nc, p


def host_weights(inp):
    w_in = np.asarray(inp['hy_w_in'][0], np.float32)
    qcols = 1792 + np.concatenate([np.concatenate([np.arange(j * 64, (j + 1) * 64), np.arange((j + 4) * 64, (j + 5) * 64)])
                                   for j in range(4)])
    perm = np.concatenate([np.arange(1792), qcols, np.arange(2304, 2560)])
    w_in_p = np.ascontiguousarray(w_in[:, perm])
    w_out = np.asarray(inp['hy_w_out'][0], np.float32)
    rperm = np.concatenate([np.arange(512), 512 + (qcols - 1792)])
    w_out_p = np.ascontiguousarray(w_out[rperm, :])
    lup = np.ascontiguousarray(np.concatenate([inp['hy_w_up'][0], inp['hy_a_up'][0]], 0).astype(np.float32))
    gup = np.ascontiguousarray(np.asarray(inp['hy_g_up'][0], np.float32))
    sinkb = np.ascontiguousarray(np.broadcast_to(np.asarray(inp['hy_sinks'][0], np.float32)[None, :], (128, 8)))
    return dict(w_in=w_in_p, w_out=w_out_p, pw1=np.ascontiguousarray(inp['cv_pw1_w'][0]),
                pw2=np.ascontiguousarray(inp['cv_pw2_w'][0]), w1=np.ascontiguousarray(inp['mlp_w1']),
                w2=np.ascontiguousarray(inp['mlp_w2']), lup=lup, gup=gup, sinkb=sinkb,
                cst=host_consts(), pcol=host_pcol(inp))


_CACHE = {}


def kernel(**inputs):
    inputs = {k: np.asarray(v) for k, v in inputs.items()}
    x = np.asarray(inputs['x'], np.float32)
    B, T, _ = x.shape
    shared = host_weights(inputs)
    nc, _ = build(T)
    in_maps = [dict(shared, x=np.ascontiguousarray(x[b])) for b in range(B)]
    res = run_bass_kernel_spmd(nc, in_maps, core_ids=list(range(B)))
    return np.stack([np.asarray(r["out"], np.float32) for r in res.results], 0)
```

```python
import numpy as np
from contextlib import ExitStack
import concourse.bass as bass
import concourse.mybir as mybir
from concourse.bass_utils import run_bass_kernel_spmd

F32 = mybir.dt.float32
BF16 = mybir.dt.bfloat16
ALU = mybir.AluOpType
AF = mybir.ActivationFunctionType
AX = mybir.AxisListType

NDMA = 12
BLK = 256
DTSZ = {mybir.dt.float32: 4, mybir.dt.bfloat16: 2, mybir.dt.float32r: 4}


class Prog:
    def __init__(self):
        self.nc = bass.Bass("TRN2", target_bir_lowering=False)
        self.es = ExitStack()
        self.ops = []
        self.tinfo = {}
        self.psum_names = set()
        self.ndma = 0

    def dram(self, name, shape, kind, dtype=F32):
        return self.nc.dram_tensor(name, list(shape), dtype, kind=kind).ap()

    def sb(self, name, shape, dtype=F32):
        t = self.es.enter_context(self.nc.sbuf_tensor(name, list(shape), dtype))
        ap = t[:]
        self.tinfo[ap.name] = int(np.prod(shape[1:])) * DTSZ[dtype]
        return ap

    def ps(self, name, shape, dtype=F32):
        t = self.es.enter_context(self.nc.psum_tensor(name, list(shape), dtype))
        ap = t[:]
        self.tinfo[ap.name] = int(np.prod(shape[1:]))
        self.psum_names.add(ap.name)
        return ap

    def blocks(self, ap):
        name = ap.name
        if name not in self.tinfo:
            return []
        if name in self.psum_names:
            return [(name, 0)]
        fs = self.tinfo[name]
        sz = DTSZ[ap.dtype]
        off = (int(ap.offset) * sz) % fs
        ext = 1
        for (st, cn) in ap.ap[1:]:
            ext += (cn - 1) * abs(st)
        ext *= sz
        b0 = off // BLK
        b1 = (off + ext - 1) // BLK
        return [(name, b) for b in range(b0, b1 + 1)]

    @staticmethod
    def _nfree(ap):
        n = 1
        for (st, cn) in ap.ap[1:]:
            n *= cn
        return n

    def _dur(self, eng, reads, writes, dma):
        n = self._nfree(writes[0]) if writes else 1
        if dma:
            return 0.1
        if eng == 'pe':
            return 0.06
        if eng == 'dve':
            return 0.08 + n / 960.0
        if eng == 'act':
            return 0.2 + n / 1200.0
        if eng == 'pool':
            return 0.3 + n / 480.0
        return 0.1

    def op(self, eng, fn, reads, writes, dma=False, rkeys=(), wkeys=()):
        rb = list(rkeys)
        for a in reads:
            if a is not None and not isinstance(a, (int, float)):
                rb += self.blocks(a)
        wb = list(wkeys)
        for a in writes:
            wb += self.blocks(a)
        wb += [b for b in rb if b[0] in self.psum_names]
        self.ops.append(dict(eng=eng, fn=fn, rb=rb, wb=wb, dma=dma, dur=self._dur(eng, reads, writes, dma)))

    @staticmethod
    def _cls(n):
        return 32 if n <= 32 else (64 if n <= 64 else 128)

    def _pemode(self, lhsT, tr):
        m = 1
        for (st, cn) in lhsT.ap[1:]:
            m *= cn
        return (int(lhsT.base_partition()), self._cls(int(lhsT.partition_size())), self._cls(m), tr)

    def mm(self, out, lhsT, rhs, start=True, stop=True):
        self.op('pe', lambda e: e.matmul(out, lhsT, rhs, start=start, stop=stop),
                [lhsT, rhs], [out])
        self.ops[-1]['pemode'] = (out.name, self._pemode(lhsT, False))
        n = self._nfree(rhs)
        base = max(0.055, n / 2200.0)
        dt_ = lhsT.dtype
        self.ops[-1]['dur'] = base * (4.0 if dt_ == mybir.dt.float32 else (2.0 if dt_ == mybir.dt.float32r else 1.0))

    def transpose(self, out, in_, ident):
        self.op('pe', lambda e: e.transpose(out, in_, ident), [in_, ident], [out])
        self.ops[-1]['pemode'] = (out.name, self._pemode(in_, True))
        self.ops[-1]['dur'] = 0.12

    def act(self, out, in_, func, bias=0.0, scale=1.0, eng='act'):
        self.op(eng, lambda e: e.activation(out, in_, func, bias=bias, scale=scale),
                [in_, bias, scale], [out])

    def tt(self, out, in0, in1, op, eng='dve'):
        self.op(eng, lambda e: e.tensor_tensor(out, in0, in1, op), [in0, in1], [out])

    def ts(self, out, in0, s1, op0, s2=None, op1=None, eng='dve'):
        if op1 is None:
            self.op(eng, lambda e: e.tensor_scalar(out, in0, s1, None, op0), [in0, s1], [out])
        else:
            self.op(eng, lambda e: e.tensor_scalar(out, in0, s1, s2, op0, op1), [in0, s1, s2], [out])

    def stt(self, out, in0, scalar, in1, op0, op1, eng='dve'):
        self.op(eng, lambda e: e.scalar_tensor_tensor(out, in0, scalar, in1, op0, op1),
                [in0, scalar, in1], [out])

    def copy(self, out, in_, eng='dve'):
        if eng == 'act':
            self.op(eng, lambda e: e.copy(out, in_), [in_], [out])
        else:
            self.op(eng, lambda e: e.tensor_copy(out, in_), [in_], [out])

    def memset(self, out, val, eng='dve'):
        self.op(eng, lambda e: e.memset(out, val), [], [out])

    def recip(self, out, in_):
        self.op('dve', lambda e: e.reciprocal(out, in_), [in_], [out])
        self.ops[-1]['dur'] = 0.1 + self._nfree(out) / 155.0

    def scan(self, out, d0, d1, init, op0, op1):
        self.op('dve', lambda e: e.tensor_tensor_scan(out, d0, d1, init, op0, op1), [d0, d1, init], [out])

    def dma(self, out, in_, q='sp', slow=False, rkeys=(), wkeys=()):
        if slow:
            self.op(q, lambda e: e.dma_start(out=out, in_=in_, allow_slow_non_contiguous=True), [in_], [out], dma=True,
                    rkeys=rkeys, wkeys=wkeys)
        else:
            self.op(q, lambda e: e.dma_start(out=out, in_=in_), [in_], [out], dma=True, rkeys=rkeys, wkeys=wkeys)

    def reschedule(self, window=512, lat=0.35, dma_lat=3.0, slack=0.2):
        ops = self.ops
        n = len(ops)
        last_w, readers = {}, {}
        preds = [None] * n
        for i, o in enumerate(ops):
            d = set()
            for b in o['rb']:
                w = last_w.get(b)
                if w is not None:
                    d.add(w)
            for b in o['wb']:
                w = last_w.get(b)
                if w is not None:
                    d.add(w)
                d.update(readers.get(b, ()))
            d.discard(i)
            preds[i] = d
            for b in o['rb']:
                readers.setdefault(b, []).append(i)
            for b in o['wb']:
                last_w[b] = i
                readers[b] = []
        queues = {}
        for i, o in enumerate(ops):
            queues.setdefault(o['eng'], []).append(i)
        dmas = [i for i, o in enumerate(ops) if o['dma']]
        for k in range(NDMA, len(dmas)):
            preds[dmas[k]].add(dmas[k - NDMA])
        done = [None] * n
        start = [None] * n
        free = {e: 0.0 for e in queues}
        head = {e: 0 for e in queues}
        issued = [False] * n
        remaining = n
        INF = 1e30

        def ready_time(i):
            t = 0.0
            for p_ in preds[i]:
                if not issued[p_]:
                    return INF
                dp = done[p_] + (0.0 if ops[p_]['eng'] == ops[i]['eng'] and not ops[p_]['dma'] else lat)
                if dp > t:
                    t = dp
            return t
        while remaining:
            best = None
            for e, q in queues.items():
                h = head[e]
                while h < len(q) and issued[q[h]]:
                    h += 1
                head[e] = h
                if h >= len(q):
                    continue
                hi = q[h]
                hr = ready_time(hi)
                hstart = max(free[e], hr) if hr < INF else INF
                cand = (hstart, hi)
                if hstart > free[e] + 1e-9:
                    cnt = 0
                    j = h + 1
                    while j < len(q) and cnt < window:
                        oj = q[j]
                        j += 1
                        if issued[oj]:
                            continue
                        cnt += 1
                        if ops[oj]['dma']:
                            continue
                        r = ready_time(oj)
                        if r >= INF:
                            continue
                        s = max(free[e], r)
                        if s + ops[oj]['dur'] <= hstart + slack + 1e-9 and s < cand[0]:
                            cand = (s, oj)
                if cand[0] < INF and (best is None or cand < best[0:2]):
                    best = (cand[0], cand[1], e)
            assert best is not None, "scheduler deadlock"
            s, i, e = best
            o = ops[i]
            start[i] = s
            issued[i] = True
            remaining -= 1
            free[e] = s + o['dur']
            done[i] = s + o['dur'] + (dma_lat if o['dma'] else 0.0)
        order = sorted(range(n), key=lambda i: (start[i], i))
        self.ops = [ops[i] for i in order]
        self.sim_time = max(done)

    def build(self, resched=True):
        nc = self.nc
        if resched:
            self.reschedule()
        ENG = ['pe', 'dve', 'act', 'pool', 'sp']
        engobj = {'pe': nc.tensor, 'dve': nc.vector, 'act': nc.scalar, 'pool': nc.gpsimd, 'sp': nc.sync}
        last_w = {}
        readers = {}
        bank_mode = {}
        known = {e: {} for e in ENG}
        eidx = {e: 0 for e in ENG}
        slot_last = [None] * NDMA
        slot_cnt = [0] * NDMA
        ndma = 0
        ops = self.ops
        for i, o in enumerate(ops):
            e = o['eng']
            deps = set()
            for b in o['rb']:
                w = last_w.get(b)
                if w is not None:
                    deps.add(w)
            for b in o['wb']:
                w = last_w.get(b)
                if w is not None:
                    deps.add(w)
                for r in readers.get(b, ()):
                    deps.add(r)
            forced = set()
            if 'pemode' in o:
                bank, mode = o['pemode']
                lm = bank_mode.get(bank)
                if lm is not None and lm[0] != mode:
                    deps.add(lm[1])
                    forced.add(lm[1])
                bank_mode[bank] = (mode, i)
            if o['dma']:
                k = ndma % NDMA
                ndma += 1
                o['slot'] = k
                if slot_last[k] is not None:
                    deps.add(slot_last[k])
                slot_cnt[k] += 1
                o['dval'] = 16 * slot_cnt[k]
                slot_last[k] = i
                o['key'] = ('dma', k)
                o['kidx'] = slot_cnt[k]
            else:
                eidx[e] += 1
                o['key'] = ('eng', e)
                o['kidx'] = eidx[e]
            waits = []
            vc = {}
            kn = known[e]
            for d in sorted(deps, reverse=True):
                if d == i:
                    continue
                p = ops[d]
                if p['key'] == ('eng', 'pe') and e == 'pe' and not o['dma'] and d not in forced:
                    continue
                if kn.get(p['key'], 0) >= p['kidx']:
                    continue
                waits.append(d)
                p['signal'] = True
                for k2, v2 in p['vc'].items():
                    if kn.get(k2, 0) < v2:
                        kn[k2] = v2
            o['waits'] = waits
            vc = dict(kn)
            vc[o['key']] = o['kidx']
            o['vc'] = vc
            o.setdefault('signal', False)
            if o['dma']:
                o['signal'] = True
            for b in o['rb']:
                readers.setdefault(b, []).append(i)
            for b in o['wb']:
                last_w[b] = i
                readers[b] = []
        self.final_dma = [(k, 16 * slot_cnt[k]) for k in range(NDMA) if slot_cnt[k]]
        sems = {e: self.es.enter_context(nc.semaphore("s_" + e)) for e in ENG}
        dsems = [self.es.enter_context(nc.semaphore("d_%d" % k)) for k in range(NDMA)]
        sig = {e: 0 for e in ENG}
        for o in ops:
            if o['dma']:
                o['sem'] = dsems[o['slot']]
                o['sval'] = o['dval']
            elif o['signal']:
                sig[o['eng']] += 1
                o['sem'] = sems[o['eng']]
                o['sval'] = sig[o['eng']]
        self.nwaits = sum(len(o['waits']) for o in ops)
        self.nsig = dict(sig)
        per = {e: [o for o in ops if o['eng'] == e] for e in ENG}
        block = self.es.enter_context(nc.Block())

        def emit(eng_name):
            def f(eo):
                for o in per[eng_name]:
                    for d in o['waits']:
                        p = ops[d]
                        eo.wait_ge(p['sem'], p['sval'])
                    ins = o['fn'](eo)
                    if o['dma']:
                        ins.then_inc(o['sem'], 16)
                    elif o['signal']:
                        ins.then_inc(o['sem'], 1)
                if eng_name == 'sp':
                    for k, v in self.final_dma:
                        eo.wait_ge(dsems[k], v)
            return f
        block.tensor(emit('pe'))
        block.vector(emit('dve'))
        block.scalar(emit('act'))
        block.gpsimd(emit('pool'))
        block.sync(emit('sp'))
        self.es.close()
        return nc


D = 1024
C0 = float(np.exp(-0.5))
NPC = 384
F32R = mybir.dt.float32r
RDT = BF16
O_ID, O_ONES2, O_MG, O_MNT, O_ID2, O_D0, O_BC, O_BP, O_SK, NCST = 0, 128, 256, 384, 512, 640, 1152, 2176, 3200, 3208


def host_consts():
    c = np.zeros((128, NCST), np.float32)
    c[:, O_ID:O_ID + 128] = np.eye(128)
    c[0:64, O_ONES2:O_ONES2 + 64] = 1.0
    c[64:128, O_ONES2 + 64:O_ONES2 + 128] = 1.0
    j = np.arange(64)[:, None]
    i = np.arange(64)[None, :]
    lt = (j < i).astype(np.float32)
    le = (j <= i).astype(np.float32)
    mg = np.zeros((128, 128), np.float32)
    mg[0:64, 0:64] = lt
    mg[0:64, 64:128] = le
    mg[64:128, 0:64] = lt
    mg[64:128, 64:128] = le
    c[:, O_MG:O_MG + 128] = mg
    c[0:64, O_MNT:O_MNT + 64] = (i.T > j.T).astype(np.float32).T * 0 + (np.arange(64)[:, None] > np.arange(64)[None, :])
    c[0:64, O_MNT + 64:O_MNT + 128] = c[0:64, O_MNT:O_MNT + 64]
    c[0:64, O_ID2:O_ID2 + 64] = np.eye(64)
    c[0:64, O_ID2 + 64:O_ID2 + 128] = np.eye(64)
    d0 = np.ones((512,), np.float32)
    d0[0::64] = 0.0
    c[:, O_D0:O_D0 + 512] = d0[None, :]
    key = np.arange(128)[:, None]
    q = np.arange(128)[None, :]
    for h in range(8):
        slope = 2.0 ** (-(h + 1))
        cur = np.where(q >= key, -slope * (q - key), -30000.0)
        prv = np.where(key > q, -slope * (q + 128 - key), -30000.0)
        c[:, O_BC + h * 128:O_BC + (h + 1) * 128] = cur
        c[:, O_BP + h * 128:O_BP + (h + 1) * 128] = prv
    return c


def host_pcol(inp):
    pc = np.zeros((128, NPC), np.float32)

    def put(col, vec):
        v = np.asarray(vec, np.float32).reshape(-1, 128)
        pc[:, col:col + v.shape[0]] = v.T
    put(0, inp['norm_mix_g'])
    put(16, inp['norm_ffn_g'])
    put(32, inp['final_norm_g'])
    put(40, inp['hy_mu'][0])
    put(54, inp['hy_w0'][0])
    put(58, inp['hy_a0'][0])
    put(62, inp['hy_k_k'][0])
    put(66, inp['hy_k_a'][0])
    put(70, inp['hy_r_k'][0])
    put(74, inp['hy_gn_g'][0])
    put(78, inp['hy_gn_b'][0])
    put(82, inp['cv_pw1_b'][0])
    put(98, inp['cv_dw_b'][0])
    put(106, inp['cv_ln_g'][0])
    put(114, inp['cv_ln_b'][0])
    put(122, inp['cv_pw2_b'][0])
    dw = np.asarray(inp['cv_dw_w'][0], np.float32)
    pc[:, 130:130 + 248] = dw.T.reshape(8, 128, 31).transpose(1, 0, 2).reshape(128, 248)
    return pc


def build(T=2048, dbg=(), stages=('att', 'rwkv', 'wout', 'mlp0', 'conv', 'mlp1'), nchunks=8, nhp=4, cut=99):
    p = Prog()
    NT = T // 512
    x = p.dram("x", [T, D], "ExternalInput")
    out = p.dram("out", [T, D], "ExternalOutput")
    w_in = p.dram("w_in", [D, 2560], "ExternalInput")
    w_out = p.dram("w_out", [D, D], "ExternalInput")
    pw1 = p.dram("pw1", [D, 2048], "ExternalInput")
    pw2 = p.dram("pw2", [D, D], "ExternalInput")
    w1 = p.dram("w1", [2, D, 4096], "ExternalInput")
    w2 = p.dram("w2", [2, 4096, D], "ExternalInput")
    lupd = p.dram("lup", [128, 512], "ExternalInput")
    gupd = p.dram("gup", [128, 512], "ExternalInput")
    cstd = p.dram("cst", [128, NCST], "ExternalInput")
    pcold = p.dram("pcol", [128, NPC], "ExternalInput")
    sinkd = p.dram("sinkb", [128, 8], "ExternalInput")
    dbg_outs = {}
    def wscr(name, K, M, lead=None):
        nt = (K // 256) * (M // 512)
        return p.dram(name, ([lead] if lead else []) + [nt, 128, 1024], "Internal", dtype=BF16)
    w_in_b = wscr("w_in_b", D, 2560)
    w_out_b = wscr("w_out_b", D, D)
    pw1_b = wscr("pw1_b", D, 2048)
    pw2_b = wscr("pw2_b", D, D)
    w1_b = wscr("w1_b", D, 4096, 2)
    w2_b = wscr("w2_b", 4096, D, 2)

    def dbg_out(name, ap):
        if name in dbg:
            shp = list(ap.shape)
            dt_ = p.dram("dbg_" + name, shp, "ExternalOutput", dtype=ap.dtype)
            p.dma(dt_, ap)

    cst = p.sb("cstsb", [128, NCST])
    pcol = p.sb("pcolsb", [128, NPC])
    ones = p.sb("ones", [128, 128])
    onesr = p.sb("onesr", [128, 128])
    sqr = [p.sb("sqr%d" % i, [128, 512]) for i in range(2)]
    esink = p.sb("esink", [128, 8])
    xT = p.sb("xT", [128, 8, T])
    wst = [p.sb("wst%d" % i, [128, 2, 512]) for i in range(2)]
    NWB = 4
    wbf = [p.sb("wbf%d" % i, [128, 2, 512], dtype=BF16) for i in range(NWB)]
    LUP = p.sb("LUP", [128, 512], dtype=BF16)
    GUP = p.sb("GUP", [128, 512], dtype=BF16)
    ones2b = p.sb("ones2b", [128, 128], dtype=BF16)
    Hbd = p.sb("Hbd", [128, 4, 128])
    UV = p.sb("UV", [128, 2, 128])
    ZV = p.sb("ZV", [128, 2, 64])
    Xn = [p.sb("Xn%d" % i, [64, 4, 128], dtype=RDT) for i in range(2)]
    Xtn = [p.sb("Xtn%d" % i, [64, 4, 128], dtype=RDT) for i in range(2)]
    Psh = [p.sb("Psh%d" % i, [64, 4, 128], dtype=RDT) for i in range(2)]
    Pa = p.sb("Pa", [64, 4, 128])
    Pb = [p.sb("Pb%d" % i, [64, 4, 128]) for i in range(2)]
    Zs = p.sb("Zs", [64, 128])
    BKhat = p.sb("BKhat", [128, 128])
    M2b = [p.sb("M2b%d" % i, [128, 4, 2, 128]) for i in range(2)]
    WC2 = [p.sb("WC2_%d" % i, [128, 4]) for i in range(2)]
    ST2 = [p.sb("ST2_%d" % i, [128, 4]) for i in range(2)]
    carry = p.sb("carry", [128, 16])
    kT = p.sb("kT", [128, 640], dtype=BF16)
    vTM = p.sb("vTM", [128, 5, 128], dtype=BF16)
    onesb = p.sb("onesb", [128, 128], dtype=BF16)
    st1 = p.sb("st1", [128, 512])
    NSCR = 19520
    scr = p.sb("scr", [128, NSCR])

    def bfv(a, b):
        return scr[:, a:b].bitcast(BF16)
    hT = bfv(17472, 17472 + 2048).rearrange("p (c t) -> p c t", t=512)
    hTf = scr[:, 8192:8192 + 4096].rearrange("p (c t) -> p c t", t=512)
    chalo = p.sb("chalo", [128, 8, 30], dtype=BF16)
    PS = [p.ps("ps%d" % i, [128, 512]) for i in range(8)]

    ident = cst[:, O_ID:O_ID + 128]
    ones2 = cst[:, O_ONES2:O_ONES2 + 128]
    maskG = cst[:, O_MG:O_MG + 128]
    maskNt2 = cst[0:64, O_MNT:O_MNT + 128]
    ident2 = cst[0:64, O_ID2:O_ID2 + 128]
    d0 = cst[:, O_D0:O_D0 + 512]
    Bcur = cst[:, O_BC:O_BC + 1024].rearrange("p (h q) -> p h q", q=128)
    Bprev = cst[:, O_BP:O_BP + 1024].rearrange("p (h q) -> p h q", q=128)

    def col(i):
        return pcol[:, i:i + 1]

    p.dma(cst, cstd)
    p.dma(pcol, pcold)
    p.dma(esink, sinkd)
    p.memset(ones, 1.0)
    p.copy(onesb, ones)
    p.copy(onesr.bitcast(F32R), ones)
    p.memset(Hbd, 0.0, eng='pool')
    p.memset(UV, 0.0, eng='pool')
    p.memset(ZV, 0.0, eng='pool')
    p.memset(carry, 0.0, eng='pool')
    p.act(esink, esink, AF.Exp)
    p.ts(pcol[:, 378:382], pcol[:, 66:70], -1.0, ALU.mult, 1.0, ALU.add)

    p.dma(scr[:, 8192:8704], lupd)
    p.dma(scr[:, 8704:9216], gupd)
    p.copy(LUP, scr[:, 8192:8704])
    p.copy(GUP, scr[:, 8704:9216], eng='act')
    p.copy(ones2b, ones2)
    xin_region = scr[:, 0:4 * D].rearrange("p (b d) -> p b d", d=D)
    for tt in range(NT):
        p.dma(xin_region, x[tt * 512:(tt + 1) * 512, :].rearrange("(b p) d -> p b d", p=128),
              q='sp')
        for c in range(8):
            ps_ = PS[4 + (c % 4)]
            for b in range(4):
                p.transpose(ps_[:, b * 128:(b + 1) * 128], xin_region[:, b, c * 128:(c + 1) * 128], ident)
            p.copy(xT[:, c, tt * 512:(tt + 1) * 512], ps_, eng='dve' if c % 2 == 0 else 'act')

    lin_ctr = [0, 0]
    pend_store = []

    def linear(W, Wb, K, M, rhs_fn, evac_fn, first, hook=None):
        nkg = K // 256
        nmg = (M + 511) // 512
        for mg in range(nmg):
            mw = min(512, M - mg * 512)
            nm = mw // 128
            banks = PS[0:4] if lin_ctr[0] % 2 == 0 else PS[4:8]
            lin_ctr[0] += 1
            for kg in range(nkg):
                wt = wbf[lin_ctr[1] % NWB]
                q = 'sp'
                key = (Wb.name, int(Wb.offset), kg, mg)
                assert mw == 512
                wsrc = Wb[mg * nkg + kg].rearrange("p (kc m) -> p kc m", kc=2)
                if first:
                    ws = wst[lin_ctr[1] % 2]
                    ce = 'dve' if lin_ctr[1] % 2 == 0 else 'act'
                    p.dma(ws[:, :, 0:mw],
                          W[kg * 256:(kg + 1) * 256, mg * 512:mg * 512 + mw].rearrange("(kc p) m -> p kc m", p=128), q=q)
                    p.copy(wt[:, :, 0:mw], ws[:, :, 0:mw], eng=ce)
                    if pend_store:
                        pend_store.pop()()
                    pend_store.append(lambda wsrc=wsrc, wt=wt, mw=mw, key=key: p.dma(wsrc, wt[:, :, 0:mw], q='sp', wkeys=[key]))
                else:
                    p.dma(wt[:, :, 0:mw], wsrc, q=q, rkeys=[key])
                lin_ctr[1] += 1
                for kc in range(2):
                    for m in range(nm):
                        p.mm(banks[m], wt[:, kc, m * 128:(m + 1) * 128], rhs_fn(kg * 2 + kc),
                             start=(kg == 0 and kc == 0), stop=(kg == nkg - 1 and kc == 1))
            for m in range(nm):
                evac_fn(mg * 4 + m, banks[m])
            if hook is not None and mg == 0:
                hook()
        if pend_store:
            pend_store.pop()()

    def rmsnorm_tile(tt, gcol0, dst=None):
        dst = hT if dst is None else dst
        ts_ = slice(tt * 512, (tt + 1) * 512)
        for c in range(8):
            sqb = sqr[c % 2]
            p.act(sqb.bitcast(F32R), xT[:, c, ts_], AF.Square)
            p.mm(PS[7], onesr.bitcast(F32R), sqb.bitcast(F32R), start=(c == 0), stop=(c == 7))
        p.act(st1, PS[7], AF.Sqrt, bias=1e-6, scale=1.0 / D)
        p.recip(st1, st1)
        for c in range(8):
            p.stt(dst[:, c, :], xT[:, c, ts_], col(gcol0 + c), st1, ALU.mult, ALU.mult)

    prenormed = set()

    def mlp_tile(tt, layer, mid=None):
        ts_ = slice(tt * 512, (tt + 1) * 512)
        rmsnorm_tile(tt, 16 + layer * 8)
        h1 = bfv(0, 8192).rearrange("p (c t) -> p c t", t=512)

        def ev1(mc, ps_):
            p.act(h1[:, mc, :], ps_, AF.Relu)
            p.tt(h1[:, mc, :], h1[:, mc, :], h1[:, mc, :], ALU.mult, eng='pool')
        linear(w1[layer], w1_b[layer], D, 4096, lambda kc: hT[:, kc, :], ev1, tt == 0)

        def ev2(mc, ps_):
            p.tt(xT[:, mc, ts_], ps_, xT[:, mc, ts_], ALU.add)
        linear(w2[layer], w2_b[layer], 4096, D, lambda kc: h1[:, kc, :], ev2, tt == 0, hook=mid)

    o = 0
    pl = scr[:, o:o + 14 * 512].rearrange("p (c t) -> p c t", t=512); o += 14 * 512
    ycat = bfv(o, o + 2048).rearrange("p (c t) -> p c t", t=512); o += 2048
    prawb = [scr[:, o:o + 513], scr[:, o + 544:o + 544 + 513]]; o += 1088
    TL = bfv(o, o + 256); o += 512
    SG = bfv(o, o + 256); o += 512
    SLOT0 = o
    assert SLOT0 + 12 * 512 == 17472, SLOT0
    slot = [scr[:, o + i * 512:o + (i + 1) * 512] for i in range(16)]
    qT = bfv(o, o + 1024).rearrange("p (c t) -> p c t", t=512)
    vattT = slot[4]
    sTa, sTb, denb = slot[5], slot[6], slot[9]
    eTa = bfv(SLOT0 + 7 * 512, SLOT0 + 7 * 512 + 256)
    eTb = bfv(SLOT0 + 8 * 512, SLOT0 + 8 * 512 + 256)
    o += 16 * 512
    assert o <= NSCR

    def hybrid_tile(tt):
        ts_ = slice(tt * 512, (tt + 1) * 512)
        if ('hyb', tt) not in prenormed:
            rmsnorm_tile(tt, 0)

        def ev_in(mc, ps_):
            if mc < 14:
                pb = prawb[mc % 2]
                p.copy(pb[:, 0:1], carry[:, mc:mc + 1], eng='pool')
                p.copy(pb[:, 1:513], ps_, eng='act')
                p.copy(carry[:, mc:mc + 1], pb[:, 512:513], eng='pool')
                p.tt(pl[:, mc, :], pb[:, 0:512], pb[:, 1:513], ALU.subtract)
                p.stt(pl[:, mc, :], pl[:, mc, :], col(40 + mc), pb[:, 1:513], ALU.mult, ALU.add)
            elif mc < 18:
                p.copy(qT[:, mc - 14, :], ps_, eng='act')
            elif mc == 18:
                p.copy(kT[:, 128:640], ps_, eng='act')
            else:
                p.copy(vattT, ps_, eng='act')
        linear(w_in, w_in_b, D, 2560, lambda kc: hT[:, kc, :], ev_in, tt == 0)

        if 'att' not in stages:
            return
        for b in range(4):
            p.transpose(PS[6][:, b * 128:(b + 1) * 128], vattT[:, b * 128:(b + 1) * 128], ident)
        p.copy(vTM[:, 1:5, :], PS[6].rearrange("p (b d) -> p b d", d=128))
        for blk in range(4):
            first = (tt == 0 and blk == 0)
            for g in range(2):
                rows = slice(g * 64, (g + 1) * 64)
                pc_, pp_ = (PS[1], PS[0]) if g == 0 else (PS[3], PS[2])
                qv = qT[rows, :, blk * 128:(blk + 1) * 128]
                p.mm(pc_, kT[rows, 128 + blk * 128:256 + blk * 128], qv)
                if not first:
                    p.mm(pp_, kT[rows, blk * 128:128 + blk * 128], qv)
                p.stt(sTa.rearrange("p (h q) -> p h q", q=128), pc_.rearrange("p (h q) -> p h q", q=128),
                      0.125, Bcur[:, 4 * g:4 * g + 4, :], ALU.mult, ALU.add)
                p.act(eTa, sTa, AF.Exp)
                if not first:
                    p.stt(sTb.rearrange("p (h q) -> p h q", q=128), pp_.rearrange("p (h q) -> p h q", q=128),
                          0.125, Bprev[:, 4 * g:4 * g + 4, :], ALU.mult, ALU.add)
                    p.act(eTb, sTb, AF.Exp)
                p.mm(PS[4], vTM[:, blk + 1, :], eTa, start=True, stop=first)
                if not first:
                    p.mm(PS[4], vTM[:, blk, :], eTb, start=False, stop=True)
                p.mm(PS[5], onesb, eTa, start=True, stop=first)
                if not first:
                    p.mm(PS[5], onesb, eTb, start=False, stop=True)
                den3 = denb.rearrange("p (h q) -> p h q", q=128)
                p.tt(den3, PS[5].rearrange("p (h q) -> p h q", q=128),
                     esink[:, 4 * g:4 * g + 4].unsqueeze(2).to_broadcast([128, 4, 128]), ALU.add)
                p.recip(denb, denb)
                p.tt(ycat[rows, 4:8, blk * 128:(blk + 1) * 128],
                     PS[4][rows, :].rearrange("p (h q) -> p h q", q=128), den3[rows], ALU.mult)
        p.copy(kT[:, 0:128], kT[:, 512:640], eng='pool')
        p.copy(vTM[:, 0, :], vTM[:, 4, :], eng='pool')

        if 'rwkv' not in stages:
            dbg_out("ycat%d" % tt, ycat)
            return
        p.copy(TL[64:128, :], pl[64:128, 12, :], eng='pool')
        p.act(TL[0:64, :], pl[0:64, 12, :], AF.Tanh)
        p.act(SG, pl[:, 13, :], AF.Sigmoid)

        def sl(i, n=512):
            b8 = SLOT0 + i * 512
            return scr[:, b8:b8 + n]

        def bigv(i):
            a_ = sl(i)
            return (a_.rearrange("p (c two t) -> p c two t", two=2, t=64), a_.rearrange("p (c n) -> p c n", n=128))
        BIG = [[bigv(s_ * 4 + k_) for k_ in range(4)] for s_ in range(2)]
        PT = [scr[:, SLOT0 + 8 * 512 + i * 256:SLOT0 + 8 * 512 + (i + 1) * 256] for i in range(8)]
        YR = [sl(12), sl(13)]
        POSTT = [sl(14), sl(15), prawb[0][:, 0:512], prawb[1][:, 0:512]]

        def v3(a):
            return a.rearrange("p (c t) -> p c t", t=64)

        def prep_closures(u):
            hp, half = u // 2, u % 2
            par = u % 2
            cs = slice(hp * 128, (hp + 1) * 128)
            tok = slice(half * 256, (half + 1) * 256)
            rl, kl, vl = pl[:, hp, tok], pl[:, 4 + hp, tok], pl[:, 8 + hp, tok]
            sS, sEe, sEi, sEni, sA, sKK, sT, sB = PT
            sTb = sT.bitcast(BF16)[:, 0:256]
            (AR4, AR3), (BK4, BK3), (BKh4, BKh3), (XV4, XV3) = BIG[par]
            WCu, STu = WC2[par], ST2[par]
            d0h = d0[:, 0:256]

            def c1():
                p.mm(PS[1][:, 0:256], LUP[0:64, cs], TL[0:64, tok])
                p.act(sEe, PS[1][:, 0:256], AF.Sigmoid, bias=col(54 + hp))
                p.scan(sS, d0h, sEe, 0.0, ALU.mult, ALU.add)
                p.tt(sEe, sS, sEe, ALU.subtract)
                p.act(sEe, sEe, AF.Exp, scale=-C0)

            def c2():
                p.act(sEi, sS, AF.Exp, scale=-C0)
                p.act(sEni, sS, AF.Exp, scale=C0)
                p.copy(STu, v3(sS)[:, :, 63], eng='pool')
                p.act(WCu, STu, AF.Exp, scale=-C0)
                p.tt(v3(sS), STu.unsqueeze(2).to_broadcast([128, 4, 64]), v3(sS), ALU.subtract, eng='pool')
                p.act(sS, sS, AF.Exp, scale=-C0)

            def c3():
                p.ts(sKK, kl, col(62 + hp), ALU.mult)
                p.act(sTb, sKK, AF.Square)
                p.mm(PS[3][:, 0:256], LUP[64:128, cs], TL[64:128, tok])
                p.act(sA, PS[3][:, 0:256], AF.Sigmoid, bias=col(58 + hp))

            def c3b():
                p.mm(PS[5][:, 0:256], ones2b, sTb)
                p.act(sT, PS[5][:, 0:256], AF.Sqrt)

            def c4():
                p.ts(sT, sT, 1e-12, ALU.max)
                p.recip(sT, sT)
                p.tt(sKK, sKK, sT, ALU.mult)
                p.ts(sT, sA, col(66 + hp), ALU.mult, col(378 + hp), ALU.add)
                p.tt(kl, kl, sT, ALU.mult, eng='pool')
                p.tt(sB, sKK, sA, ALU.mult, eng='pool')

            def c5():
                p.stt(AR4[:, :, 0, :], v3(sKK), -1.0, v3(sEe), ALU.mult, ALU.mult)
                p.tt(AR4[:, :, 1, :], v3(rl), v3(sEi), ALU.mult, eng='pool')
                p.tt(BK4[:, :, 0, :], v3(sB), v3(sEni), ALU.mult)
                p.tt(BK4[:, :, 1, :], v3(kl), v3(sEni), ALU.mult, eng='pool')

            def c6():
                p.tt(BKh4[:, :, 0, :], v3(sB), v3(sS), ALU.mult)
                p.tt(BKh4[:, :, 1, :], v3(kl), v3(sS), ALU.mult, eng='pool')
                p.memset(XV4[:, :, 0, :], 0.0, eng='pool')
                p.copy(XV4[:, :, 1, :], v3(vl), eng='pool')
            return [c1, c2, c3, c3b, c4, c5, c6]

        def pre_closures(u):
            par = u % 2
            (AR4, AR3), (BK4, BK3), _, _ = BIG[par]
            M2h = M2b[par]
            Pfin = Pb[par]
            st = []

            def stageA():
                for ci in range(4):
                    for h in range(2):
                        rows = slice(h * 64, (h + 1) * 64)
                        bank = PS[h * 2 + ci // 2]
                        o_ = (ci % 2) * 256
                        p.mm(bank[:, o_:o_ + 128], BK3[rows, ci, :], AR3[rows, ci, :])
                        p.mm(bank[:, o_ + 128:o_ + 192], AR3[rows, ci, :], BK4[rows, ci, 0, :])
                for h in range(2):
                    for pr in range(2):
                        bank = PS[h * 2 + pr].rearrange("p (c n) -> p c n", n=256)
                        p.tt(M2h[:, pr * 2:pr * 2 + 2, h, :], bank[:, :, 0:128],
                             maskG.unsqueeze(1).to_broadcast([128, 2, 128]), ALU.mult)
                        p.tt(Xtn[0][:, pr * 2:pr * 2 + 2, h * 64:(h + 1) * 64], bank[0:64, :, 128:192],
                             maskNt2[:, 0:64].unsqueeze(1).to_broadcast([64, 2, 64]), ALU.mult)
                p.tt(Pa.rearrange("p c (h t) -> p (c h) t", t=64),
                     M2h[0:64].rearrange("p c h n -> p (c h) n")[:, :, 0:64],
                     ident2[:, 0:64].unsqueeze(1).to_broadcast([64, 8, 64]), ALU.add, eng='pool')
                p.copy(Xn[0].rearrange("p c (h t) -> p (c h) t", t=64),
                       M2h[0:64].rearrange("p c h n -> p (c h) n")[:, :, 0:64], eng='pool')
                p.copy(Psh[0].rearrange("p c n -> p (c n)"), Pa.rearrange("p c n -> p (c n)"), eng='act')
            st.append(stageA)

            def mk_round(k, sb):
                cis = (0, 1) if sb == 0 else (2, 3)
                bX, bXt = (PS[0], PS[1]) if sb == 0 else (PS[2], PS[3])
                c0 = cis[0]

                def sq():
                    nx = (k + 1) % 2
                    for ci in cis:
                        for h in range(2):
                            hs = slice(h * 64, (h + 1) * 64)
                            xc = Xn[k % 2][:, ci, hs]
                            xtc = Xtn[k % 2][:, ci, hs]
                            uo = ((ci - c0) * 2 + h) * 64
                            if k < 4:
                                p.mm(bX[0:64, uo:uo + 64], xtc, xc)
                            p.mm(bXt[0:64, uo:uo + 64], xc, xtc)
                    if k < 4:
                        p.copy(Xn[nx][:, c0:c0 + 2, :].rearrange("p c n -> p (c n)"), bX[0:64, 0:256], eng='act')
                    p.copy(Xtn[nx][:, c0:c0 + 2, :].rearrange("p c n -> p (c n)"), bXt[0:64, 0:256], eng='dve')

                def pu():
                    nx = (k + 1) % 2
                    Pcur = Pa if k % 2 == 0 else Pfin
                    Pnxt = Pfin if k % 2 == 0 else Pa
                    for ci in cis:
                        for h in range(2):
                            hs = slice(h * 64, (h + 1) * 64)
                            uo = 256 + ((ci - c0) * 2 + h) * 64
                            p.mm(bX[0:64, uo:uo + 64], Xtn[nx][:, ci, hs], Psh[k % 2][:, ci, hs])
                    p.tt(Pnxt[:, c0:c0 + 2, :].rearrange("p c n -> p (c n)"), bX[0:64, 256:512],
                         Pcur[:, c0:c0 + 2, :].rearrange("p c n -> p (c n)"), ALU.add)
                    if k < 4:
                        p.copy(Psh[nx][:, c0:c0 + 2, :].rearrange("p c n -> p (c n)"),
                               Pnxt[:, c0:c0 + 2, :].rearrange("p c n -> p (c n)"), eng='act')
                return sq, pu
            for k in range(5):
                sq0, pu0 = mk_round(k, 0)
                sq1, pu1 = mk_round(k, 1)
                st += [sq0, sq1, pu0, pu1]
            return st

        def serial_closures(u):
            hp, half = u // 2, u % 2
            par = u % 2
            (AR4, AR3), _, (BKh4, BKh3), (XV4, XV3) = BIG[par]
            yraw = YR[hp % 2]
            out_ = []
            for ci in range(4):
                def mk(ci=ci):
                    ARc = AR4[:, ci]
                    M2 = M2b[par][:, ci]
                    Tt = Pb[par][:, ci, :]
                    c = half * 4 + ci

                    def s1():
                        p.transpose(PS[6][:, 0:128], BKh3[:, ci, :], ident)
                        p.transpose(PS[6][:, 128:256], XV3[:, ci, :], ident)
                        p.copy(BKhat, PS[6][:, 0:128], eng='act')
                        for h in range(2):
                            src = PS[6][64:128, 128 + h * 64:128 + (h + 1) * 64]
                            p.copy(UV[64:128, h, h * 64:(h + 1) * 64], src, eng='act')
                            p.copy(ZV[64:128, h, :], src, eng='act')

                    def s2():
                        for h in range(2):
                            zo = PS[4][0:64, h * 64:(h + 1) * 64]
                            p.mm(zo, ARc[:, 0, :], Hbd[:, hp, h * 64:(h + 1) * 64], start=True, stop=False)
                            p.mm(zo, M2[:, h, 0:64], ZV[:, h, :], start=False, stop=True)
                        p.copy(Zs, PS[4][0:64, 0:128], eng='dve')
                        for h in range(2):
                            p.mm(PS[4][0:64, 256 + h * 64:256 + (h + 1) * 64], Tt[:, h * 64:(h + 1) * 64], Zs[:, h * 64:(h + 1) * 64])
                        for h in range(2):
                            p.copy(UV[0:64, h, h * 64:(h + 1) * 64], PS[4][0:64, 256 + h * 64:256 + (h + 1) * 64], eng='dve')

                    def s3():
                        p.mm(PS[7][:, 0:64], Hbd[:, hp, :], ARc[:, 1, :], start=True, stop=False)
                        p.mm(PS[7][:, 0:64], UV[:, 0, :], M2[:, 0, 64:128], start=False, stop=False)
                        p.mm(PS[7][:, 0:64], UV[:, 1, :], M2[:, 1, 64:128], start=False, stop=True)
                        p.mm(PS[7][:, 128:256], BKhat, UV[:, 0, :], start=True, stop=False)
                        p.mm(PS[7][:, 128:256], BKhat, UV[:, 1, :], start=False, stop=True)
                        p.copy(yraw[:, c * 64:(c + 1) * 64], PS[7][:, 0:64], eng='dve')
                        for h in range(2):
                            rows = slice(h * 64, (h + 1) * 64)
                            hb = Hbd[rows, hp, h * 64:(h + 1) * 64]
                            p.stt(hb, hb, WC2[par][rows, ci:ci + 1], PS[7][rows, 128 + h * 64:128 + (h + 1) * 64], ALU.mult, ALU.add)
                    return [s1, s2, s3]
                out_ += mk()
            return out_

        def post_closures(hp):
            cs = slice(hp * 128, (hp + 1) * 128)
            rl, kl, vl = pl[:, hp, :], pl[:, 4 + hp, :], pl[:, 8 + hp, :]
            yraw = YR[hp % 2]
            t1, t2, t3, t4 = POSTT

            sqb_ = t3.bitcast(BF16)[:, 0:512]
            yrb_ = t3.bitcast(BF16)[:, 512:1024]
            t4b_ = t4.bitcast(BF16)[:, 0:512]

            def q1():
                p.act(sqb_, yraw, AF.Square)
                p.copy(yrb_, yraw, eng='pool')
                p.mm(PS[5], ones2b, yrb_)
                p.ts(t2, PS[5], 1.0 / 64, ALU.mult)

            def q1b():
                p.mm(PS[5], ones2b, sqb_)
                p.tt(t3, t2, t2, ALU.mult, eng='pool')
                p.stt(t3, PS[5], 1.0 / 64, t3, ALU.mult, ALU.subtract)
                p.stt(t4b_, rl, col(70 + hp), kl, ALU.mult, ALU.mult)

            def q1c():
                p.mm(PS[5], ones2b, t4b_)
                p.act(t3, t3, AF.Sqrt, bias=64e-5)
                p.tt(t4, PS[5], vl, ALU.mult)
                p.tt(t1, yraw, t2, ALU.subtract, eng='pool')

            def q2():
                p.recip(t3, t3)
                p.mm(PS[5], GUP[:, cs], SG)
                p.tt(t1, t1, t3, ALU.mult)
                p.ts(t1, t1, col(74 + hp), ALU.mult, col(78 + hp), ALU.add)

            def q3():
                p.tt(t1, t1, t4, ALU.add, eng='pool')
                p.tt(ycat[:, hp, :], PS[5], t1, ALU.mult)
            return [q1, q1b, q1c, q2, q3]

        def interleave(fg, bg):
            nf, nb = len(fg), len(bg)
            bi = 0
            for i, f_ in enumerate(fg):
                f_()
                tgt = ((i + 1) * nb) // max(nf, 1)
                while bi < tgt:
                    bg[bi]()
                    bi += 1
            while bi < nb:
                bg[bi]()
                bi += 1

        NU = 2 * nhp
        for f_ in prep_closures(0) + pre_closures(0):
            f_()
        for u in range(NU):
            bg = []
            if u % 2 == 0 and u >= 2:
                bg += post_closures(u // 2 - 1)
            if u + 1 < NU:
                bg += prep_closures(u + 1) + pre_closures(u + 1)
            interleave(serial_closures(u), bg)
        for f_ in post_closures(nhp - 1):
            f_()
        dbg_out("ycat%d" % tt, ycat)
        if 'wout' not in stages:
            return

        def ev_out(mc, ps_):
            p.tt(xT[:, mc, ts_], ps_, xT[:, mc, ts_], ALU.add)
        linear(w_out, w_out_b, D, D, lambda kc: ycat[:, kc, :], ev_out, tt == 0)

    o = 0
    U_ = bfv(o, o + 2176).rearrange("p (c t) -> p c t", t=544); o += 2176
    acc = scr[:, o:o + 8 * 512].rearrange("p (c t) -> p c t", t=512); o += 8 * 512
    SQ2 = o
    vT = bfv(o, o + 2048).rearrange("p (c t) -> p c t", t=512); o += 2048
    gsig = [scr[:, o:o + 512], scr[:, o + 512:o + 1024]]; o += 1024
    cm, cr, ctmp = scr[:, o:o + 512], scr[:, o + 512:o + 1024], scr[:, o + 1024:o + 1536]; o += 1536
    Dg = [bfv(o, o + 1984).rearrange("p (j q) -> p j q", q=128), bfv(o + 1984, o + 3968).rearrange("p (j q) -> p j q", q=128)]
    o += 3968
    assert o <= 17472

    def build_diag(c):
        p.tt(Dg[c % 2], ident.unsqueeze(1).to_broadcast([128, 31, 128]),
             pcol[:, 130 + c * 31:130 + (c + 1) * 31].unsqueeze(2).to_broadcast([128, 31, 128]), ALU.mult,
             eng='pool' if c % 2 == 0 else 'dve')

    def conv_tile(tt):
        ts_ = slice(tt * 512, (tt + 1) * 512)
        if ('conv', tt) not in prenormed:
            rmsnorm_tile(tt, 8)
        if tt == 0:
            p.memset(U_[:, :, 0:30], 0.0, eng='pool')
        else:
            p.copy(U_[:, :, 0:30], chalo, eng='pool')
        build_diag(0)
        build_diag(1)

        def ev_pw1(mc, ps_):
            if mc < 8:
                p.act(acc[:, mc, :], ps_, AF.Identity, bias=col(82 + mc))
            else:
                m = mc - 8
                gb = gsig[m % 2]
                p.act(gb, ps_, AF.Sigmoid, bias=col(82 + mc))
                p.tt(U_[:, m, 30:542], acc[:, m, :], gb, ALU.mult, eng='pool')
        linear(pw1, pw1_b, D, 2048, lambda kc: hT[:, kc, :], ev_pw1, tt == 0)
        p.copy(chalo, U_[:, :, 512:542], eng='pool')
        for c in range(8):
            ps_ = PS[4 + (c % 4)]
            for j in range(31):
                p.mm(ps_, Dg[c % 2][:, j, :], U_[:, c, j:j + 512], start=(j == 0), stop=(j == 30))
            p.act(acc[:, c, :], ps_, AF.Identity, bias=col(98 + c))
            if c + 2 < 8:
                build_diag(c + 2)
        for c in range(8):
            p.mm(PS[6], ones, acc[:, c, :], start=(c == 0), stop=(c == 7))
        for c in range(8):
            sqb = sqr[c % 2]
            p.act(sqb.bitcast(F32R), acc[:, c, :], AF.Square)
            p.mm(PS[7], onesr.bitcast(F32R), sqb.bitcast(F32R), start=(c == 0), stop=(c == 7))
        p.ts(cm, PS[6], 1.0 / D, ALU.mult)
        p.tt(ctmp, cm, cm, ALU.mult, eng='pool')
        p.stt(cr, PS[7], 1.0 / D, ctmp, ALU.mult, ALU.subtract)
        p.act(cr, cr, AF.Sqrt, bias=1e-5)
        p.recip(cr, cr)
        for c in range(8):
            p.tt(acc[:, c, :], acc[:, c, :], cm, ALU.subtract)
            p.tt(acc[:, c, :], acc[:, c, :], cr, ALU.mult, eng='pool')
            p.act(vT[:, c, :], acc[:, c, :], AF.Silu, bias=col(114 + c), scale=col(106 + c))
        dbg_out("vT%d" % tt, vT)

        def ev_pw2(mc, ps_):
            p.stt(xT[:, mc, ts_], ps_, col(122 + mc), xT[:, mc, ts_], ALU.add, ALU.add)
        linear(pw2, pw2_b, D, D, lambda kc: vT[:, kc, :], ev_pw2, tt == 0)

    for tt in range(NT):
        hybrid_tile(tt)
        dbg_out("xmix%d" % tt, xT[:, :, tt * 512:(tt + 1) * 512])
        if 'mlp0' in stages:
            def mid0(tt=tt):
                if tt + 1 < NT:
                    rmsnorm_tile(tt + 1, 0)
                    prenormed.add(('hyb', tt + 1))
                elif 'conv' in stages and NT > 1:
                    rmsnorm_tile(0, 8)
                    prenormed.add(('conv', 0))
            mlp_tile(tt, 0, mid0)
    dbg_out("xl0", xT)
    for tt in range(NT):
        if 'conv' in stages:
            conv_tile(tt)
        if 'mlp1' in stages:
            def mid1(tt=tt):
                if tt + 1 < NT and 'conv' in stages:
                    rmsnorm_tile(tt + 1, 8)
                    prenormed.add(('conv', tt + 1))
            mlp_tile(tt, 1, mid1)
    yo = scr[:, 0:4 * D].rearrange("p (b d) -> p b d", d=D)
    for tt in range(NT):
        rmsnorm_tile(tt, 32, dst=hTf)
        for b in range(4):
            for half in range(2):
                ps_ = PS[4 + ((b * 2 + half) % 4)]
                for cc in range(4):
                    c = half * 4 + cc
                    p.transpose(ps_[:, cc * 128:(cc + 1) * 128], hTf[:, c, b * 128:(b + 1) * 128], ident)
                p.copy(yo[:, b, half * 512:(half + 1) * 512], ps_, eng='dve' if half == 0 else 'act')
        p.dma(out[tt * 512:(tt + 1) * 512, :].rearrange("(b p) d -> p b d", p=128), yo,
              q='sp')
    nc = p.build()
    return nc, p


def host_weights(inp):
    w_in = np.asarray(inp['hy_w_in'][0], np.float32)
    qcols = 1792 + np.concatenate([np.concatenate([np.arange(j * 64, (j + 1) * 64), np.arange((j + 4) * 64, (j + 5) * 64)])
                                   for j in range(4)])
    perm = np.concatenate([np.arange(1792), qcols, np.arange(2304, 2560)])
    w_in_p = np.ascontiguousarray(w_in[:, perm])
    w_out = np.asarray(inp['hy_w_out'][0], np.float32)
    rperm = np.concatenate([np.arange(512), 512 + (qcols - 1792)])
    w_out_p = np.ascontiguousarray(w_out[rperm, :])
    lup = np.ascontiguousarray(np.concatenate([inp['hy_w_up'][0], inp['hy_a_up'][0]], 0).astype(np.float32))
    gup = np.ascontiguousarray(np.asarray(inp['hy_g_up'][0], np.float32))
    sinkb = np.ascontiguousarray(np.broadcast_to(np.asarray(inp['hy_sinks'][0], np.float32)[None, :], (128, 8)))
    return dict(w_in=w_in_p, w_out=w_out_p, pw1=np.ascontiguousarray(inp['cv_pw1_w'][0]),
                pw2=np.ascontiguousarray(inp['cv_pw2_w'][0]), w1=np.ascontiguousarray(inp['mlp_w1']),
                w2=np.ascontiguousarray(inp['mlp_w2']), lup=lup, gup=gup, sinkb=sinkb,
                cst=host_consts(), pcol=host_pcol(inp))


_CACHE = {}


def kernel(**inputs):
    inputs = {k: np.asarray(v) for k, v in inputs.items()}
    x = np.asarray(inputs['x'], np.float32)
    B, T, _ = x.shape
    shared = host_weights(inputs)
    nc, _ = build(T)
    in_maps = [dict(shared, x=np.ascontiguousarray(x[b])) for b in range(B)]
    res = run_bass_kernel_spmd(nc, in_maps, core_ids=list(range(B)))
    return np.stack([np.asarray(r["out"], np.float32) for r in res.results], 0)
```
